# Optimizing a Trainium2 kernel written in Bass

```python
import math
import jax, jax.numpy as jnp
from jax import lax
import numpy as np


D_MODEL = 1024
BATCH = 4
SEQ = 4096
DEPTH = 2

RMS_EPS = 1e-6
D_FF = 2816
MACARON_WEIGHT = 0.5
N_EVEN = (DEPTH + 1) // 2
N_ODD = DEPTH // 2

POOL_WIDTH = D_MODEL // 2
POOL_GROUPS = 4
POOL_GROUP_DIM = POOL_WIDTH // POOL_GROUPS
POOL_WINDOWS = (2, 4, 8, 16)

HYENA_WIDTH = D_MODEL // 2
HYENA_ORDER = 2
HYENA_SHORT_CONV = 3
HYENA_POS_DIM = 33
HYENA_FILTER_HIDDEN = 64
HYENA_FAST_DECAY_PCT = 0.3
HYENA_SLOW_DECAY_PCT = 1.5
HYENA_DECAY_TARGET = 1e-2

EVEN_IN_WIDTH = POOL_WIDTH + (HYENA_ORDER + 1) * HYENA_WIDTH
EVEN_OUT_WIDTH = POOL_WIDTH + HYENA_WIDTH

MLA_HEADS = 16
MLA_Q_LORA = 256
MLA_KV_LORA = 128
MLA_NOPE = 64
MLA_ROPE = 32
MLA_V = 64
ROPE_THETA = 10000.0
Q_BLOCK = 128

kernel_name = "hybrid_pool_hyena_mla_macaron_encoder"


def rmsnorm(x, g):
    x32 = x.astype(jnp.float32)
    y = x32 * lax.rsqrt(jnp.mean(x32 * x32, axis=-1, keepdims=True) + RMS_EPS)
    return (y * g.astype(jnp.float32)).astype(x.dtype)


def swiglu(h, w_gate, w_up, w_down):
    return (jax.nn.silu(h @ w_gate) * (h @ w_up)) @ w_down


def centred_pool_minus_self(a):
    L = a.shape[1]
    a32 = a.astype(jnp.float32)
    csum = jnp.pad(jnp.cumsum(a32, axis=1), ((0, 0), (1, 0), (0, 0)))
    t = jnp.arange(L)
    outs = []
    for g, w in enumerate(POOL_WINDOWS):
        cg = csum[..., g * POOL_GROUP_DIM:(g + 1) * POOL_GROUP_DIM]
        lo = jnp.clip(t - w // 2, 0, L)
        hi = jnp.clip(t - w // 2 + w, 0, L)
        cnt = (hi - lo).astype(jnp.float32)[None, :, None]
        outs.append((jnp.take(cg, hi, axis=1) - jnp.take(cg, lo, axis=1)) / cnt)
    return (jnp.concatenate(outs, axis=-1) - a32).astype(a.dtype)


def pool_mixer(a, pool_w, pool_scale):
    B, L, _ = a.shape
    p = centred_pool_minus_self(a).reshape(B, L, POOL_GROUPS, POOL_GROUP_DIM)
    y = jnp.einsum('blgc,gcd->blgd', p, pool_w).reshape(B, L, POOL_WIDTH)
    return y * pool_scale


def short_conv_centred(u, w, b):
    L = u.shape[1]
    pad = HYENA_SHORT_CONV // 2
    up = jnp.pad(u, ((0, 0), (pad, HYENA_SHORT_CONV - 1 - pad), (0, 0)))
    y = b
    for k in range(HYENA_SHORT_CONV):
        y = y + up[:, k:k + L] * w[k]
    return y


def hyena_position_features(L):
    f32 = jnp.float32
    t = jnp.linspace(0.0, 1.0, L, dtype=f32)
    bands = (HYENA_POS_DIM - 1) // 2
    w = 2.0 * math.pi * jnp.arange(L, dtype=f32) / L
    f = jnp.linspace(1e-4, bands - 1, bands, dtype=f32)
    phase = w[:, None] * f[None, :]
    z = jnp.concatenate([t[:, None], jnp.cos(phase), -jnp.sin(phase)], axis=-1)
    return t, z


def hyena_filter_spectra(t, z, w1, b1, w2, b2, w3, b3, sin_freq, w_out, decay):
    f32 = jnp.float32
    L = t.shape[0]
    h = jnp.sin(sin_freq[0].astype(f32) * (z @ w1.astype(f32) + b1.astype(f32)))
    h = jnp.sin(sin_freq[1].astype(f32) * (h @ w2.astype(f32) + b2.astype(f32)))
    h = jnp.sin(sin_freq[2].astype(f32) * (h @ w3.astype(f32) + b3.astype(f32)))
    h = (h @ w_out.astype(f32)).reshape(L, HYENA_ORDER, 2, HYENA_WIDTH)
    h = h * jnp.exp(-t[:, None, None, None] * jnp.abs(decay.astype(f32)))
    fwd, bwd = h[:, :, 0], h[:, :, 1]
    two = jnp.concatenate(
        [fwd, jnp.zeros((1, HYENA_ORDER, HYENA_WIDTH), f32), bwd[:0:-1]], axis=0)
    two = two * lax.rsqrt(jnp.sum(two * two, axis=0, keepdims=True))
    return jnp.fft.rfft(two, axis=0)


def fft_long_conv(u, spec, bias):
    L = u.shape[1]
    u32 = u.astype(jnp.float32)
    y = jnp.fft.irfft(jnp.fft.rfft(u32, n=2 * L, axis=1) * spec[None], n=2 * L, axis=1)[:, :L]
    return (y + u32 * bias.astype(jnp.float32)).astype(u.dtype)


def hyena_mixer(u, conv_w, conv_b, spec, bias):
    u = short_conv_centred(u, conv_w, conv_b)
    parts = jnp.split(u, HYENA_ORDER + 1, axis=-1)
    z = parts[-1]
    for o in range(HYENA_ORDER):
        z = parts[o] * fft_long_conv(z, spec[:, o], bias[o])
    return z


def apply_rope(x, cos, sin):
    half = x.shape[-1] // 2
    x1, x2 = x[..., :half], x[..., half:]
    return jnp.concatenate([x1 * cos - x2 * sin, x1 * sin + x2 * cos], axis=-1)


def mla_mixer(h, w_dq, q_norm_g, w_uq, w_dkv, kv_norm_g, w_ukv, w_o, cos, sin):
    B, L, _ = h.shape
    cq = rmsnorm(h @ w_dq, q_norm_g)
    q = (cq @ w_uq).reshape(B, L, MLA_HEADS, MLA_NOPE + MLA_ROPE)
    q_nope, q_rope = q[..., :MLA_NOPE], q[..., MLA_NOPE:]
    q_rope = apply_rope(q_rope, cos[None, :, None, :], sin[None, :, None, :])
    ckv_full = h @ w_dkv
    ckv = rmsnorm(ckv_full[..., :MLA_KV_LORA], kv_norm_g)
    k_rope = apply_rope(ckv_full[..., MLA_KV_LORA:], cos[None], sin[None])
    kv = (ckv @ w_ukv).reshape(B, L, MLA_HEADS, MLA_NOPE + MLA_V)
    k_nope, v = kv[..., :MLA_NOPE], kv[..., MLA_NOPE:]
    scale = (MLA_NOPE + MLA_ROPE) ** -0.5
    nb = L // Q_BLOCK

    def to_blocks(a):
        return jnp.moveaxis(a.reshape(B, nb, Q_BLOCK, *a.shape[2:]), 1, 0)

    def attend(qs):
        qn, qr = qs
        s = (jnp.einsum('bqhd,bkhd->bhqk', qn, k_nope)
             + jnp.einsum('bqhr,bkr->bhqk', qr, k_rope))
        p = jax.nn.softmax(s.astype(jnp.float32) * scale, axis=-1).astype(v.dtype)
        return jnp.einsum('bhqk,bkhd->bqhd', p, v)

    o = lax.map(attend, (to_blocks(q_nope), to_blocks(q_rope)))
    o = jnp.moveaxis(o, 0, 1).reshape(B, L, MLA_HEADS * MLA_V)
    return o @ w_o


def setup_inputs(seed: int = 0) -> dict:
    key = jax.random.key(seed)
    ks = iter(jax.random.split(key, 32))

    def nrm(shape, scale):
        return jax.random.normal(next(ks), shape, jnp.float32) * scale

    D, F, W, FH = D_MODEL, D_FF, HYENA_WIDTH, HYENA_FILTER_HIDDEN
    decay_base = jnp.linspace(-math.log(HYENA_DECAY_TARGET) / HYENA_SLOW_DECAY_PCT,
                              -math.log(HYENA_DECAY_TARGET) / HYENA_FAST_DECAY_PCT, W,
                              dtype=jnp.float32)
    inp = {}
    inp["x"] = nrm((BATCH, SEQ, D), 1.0)
    inp["norm_g"] = 1.0 + nrm((DEPTH, 3, D), 0.05)
    inp["ffn_w_gate"] = nrm((DEPTH, 2, D, F), D ** -0.5)
    inp["ffn_w_up"] = nrm((DEPTH, 2, D, F), D ** -0.5)
    inp["ffn_w_down"] = nrm((DEPTH, 2, F, D), F ** -0.5)
    inp["mix_w_in"] = nrm((N_EVEN, D, EVEN_IN_WIDTH), D ** -0.5)
    inp["pool_w"] = nrm((N_EVEN, POOL_GROUPS, POOL_GROUP_DIM, POOL_GROUP_DIM), POOL_GROUP_DIM ** -0.5)
    inp["pool_scale"] = 1.0 + nrm((N_EVEN, POOL_WIDTH), 0.05)
    inp["hyena_conv_w"] = nrm((N_EVEN, HYENA_SHORT_CONV, (HYENA_ORDER + 1) * W), HYENA_SHORT_CONV ** -0.5)
    inp["hyena_conv_b"] = nrm((N_EVEN, (HYENA_ORDER + 1) * W), 0.02)
    inp["hyena_ffn_w1"] = nrm((N_EVEN, HYENA_POS_DIM, FH), HYENA_POS_DIM ** -0.5)
    inp["hyena_ffn_b1"] = nrm((N_EVEN, FH), 0.02)
    inp["hyena_ffn_w2"] = nrm((N_EVEN, FH, FH), FH ** -0.5)
    inp["hyena_ffn_b2"] = nrm((N_EVEN, FH), 0.02)
    inp["hyena_ffn_w3"] = nrm((N_EVEN, FH, FH), FH ** -0.5)
    inp["hyena_ffn_b3"] = nrm((N_EVEN, FH), 0.02)
    inp["hyena_sin_freq"] = 1.0 + nrm((N_EVEN, 3, FH), 0.1)
    inp["hyena_ffn_w_out"] = nrm((N_EVEN, FH, HYENA_ORDER * 2 * W), FH ** -0.5)
    inp["hyena_decay"] = decay_base * jnp.exp(nrm((N_EVEN, HYENA_ORDER, 2, W), 0.1))
    inp["hyena_bias"] = nrm((N_EVEN, HYENA_ORDER, W), 0.5)
    inp["mix_w_out"] = nrm((N_EVEN, EVEN_OUT_WIDTH, D), EVEN_OUT_WIDTH ** -0.5)
    inp["mla_w_dq"] = nrm((N_ODD, D, MLA_Q_LORA), D ** -0.5)
    inp["mla_q_norm_g"] = 1.0 + nrm((N_ODD, MLA_Q_LORA), 0.05)
    inp["mla_w_uq"] = nrm((N_ODD, MLA_Q_LORA, MLA_HEADS * (MLA_NOPE + MLA_ROPE)), MLA_Q_LORA ** -0.5)
    inp["mla_w_dkv"] = nrm((N_ODD, D, MLA_KV_LORA + MLA_ROPE), D ** -0.5)
    inp["mla_kv_norm_g"] = 1.0 + nrm((N_ODD, MLA_KV_LORA), 0.05)
    inp["mla_w_ukv"] = nrm((N_ODD, MLA_KV_LORA, MLA_HEADS * (MLA_NOPE + MLA_V)), MLA_KV_LORA ** -0.5)
    inp["mla_w_o"] = nrm((N_ODD, MLA_HEADS * MLA_V, D), (MLA_HEADS * MLA_V) ** -0.5)
    inp["final_norm_g"] = 1.0 + nrm((D,), 0.05)
    return inp


def reference(x, norm_g, ffn_w_gate, ffn_w_up, ffn_w_down, mix_w_in, pool_w, pool_scale,
              hyena_conv_w, hyena_conv_b, hyena_ffn_w1, hyena_ffn_b1, hyena_ffn_w2, hyena_ffn_b2,
              hyena_ffn_w3, hyena_ffn_b3, hyena_sin_freq, hyena_ffn_w_out, hyena_decay, hyena_bias,
              mix_w_out, mla_w_dq, mla_q_norm_g, mla_w_uq, mla_w_dkv, mla_kv_norm_g, mla_w_ukv,
              mla_w_o, final_norm_g):
    L = x.shape[1]
    t_pos, z_pos = hyena_position_features(L)
    inv_freq = ROPE_THETA ** (-jnp.arange(0, MLA_ROPE, 2, dtype=jnp.float32) / MLA_ROPE)
    ang = jnp.arange(L, dtype=jnp.float32)[:, None] * inv_freq[None, :]
    cos = jnp.cos(ang).astype(x.dtype)
    sin = jnp.sin(ang).astype(x.dtype)

    for i in range(DEPTH):
        x = x + MACARON_WEIGHT * swiglu(rmsnorm(x, norm_g[i, 0]),
                                        ffn_w_gate[i, 0], ffn_w_up[i, 0], ffn_w_down[i, 0])
        h = rmsnorm(x, norm_g[i, 1])
        j = i // 2
        if i % 2 == 0:
            proj = h @ mix_w_in[j]
            y_pool = pool_mixer(proj[..., :POOL_WIDTH], pool_w[j], pool_scale[j])
            spec = hyena_filter_spectra(t_pos, z_pos, hyena_ffn_w1[j], hyena_ffn_b1[j],
                                        hyena_ffn_w2[j], hyena_ffn_b2[j], hyena_ffn_w3[j],
                                        hyena_ffn_b3[j], hyena_sin_freq[j], hyena_ffn_w_out[j],
                                        hyena_decay[j])
            y_hyena = hyena_mixer(proj[..., POOL_WIDTH:], hyena_conv_w[j], hyena_conv_b[j],
                                  spec, hyena_bias[j])
            x = x + jnp.concatenate([y_pool, y_hyena], axis=-1) @ mix_w_out[j]
        else:
            x = x + mla_mixer(h, mla_w_dq[j], mla_q_norm_g[j], mla_w_uq[j], mla_w_dkv[j],
                              mla_kv_norm_g[j], mla_w_ukv[j], mla_w_o[j], cos, sin)
        x = x + MACARON_WEIGHT * swiglu(rmsnorm(x, norm_g[i, 2]),
                                        ffn_w_gate[i, 1], ffn_w_up[i, 1], ffn_w_down[i, 1])
    return rmsnorm(x, final_norm_g)
```

```python
import math
from contextlib import ExitStack

import numpy as np
import ml_dtypes

import concourse.bass as bass
import concourse.mybir as mybir
from concourse.bass_utils import run_bass_kernel_spmd

F32 = mybir.dt.float32
BF16 = mybir.dt.bfloat16
AF = mybir.ActivationFunctionType
ALU = mybir.AluOpType
AX = mybir.AxisListType

D = 1024
DFF = 2816
NF = DFF // 128
L = 4096
TOK = 2048
EPS = 1e-6
NCORES = 8


class Dep:
    __slots__ = ("w", "r", "name")

    def __init__(self, name=""):
        self.w = []
        self.r = []
        self.name = name


class Slot:
    __slots__ = ("sem", "val")

    def __init__(self, sem):
        self.sem = sem
        self.val = 0


class KB:
    ENG = ("pe", "act", "dve", "pool", "sp")

    def __init__(self, nc, es, ring_sizes=None):
        self.nc = nc
        self.es = es
        self.engs = {"pe": nc.tensor, "act": nc.scalar, "dve": nc.vector,
                     "pool": nc.gpsimd, "sp": nc.sync}
        self.sem = {}
        self.cnt = {}
        self.waited = {e: {} for e in self.ENG}
        for e in ("pe", "act", "dve", "pool"):
            self.sem[e] = es.enter_context(nc.semaphore("s_" + e))
            self.cnt[e] = 0
        rs = {"sp": 40, "pool": 24, "act": 8}
        if ring_sizes:
            rs.update(ring_sizes)
        self.rings = {}
        self.ring_pos = {}
        for q, n in rs.items():
            self.rings[q] = [Slot(es.enter_context(nc.semaphore("d_%s%d" % (q, i)))) for i in range(n)]
            self.ring_pos[q] = 0
        self.ninstr = {e: 0 for e in self.ENG}
        self.cc = []

    def _wait(self, E, tok):
        src, val = tok
        if isinstance(src, Slot):
            key = id(src)
            if self.waited[E].get(key, 0) >= val:
                return
            self.engs[E].wait_ge(src.sem, val)
            self.waited[E][key] = val
        else:
            if src == E and src == "pe":
                return
            if self.waited[E].get(src, 0) >= val:
                return
            self.engs[E].wait_ge(self.sem[src], val)
            self.waited[E][src] = val
        self.ninstr[E] += 1

    def _deps(self, E, reads, writes):
        for d in reads:
            for t in d.w:
                self._wait(E, t)
        for d in writes:
            for t in d.w:
                self._wait(E, t)
            for t in d.r:
                self._wait(E, t)

    def _commit(self, tok, reads, writes, wacc=()):
        for d in wacc:
            d.w.append(tok)
        for d in reads:
            d.r.append(tok)
            if len(d.r) > 64:
                d.r = d.r[-48:]
        for d in writes:
            d.w = [tok]
            d.r = []

    def op(self, E, fn, reads=(), writes=(), inc=True, wacc=()):
        self._deps(E, reads, writes)
        for d in wacc:
            for t in d.r:
                self._wait(E, t)
        ins = fn(self.engs[E])
        self.ninstr[E] += 1
        if inc:
            self.cnt[E] += 1
            ins.then_inc(self.sem[E], 1)
            tok = (E, self.cnt[E])
        else:
            tok = (E, self.cnt[E] + 1)
        self._commit(tok, reads, writes, wacc)
        return tok

    def dma(self, q, out, in_, reads=(), writes=(), wacc=(), **kw):
        ring = self.rings[q]
        slot = ring[self.ring_pos[q] % len(ring)]
        self.ring_pos[q] += 1
        if slot.val > 0:
            self._wait(q, (slot, slot.val))
        self._deps(q, reads, writes)
        for d in wacc:
            for t in d.r:
                self._wait(q, t)
        ins = self.engs[q].dma_start(out=out, in_=in_, **kw)
        self.ninstr[q] += 1
        slot.val += 16
        ins.then_inc(slot.sem, 16)
        tok = (slot, slot.val)
        self._commit(tok, reads, writes, wacc)
        return tok

    def gather(self, out, in_, idx_ap, reads=(), writes=(), wacc=()):
        q = "pool"
        ring = self.rings[q]
        slot = ring[self.ring_pos[q] % len(ring)]
        self.ring_pos[q] += 1
        if slot.val > 0:
            self._wait(q, (slot, slot.val))
        self._deps(q, reads, writes)
        for d in wacc:
            for t in d.r:
                self._wait(q, t)
        ins = self.engs[q].indirect_dma_start(out=out, out_offset=None, in_=in_,
                                              in_offset=bass.IndirectOffsetOnAxis(ap=idx_ap, axis=0))
        self.ninstr[q] += 1
        slot.val += 16
        ins.then_inc(slot.sem, 16)
        tok = (slot, slot.val)
        self._commit(tok, reads, writes, wacc)
        return tok

    def allgather_pairs(self, src_t, dst_t, src_deps, d_dst, acc=False):
        for d in src_deps:
            for t in d.w:
                self._wait("pool", t)
        for t in d_dst.r + d_dst.w:
            self._wait("pool", t)
        sem = self.es.enter_context(self.nc.semaphore("cc%d" % len(self.cc)))
        slot = Slot(sem)
        self.cc.append(slot)
        self.nc.gpsimd.collective_compute("AllGather", ALU.bypass, replica_groups=[[0, 1], [2, 3], [4, 5], [6, 7]],
                                          ins=[src_t.ap().opt()], outs=[dst_t.ap().opt()]).then_inc(sem)
        slot.val = 1
        self.ninstr["pool"] += 1
        if acc:
            d_dst.w.append((slot, 1))
        else:
            d_dst.w = [(slot, 1)]
            d_dst.r = []

    def wait_all(self, E, deps):
        for d in deps:
            for t in d.w:
                self._wait(E, t)


_UNIQ = [0]


def sb(nc, es, name, shape, dt, side=None):
    _UNIQ[0] += 1
    if side is None:
        return es.enter_context(nc.sbuf_tensor("%s_%d" % (name, _UNIQ[0]), list(shape), dt))
    return es.enter_context(nc.sbuf_tensor("%s_%d" % (name, _UNIQ[0]), list(shape), dt, side=side))


def ps(nc, es, name, shape, dt=F32):
    return es.enter_context(nc.psum_tensor(name, list(shape), dt))


def emit_rmsnorm(kb, C, xT, dxs, g_ap, hT, d_h, t0, ntok, h0=0, nch=8, ones=None, np_=128):
    if ones is None:
        ones = C["ones_b"]
    for tt in range(ntok // 512):
        a = t0 + tt * 512
        d_x = dxs[a // 512]
        for k in range(nch):
            kb.op("act", lambda e, k=k: e.activation(out=C["sq"][0:np_, k, :], in_=xT[:, k, a:a + 512], func=AF.Square),
                  reads=[d_x], writes=[C["d_sq"]])
        for k in range(nch):
            kb.op("pe", lambda e, k=k: e.matmul(C["ps_ss"][0:np_, :], lhsT=ones[0:np_, 0:np_], rhs=C["sq"][0:np_, k, :],
                                                 start=(k == 0), stop=(k == nch - 1)),
                  reads=[C["d_sq"], C["d_const"]], writes=[C["d_ps_ss"]], inc=(k == nch - 1))
        kb.op("act", lambda e: e.activation(out=C["rstd"][0:np_, :], in_=C["ps_ss"][0:np_, :], func=AF.Sqrt, bias=C["eps"][0:np_, 0:1]),
              reads=[C["d_ps_ss"], C["d_const"]], writes=[C["d_rstd"]])
        kb.op("dve", lambda e: e.reciprocal(out=C["rstd"][0:np_, :], in_=C["rstd"][0:np_, :]),
              reads=[C["d_rstd"]], writes=[C["d_rstd"]])
        for k in range(nch):
            o = h0 + tt * 512
            kb.op("dve", lambda e, k=k, o=o: e.scalar_tensor_tensor(
                out=hT[:, k, o:o + 512], in0=xT[:, k, a:a + 512], scalar=g_ap[:, k:k + 1],
                in1=C["rstd"][0:np_, :], op0=ALU.mult, op1=ALU.mult),
                reads=[d_x, C["d_rstd"], C["d_const"]], writes=[d_h])


def emit_ffn(kb, C, xT, dxs, g_ap, wg_t, wu_t, wd_t, tag):
    nc = kb.nc
    with ExitStack() as es:
        hT = sb(nc, es, "hT" + tag, [128, 8, TOK], BF16)
        aT = sb(nc, es, "aT" + tag, [128, 12, TOK], BF16)
        NB = 3
        wgb = [sb(nc, es, "wg%d%s" % (i, tag), [128, 8, 256], BF16) for i in range(NB)]
        wub = [sb(nc, es, "wu%d%s" % (i, tag), [128, 8, 256], BF16) for i in range(NB)]
        wdb = [sb(nc, es, "wd%d%s" % (i, tag), [128, 12, 256], BF16) for i in range(2)]
        sg = [sb(nc, es, "sg%d%s" % (i, tag), [128, 512], F32) for i in range(2)]
        d_wg = [Dep() for _ in range(NB)]
        d_wu = [Dep() for _ in range(NB)]
        d_wd = [Dep() for _ in range(2)]
        d_sg = [Dep() for _ in range(2)]
        d_h = Dep()
        d_a = [Dep() for _ in range(12)]
        ps_g = [C["ps_a"], C["ps_b"]]
        ps_u = [C["ps_c"], C["ps_d"]]
        d_psg = [C["d_ps_a"], C["d_ps_b"]]
        d_psu = [C["d_ps_c"], C["d_ps_d"]]
        ps_y = [C["ps_e"], C["ps_f"]]
        d_psy = [C["d_ps_e"], C["d_ps_f"]]

        def load_gu(fg):
            b = fg % NB
            kb.dma("pool", wgb[b][:], wg_t[fg], writes=[d_wg[b]])
            kb.dma("pool", wub[b][:], wu_t[fg], writes=[d_wu[b]])

        def load_d(p, dg):
            b = dg % 2
            nfc = 12 if p == 0 else 10
            kb.dma("pool", wdb[b][:, 0:nfc, :], wd_t[p, dg, :, 0:nfc, :], writes=[d_wd[b]])

        load_gu(0)
        load_gu(1)
        emit_rmsnorm(kb, C, xT, dxs, g_ap, hT, d_h, 0, TOK)
        it = 0
        yit = 0
        for p in range(2):
            g0, g1 = (0, 6) if p == 0 else (6, 11)
            nfc = (g1 - g0) * 2
            for fg in range(g0, g1):
                if fg + 2 < 11:
                    load_gu(fg + 2)
                if fg == g1 - 2:
                    load_d(p, 0)
                if fg == g1 - 1:
                    load_d(p, 1)
                b = fg % NB
                for j in range(2):
                    fl = (fg - g0) * 2 + j
                    for tt in range(4):
                        pb = it % 2
                        it += 1
                        for k in range(8):
                            kb.op("pe", lambda e, k=k: e.matmul(
                                ps_g[pb][:], lhsT=wgb[b][:, k, j * 128:(j + 1) * 128],
                                rhs=hT[:, k, tt * 512:(tt + 1) * 512], start=(k == 0), stop=(k == 7)),
                                reads=[d_wg[b], d_h], writes=[d_psg[pb]], inc=(k == 7))
                        for k in range(8):
                            kb.op("pe", lambda e, k=k: e.matmul(
                                ps_u[pb][:], lhsT=wub[b][:, k, j * 128:(j + 1) * 128],
                                rhs=hT[:, k, tt * 512:(tt + 1) * 512], start=(k == 0), stop=(k == 7)),
                                reads=[d_wu[b], d_h], writes=[d_psu[pb]], inc=(k == 7))
                        kb.op("act", lambda e: e.activation(out=sg[pb][:], in_=ps_g[pb][:], func=AF.Silu),
                              reads=[d_psg[pb]], writes=[d_sg[pb]])
                        kb.op("dve", lambda e: e.tensor_tensor(
                            out=aT[:, fl, tt * 512:(tt + 1) * 512], in0=ps_u[pb][:], in1=sg[pb][:], op=ALU.mult),
                            reads=[d_psu[pb], d_sg[pb]], writes=[d_a[fl]])
            for dg in range(4):
                b = dg % 2
                for dj in range(2):
                    dch = dg * 2 + dj
                    for tt in range(4):
                        yb = yit % 2
                        yit += 1
                        for f in range(nfc):
                            kb.op("pe", lambda e, f=f: e.matmul(
                                ps_y[yb][:], lhsT=wdb[b][:, f, dj * 128:(dj + 1) * 128],
                                rhs=aT[:, f, tt * 512:(tt + 1) * 512], start=(f == 0), stop=(f == nfc - 1)),
                                reads=[d_wd[b], d_a[f]], writes=[d_psy[yb]], inc=(f == nfc - 1))
                        a = tt * 512
                        d_x = dxs[tt]
                        kb.op("dve", lambda e: e.scalar_tensor_tensor(
                            out=xT[:, dch, a:a + 512], in0=ps_y[yb][:], scalar=0.5,
                            in1=xT[:, dch, a:a + 512], op0=ALU.mult, op1=ALU.add),
                            reads=[d_psy[yb], d_x], writes=[d_x])
                if dg + 2 < 4:
                    load_d(p, dg + 2)
        kb.drain_scope([d_h] + d_a + d_wg + d_wu + d_wd + d_sg)


def _drain_scope(self, deps):
    toks = []
    for d in deps:
        toks += d.w + d.r
    for E in ("pe", "act", "dve", "pool", "sp"):
        for t in toks:
            self._wait(E, t)


KB.drain_scope = _drain_scope


def common_tiles(nc, es, kb):
    C = {}
    C["ones_b"] = sb(nc, es, "ones_b", [128, 128], BF16)
    C["sq"] = sb(nc, es, "sq", [128, 8, 512], BF16)
    C["rstd"] = sb(nc, es, "rstd", [128, 512], F32)
    for n2 in ("ab", "cd", "ef", "gh"):
        t2 = ps(nc, es, "ps_" + n2, [128, 1024])
        C["ps_" + n2] = t2
        C["ps_" + n2[0]] = t2[:, 0:512]
        C["ps_" + n2[1]] = t2[:, 512:1024]
    for n in "abcdefgh":
        C["d_ps_" + n] = Dep()
    C["ps_ss"] = C["ps_g"]
    C["d_ps_ss"] = C["d_ps_g"]
    for n in ("d_sq", "d_rstd", "d_const"):
        C[n] = Dep()
    C["rr"] = 0
    C["eps"] = sb(nc, es, "eps", [128, 1], F32)
    kb.op("pool", lambda e: e.memset(C["ones_b"][:], 1.0 / 1024.0), writes=[C["d_const"]])
    kb.op("pool", lambda e: e.memset(C["eps"][:], EPS), writes=[C["d_const"]])
    C["ones256"] = sb(nc, es, "ones256", [128, 128], BF16)
    C["ones128"] = sb(nc, es, "ones128", [128, 128], BF16)
    kb.op("pool", lambda e: e.memset(C["ones256"][:], 1.0 / 256.0), writes=[C["d_const"]])
    kb.op("pool", lambda e: e.memset(C["ones128"][:], 1.0 / 128.0), writes=[C["d_const"]])
    return C


NFFT = 8192
CH = 64
HALF = CH // 2


def hyena_constants():
    bf = ml_dtypes.bfloat16
    t1 = np.arange(32)[:, None].astype(np.float64)
    f1 = np.arange(32)[None, :].astype(np.float64)
    ang = 2 * np.pi * (f1 + 0.5) * t1 / 64.0
    d1 = np.concatenate([np.cos(ang), -np.sin(ang)], axis=1)
    t2 = np.arange(128)[:, None].astype(np.float64)
    f2 = np.arange(128)[None, :].astype(np.float64)
    w = np.zeros((128, 32, 3, 128))
    for a in range(32):
        th = 2 * np.pi * ((a + 0.5) * t2 / NFFT + f2 * t2 / 128.0)
        w[:, a, 0] = np.cos(th)
        w[:, a, 1] = -np.sin(th)
        w[:, a, 2] = np.sin(th)
    f2c = np.arange(128)[:, None].astype(np.float64)
    t2r = np.arange(128)[None, :].astype(np.float64)
    ph = 2 * np.pi * f2c * t2r / 128.0
    i1 = np.zeros((128, 2, 256))
    i1[:, 0, :128] = np.cos(ph)
    i1[:, 0, 128:] = np.sin(ph)
    i1[:, 1, :128] = -np.sin(ph)
    i1[:, 1, 128:] = np.cos(ph)
    g = np.zeros((64, 128, 32))
    f1c = np.arange(32)[:, None].astype(np.float64)
    t1r = np.arange(32)[None, :].astype(np.float64)
    for b in range(128):
        phi = 2 * np.pi * (f1c + 0.5) * (t1r / 64.0 + b / NFFT)
        g[0:32, b] = (2.0 / NFFT) * np.cos(phi)
        g[32:64, b] = -(2.0 / NFFT) * np.sin(phi)
    return {"c_d1": d1.astype(bf), "c_f2": w.astype(bf), "c_i1": i1.astype(bf), "c_i2": g.astype(bf)}


def hyena_pos_constants():
    f32 = np.float32
    t = np.linspace(0.0, 1.0, L, dtype=f32)
    bands = 16
    w = (2.0 * math.pi * np.arange(L, dtype=f32) / L).astype(f32)
    f = np.linspace(1e-4, bands - 1, bands, dtype=f32)
    phase = (w[:, None] * f[None, :]).astype(f32)
    z = np.concatenate([t[:, None], np.cos(phase), -np.sin(phase)], axis=-1).astype(f32)
    return {"c_zT": np.ascontiguousarray(z.T), "c_trow": np.ascontiguousarray(-t[None, :])}


def bcast_rows(ap_row, n):
    if hasattr(ap_row, "broadcast"):
        return ap_row.broadcast(0, n)
    return ap_row[0:1, :].to_broadcast([n, ap_row.shape[1]])


def emit_fft_fwd(kb, HC, src, ncols_chunks, consumer, src_is_f32, src_dep):
    nc = kb.nc
    for ci in range(ncols_chunks):
        xb = ci % 2
        X, dX = HC["X"][xb], HC["dX"][xb]
        for (ap, c0, n) in src(ci):
            kb.dma("pool" if src_is_f32 else "sp", X[:, c0:c0 + n, :],
                   ap.rearrange("c (t1 t2) -> t1 c t2", t2=128), reads=[src_dep], writes=[dX])
        A, dA = HC["A"][xb], HC["dA"][xb]
        for j0 in range(0, CH, 8):
            pb = (j0 // 8) % 2
            psA, dpsA = HC["psA"][pb], HC["dpsA"][pb]
            for j in range(8):
                kb.op("pe", lambda e, j=j: e.matmul(psA[:, j * 64:(j + 1) * 64], lhsT=X[:, j0 + j, :], rhs=HC["d1"][:],
                                                    start=True, stop=True),
                      reads=[dX, HC["dconst"]], writes=[dpsA], inc=(j == 7))
            eng = "act" if (j0 // 8) % 2 == 0 else "dve"
            if eng == "act":
                kb.op("act", lambda e: e.copy(out=A[:, j0:j0 + 8, :], in_=psA[:].rearrange("p (j f) -> p j f", f=64)),
                      reads=[dpsA], writes=[dA])
            else:
                kb.op("dve", lambda e: e.tensor_copy(out=A[:, j0:j0 + 8, :], in_=psA[:].rearrange("p (j f) -> p j f", f=64)),
                      reads=[dpsA], writes=[dA])
        for q in range(8):
            pb = q % 2
            psU, dpsU = HC["psU"][pb], HC["dpsU"][pb]
            U4 = psU[:, 0:8 * CH].rearrange("p (a r c) -> p a r c", a=4, r=2)
            for a in range(4):
                f1 = 4 * q + a
                Ar = A[:, :, f1]
                Ai = A[:, :, 32 + f1]
                W = HC["f2"]
                kb.op("pe", lambda e: e.matmul(U4[:, a, 0, :], lhsT=W[:, f1, 0, :], rhs=Ar, start=True, stop=False),
                      reads=[dA, HC["dconst"]], writes=[dpsU], inc=False)
                kb.op("pe", lambda e: e.matmul(U4[:, a, 0, :], lhsT=W[:, f1, 2, :], rhs=Ai, start=False, stop=True),
                      reads=[dA], writes=[dpsU], inc=False)
                kb.op("pe", lambda e: e.matmul(U4[:, a, 1, :], lhsT=W[:, f1, 1, :], rhs=Ar, start=True, stop=False),
                      reads=[dA], writes=[dpsU], inc=False)
                kb.op("pe", lambda e: e.matmul(U4[:, a, 1, :], lhsT=W[:, f1, 0, :], rhs=Ai, start=False, stop=True),
                      reads=[dA], writes=[dpsU], inc=(a == 3))
            consumer(ci, q, U4, dpsU)


def emit_fft_inv(kb, HC, ci, Y, dY, dst, Y2=None, dY2=None):
    Z, dZ = HC["Z"], HC["dZ"]
    for j0 in range(0, CH, 4):
        pb = (j0 // 4) % 2
        psZ, dpsZ = HC["psZ"][pb], HC["dpsZ"][pb]
        for j in range(4):
            kb.op("pe", lambda e, j=j: e.matmul(psZ[0:64, j * 128:(j + 1) * 128], lhsT=Y[:, :, j0 + j],
                                                rhs=HC["i1"][:, 0, 0:128], start=True, stop=False),
                  reads=[dY, HC["dconst"]], writes=[dpsZ], inc=False)
            kb.op("pe", lambda e, j=j: e.matmul(psZ[0:64, j * 128:(j + 1) * 128], lhsT=Y2[:, :, j0 + j],
                                                rhs=HC["i1"][:, 0, 128:256], start=False, stop=True),
                  reads=[dY2], writes=[dpsZ], inc=(j == 3))
        if pb == 0:
            kb.op("act", lambda e: e.copy(out=Z[:, j0:j0 + 4, :], in_=psZ[0:64, :].rearrange("p (j f) -> p j f", f=128)),
                  reads=[dpsZ], writes=[dZ])
        else:
            kb.op("dve", lambda e: e.tensor_copy(out=Z[:, j0:j0 + 4, :], in_=psZ[0:64, :].rearrange("p (j f) -> p j f", f=128)),
                  reads=[dpsZ], writes=[dZ])
    ysb, dys = HC["ysb"], HC["dys"]
    G = HC["i2"]
    for b0 in range(0, 128, 8):
        pb = (b0 // 8) % 2
        psy, dpsy = HC["psy"][pb], HC["dpsy"][pb]
        for b in range(8):
            t2 = b0 + b
            kb.op("pe", lambda e, b=b, t2=t2: e.matmul(psy[0:32, b * CH:(b + 1) * CH], lhsT=G[:, t2, :],
                                                     rhs=Z[:, :, t2], start=True, stop=True),
                  reads=[dZ, HC["dconst"]], writes=[dpsy], inc=(b == 7))
        src = psy[0:32, 0:8 * CH].rearrange("p (b c) -> p b c", c=CH)
        dstv = ysb[:, :, b0:b0 + 8].rearrange("p c b -> p b c")
        if pb == 0:
            kb.op("act", lambda e: e.copy(out=dstv, in_=src), reads=[dpsy], writes=[dys])
        else:
            kb.op("dve", lambda e: e.tensor_copy(out=dstv, in_=src), reads=[dpsy], writes=[dys])
    kb.dma("sp", dst.rearrange("c (t1 t2) -> t1 c t2", t2=128), ysb[:], reads=[dys], wacc=[HC["d_sy"]])


def emit_sin_layer(kb, nc, S, w_sb, kdim, rhs_of, fcol, bfcol, hout, d_in, d_out):
    MAGIC = 12582912.0
    for tc in range(8):
        pb = tc % 2
        pp, dpp = S["ps"][pb], S["dps"][pb]
        kb.op("pe", lambda e: e.matmul(pp[0:64, :], lhsT=w_sb, rhs=rhs_of(tc), start=True, stop=True),
              reads=[d_in, S["dw"]], writes=[dpp])
        a, k = S["arg"][pb], S["kk"][pb]
        da = S["darg"][pb]
        kb.op("act", lambda e: e.activation(out=a[:], in_=pp[0:64, :], func=AF.Identity, scale=fcol, bias=bfcol),
              reads=[dpp, S["dw"]], writes=[da])
        kb.op("dve", lambda e: e.tensor_scalar(out=k[:], in0=a[:], scalar1=1.0 / (2 * math.pi), scalar2=MAGIC,
                                               op0=ALU.mult, op1=ALU.add), reads=[da], writes=[da])
        kb.op("dve", lambda e: e.tensor_scalar(out=k[:], in0=k[:], scalar1=MAGIC, scalar2=2 * math.pi,
                                               op0=ALU.subtract, op1=ALU.mult), reads=[da], writes=[da])
        kb.op("dve", lambda e: e.tensor_tensor(out=a[:], in0=a[:], in1=k[:], op=ALU.subtract), reads=[da], writes=[da])
        kb.op("act", lambda e: e.activation(out=hout[:, tc * 512:(tc + 1) * 512], in_=a[:], func=AF.Sin,
                                            scale=1.0 - 2e-6), reads=[da], writes=[d_out])


def emit_hyena(kb, nc, I, psb):
    with ExitStack() as es0:
        HC = {}
        HC["dconst"] = Dep()
        HC["d1"] = sb(nc, es0, "h_d1", [32, 64], BF16)
        HC["f2"] = sb(nc, es0, "h_f2", [128, 32, 3, 128], BF16)
        HC["i1"] = sb(nc, es0, "h_i1", [128, 2, 256], BF16)
        HC["i2"] = sb(nc, es0, "h_i2", [64, 128, 32], BF16)
        def load_tables():
            kb.dma("sp", HC["d1"][:], I["c_d1"][:, :], wacc=[HC["dconst"]])
            kb.dma("sp", HC["f2"][:], I["c_f2"][:, :, :, :], wacc=[HC["dconst"]])
            kb.dma("sp", HC["i1"][:], I["c_i1"][:, :, :], wacc=[HC["dconst"]])
            kb.dma("sp", HC["i2"][:], I["c_i2"][:, :, :], wacc=[HC["dconst"]])
        convw = sb(nc, es0, "h_convw", [128, 6, 3], F32)
        convb = sb(nc, es0, "h_convb", [128, 6], F32)
        hbias = sb(nc, es0, "h_bias", [128, 4], F32)
        kb.dma("sp", convw[:], I["conv_w"][:, :, :], wacc=[HC["dconst"]])
        kb.dma("sp", convb[:], I["conv_b"][:, :], wacc=[HC["dconst"]])
        kb.dma("sp", hbias[:], I["hy_bias"][:, :], wacc=[HC["dconst"]])
        names = ["psA", "psU", "psZ", "psy"]
        for i, n in enumerate(names):
            HC[n] = [psb[2 * i][0], psb[2 * i + 1][0]]
            HC["d" + n] = [psb[2 * i][1], psb[2 * i + 1][1]]
        HC["d_sy"] = Dep()
        d_sf = Dep()

        with ExitStack() as es:
            S = {}
            S["ps"] = [psb[0][0], psb[1][0]]
            S["dps"] = [psb[0][1], psb[1][1]]
            S["dw"] = Dep()
            zT = sb(nc, es, "f_zT", [33, L], F32)
            w1 = sb(nc, es, "f_w1", [33, 64], F32)
            w2 = sb(nc, es, "f_w2", [64, 64], F32)
            w3 = sb(nc, es, "f_w3", [64, 64], F32)
            wo = sb(nc, es, "f_wo", [64, 1024], F32)
            bb = sb(nc, es, "f_b", [64, 3], F32)
            fq = sb(nc, es, "f_fq", [64, 3], F32)
            bf_ = sb(nc, es, "f_bf", [64, 3], F32)
            dec = sb(nc, es, "f_dec", [128, 8], F32)
            trow = sb(nc, es, "f_trow", [128, L], F32)
            for t_, s_ in ((zT, I["c_zT"]), (w1, I["w1"]), (w2, I["w2"]), (w3, I["w3"]), (wo, I["w_out"]),
                           (bb, I["b123"]), (fq, I["sin_freq"]), (dec, I["decay"])):
                kb.dma("sp", t_[:], s_[:, :], wacc=[S["dw"]])
            kb.dma("sp", trow[:], bcast_rows(I["c_trow"], 128), wacc=[S["dw"]])
            load_tables()
            kb.op("dve", lambda e: e.tensor_tensor(out=bf_[:], in0=bb[:], in1=fq[:], op=ALU.mult),
                  reads=[S["dw"]], writes=[S["dw"]])
            kb.op("act", lambda e: e.activation(out=dec[:], in_=dec[:], func=AF.Abs), reads=[S["dw"]], writes=[S["dw"]])
            S["arg"] = [sb(nc, es, "f_arg%d" % i, [64, 512], F32) for i in range(2)]
            S["kk"] = [sb(nc, es, "f_kk%d" % i, [64, 512], F32) for i in range(2)]
            S["darg"] = [Dep(), Dep()]
            hA = sb(nc, es, "f_hA", [64, L], F32)
            hB = sb(nc, es, "f_hB", [64, L], F32)
            dhA, dhB = Dep(), Dep()
            emit_sin_layer(kb, nc, S, w1[:], 33, lambda tc: zT[:, tc * 512:(tc + 1) * 512], fq[:, 0:1], bf_[:, 0:1], hA, S["dw"], dhA)
            emit_sin_layer(kb, nc, S, w2[:], 64, lambda tc: hA[:, tc * 512:(tc + 1) * 512], fq[:, 1:2], bf_[:, 1:2], hB, dhA, dhB)
            emit_sin_layer(kb, nc, S, w3[:], 64, lambda tc: hB[:, tc * 512:(tc + 1) * 512], fq[:, 2:3], bf_[:, 2:3], hA, dhB, dhA)
            wob = sb(nc, es, "f_wob", [64, 1024], BF16)
            hAb = sb(nc, es, "f_hAb", [64, L], BF16)
            dhAb = Dep()
            kb.op("act", lambda e: e.copy(out=wob[:], in_=wo[:]), reads=[S["dw"]], writes=[S["dw"]])
            kb.op("act", lambda e: e.copy(out=hAb[:], in_=hA[:]), reads=[dhA], writes=[dhAb])
            win = sb(nc, es, "f_win", [128, L], F32)
            dwin = Dep()
            filt = [sb(nc, es, "f_filt%d" % i, [128, L], F32) for i in range(2)]
            dfilt = [Dep(), Dep()]
            fo = [sb(nc, es, "f_fo%d" % i, [128, L], BF16) for i in range(2)]
            dfo = [Dep(), Dep()]
            ssq = sb(nc, es, "f_ssq", [128, 4], F32)
            dss = Dep()
            Sf = I["S_f"]
            for o in range(2):
                for ct in range(2):
                    kb.op("dve", lambda e: e.memset(ssq[:], 0.0), writes=[dss])
                    for dr in range(2):
                        tix = (o * 2 + dr) * 2 + ct
                        kb.op("act", lambda e: e.activation(out=win[:], in_=trow[:], func=AF.Exp, scale=dec[:, tix:tix + 1]),
                              reads=[S["dw"]], writes=[dwin])
                        for tc in range(8):
                            pb = tc % 2
                            pp, dpp = S["ps"][pb], S["dps"][pb]
                            kb.op("pe", lambda e: e.matmul(pp[:], lhsT=wob[:, tix * 128:(tix + 1) * 128],
                                                           rhs=hAb[:, tc * 512:(tc + 1) * 512], start=True, stop=True),
                                  reads=[dhAb, S["dw"]], writes=[dpp])
                            kb.op("dve", lambda e: e.tensor_tensor(out=filt[dr][:, tc * 512:(tc + 1) * 512], in0=pp[:],
                                                                   in1=win[:, tc * 512:(tc + 1) * 512], op=ALU.mult),
                                  reads=[dpp, dwin], writes=[dfilt[dr]])
                        if dr == 1:
                            kb.op("dve", lambda e: e.memset(filt[1][:, 0:1], 0.0), writes=[dfilt[1]])
                        kb.op("act", lambda e: e.activation(out=win[:], in_=filt[dr][:], func=AF.Square,
                                                            accum_out=ssq[:, dr:dr + 1]),
                              reads=[dfilt[dr]], writes=[dwin, dss])
                    kb.op("dve", lambda e: e.tensor_tensor(out=ssq[:, 2:3], in0=ssq[:, 0:1], in1=ssq[:, 1:2], op=ALU.add),
                          reads=[dss], writes=[dss])
                    kb.op("act", lambda e: e.activation(out=ssq[:, 2:3], in_=ssq[:, 2:3], func=AF.Sqrt), reads=[dss], writes=[dss])
                    kb.op("dve", lambda e: e.reciprocal(out=ssq[:, 3:4], in_=ssq[:, 2:3]), reads=[dss], writes=[dss])
                    kb.op("dve", lambda e: e.tensor_scalar(out=filt[1][:], in0=filt[1][:], scalar1=ssq[:, 3:4], scalar2=None,
                                                           op0=ALU.mult), reads=[dfilt[1], dss], writes=[dfilt[1]])
                    for dr, op1 in ((0, ALU.add), (1, ALU.subtract)):
                        kb.op("dve", lambda e, dr=dr, op1=op1: e.scalar_tensor_tensor(out=fo[dr][:], in0=filt[0][:], scalar=ssq[:, 3:4],
                                                                                  in1=filt[1][:], op0=ALU.mult, op1=op1),
                              reads=[dfilt[0], dfilt[1], dss], writes=[dfo[dr]])
                        row = (o * 2 + dr) * 256 + ct * 128
                        kb.dma("sp", Sf[row:row + 128, :], fo[dr][:], reads=[dfo[dr]], wacc=[d_sf])
            kb.drain_scope([S["dw"], dhA, dhB, dhAb, dwin, dss] + S["darg"] + dfilt + dfo)

        Kh1 = sb(nc, es0, "h_K", [128, 32, 2, 256], BF16)
        dK1 = Dep()
        Kh = [Kh1, Kh1]
        dK = [dK1, dK1]
        NFC = 256 // HALF

        def run_filter(o):
            with ExitStack() as es:
                Xs = [sb(nc, es, "hx%d_%d" % (i, o), [32, CH, 128], BF16) for i in range(2)]
                dXs = [Dep(), Dep()]
                As = [sb(nc, es, "ha%d_%d" % (i, o), [128, CH, 64], BF16) for i in range(2)]
                dAs = [Dep(), Dep()]
                Sf = I["S_f"]
                W = HC["f2"]
                for ci in range(8):
                    kind, cc = ci // 4, ci % 4
                    xb = ci % 2
                    X, dX, A, dA = Xs[xb], dXs[xb], As[xb], dAs[xb]
                    r0 = (o * 2 + kind) * 256 + cc * CH
                    kb.dma("sp", X[:], Sf[r0:r0 + CH, :].rearrange("c (t1 t2) -> t1 c t2", t2=128), reads=[d_sf], writes=[dX])
                    for j0 in range(0, CH, 8):
                        pb = (j0 // 8) % 2
                        psA, dpsA = HC["psA"][pb], HC["dpsA"][pb]
                        for j in range(8):
                            kb.op("pe", lambda e, j=j: e.matmul(psA[:, j * 64:(j + 1) * 64], lhsT=X[:, j0 + j, :], rhs=HC["d1"][:],
                                                                start=True, stop=True),
                                  reads=[dX, HC["dconst"]], writes=[dpsA], inc=(j == 7))
                        if pb == 0:
                            kb.op("act", lambda e: e.copy(out=A[:, j0:j0 + 8, :], in_=psA[:].rearrange("p (j f) -> p j f", f=64)),
                                  reads=[dpsA], writes=[dA])
                        else:
                            kb.op("dve", lambda e: e.tensor_copy(out=A[:, j0:j0 + 8, :], in_=psA[:].rearrange("p (j f) -> p j f", f=64)),
                                  reads=[dpsA], writes=[dA])
                    for q in range(4):
                        pb = q % 2
                        psU, dpsU = HC["psU"][pb], HC["dpsU"][pb]
                        U8 = psU[:, 0:8 * CH].rearrange("p (a c) -> p a c", a=8)
                        for a in range(8):
                            f1 = 8 * q + a
                            Ar = A[:, :, f1]
                            Ai = A[:, :, 32 + f1]
                            m0, m1 = (0, 2) if kind == 0 else (1, 0)
                            kb.op("pe", lambda e: e.matmul(U8[:, a, :], lhsT=W[:, f1, m0, :], rhs=Ar, start=True, stop=False),
                                  reads=[dA, HC["dconst"]], writes=[dpsU], inc=False)
                            kb.op("pe", lambda e: e.matmul(U8[:, a, :], lhsT=W[:, f1, m1, :], rhs=Ai, start=False, stop=True),
                                  reads=[dA], writes=[dpsU], inc=(a == 7))
                        dst = Kh1[:, 8 * q:8 * q + 8, kind, cc * CH:(cc + 1) * CH]
                        if q % 2 == 0:
                            kb.op("act", lambda e: e.copy(out=dst, in_=U8), reads=[dpsU], writes=[dK1])
                        else:
                            kb.op("dve", lambda e: e.tensor_copy(out=dst, in_=U8), reads=[dpsU], writes=[dK1])
                kb.drain_scope(dXs + dAs)

        Sz, Sy = I["S_z"], I["S_y"]
        d_sz = Dep()

        def shortconv(es, part, ct, hin_rows, name):
            u32 = sb(nc, es, "u_32" + name, [128, L + 2], F32)
            acc = sb(nc, es, "u_acc" + name, [128, L], F32)
            du, du32, dacc = Dep(), Dep(), Dep()
            kb.op("pool", lambda e: e.memset(u32[:, 0:1], 0.0), writes=[du32])
            kb.op("pool", lambda e: e.memset(u32[:, L + 1:L + 2], 0.0), writes=[du32])
            I["load_u32"](es, u32[:, 1:L + 1], hin_rows[0], hin_rows[1], du, du32)
            ix = part * 2 + ct
            kb.op("act", lambda e: e.activation(out=acc[:], in_=u32[:, 1:L + 1], func=AF.Identity, scale=convw[:, ix, 1:2],
                                                bias=convb[:, ix:ix + 1]),
                  reads=[du32, HC["dconst"]], writes=[dacc])
            kb.op("dve", lambda e: e.scalar_tensor_tensor(out=acc[:], in0=u32[:, 0:L], scalar=convw[:, ix, 0:1],
                                                          in1=acc[:], op0=ALU.mult, op1=ALU.add),
                  reads=[du32, dacc], writes=[dacc])
            kb.op("dve", lambda e: e.scalar_tensor_tensor(out=acc[:], in0=u32[:, 2:L + 2], scalar=convw[:, ix, 2:3],
                                                          in1=acc[:], op0=ALU.mult, op1=ALU.add),
                  reads=[du32, dacc], writes=[dacc])
            return acc, dacc, [du, du32, dacc]

        for ct in range(2):
            with ExitStack() as es:
                acc, dacc, dl = shortconv(es, 2, ct, (2, ct), "v")
                kb.dma("sp", Sz[ct * 128:(ct + 1) * 128, :], acc[:], reads=[dacc], wacc=[d_sz])
                kb.drain_scope(dl + [d_sz])

        def run_conv(o):
            with ExitStack() as es:
                X0 = sb(nc, es, "cx_%d" % o, [32, CH, 128], BF16)
                dX0 = Dep()
                HC["X"] = [X0, X0]
                HC["dX"] = [dX0, dX0]
                HC["A"] = [sb(nc, es, "ca%d_%d" % (i, o), [128, CH, 64], BF16) for i in range(2)]
                HC["dA"] = [Dep(), Dep()]
                Y = [sb(nc, es, "cy%d_%d" % (i, o), [128, 64, CH], BF16) for i in range(2)]
                dY = [Dep(), Dep()]
                HC["Z"] = sb(nc, es, "cz_%d" % o, [64, CH, 128], BF16)
                Y2 = [sb(nc, es, "cy2%d_%d" % (i, o), [128, 64, CH], BF16) for i in range(2)]
                dY2 = [Dep(), Dep()]
                HC["dZ"] = Dep()
                HC["ysb"] = sb(nc, es, "cys_%d" % o, [32, CH, 128], F32)
                HC["dys"] = Dep()
                m_all = [sb(nc, es, "cm%d_%d" % (i, o), [128, 4, CH], F32) for i in range(8)]
                dm_all = [Dep() for _ in range(8)]

                def dsrc(ci):
                    return [(Sz[ci * CH:(ci + 1) * CH, :], 0, CH)]

                def dcons(ci, q, U4, dpsU):
                    yb = ci % 2
                    m = m_all[(q % 2) * 4:(q % 2) * 4 + 4]
                    dm = dm_all[(q % 2) * 4:(q % 2) * 4 + 4]
                    Kr = Kh[o][:, 4 * q:4 * q + 4, 0, ci * CH:(ci + 1) * CH]
                    Ki = Kh[o][:, 4 * q:4 * q + 4, 1, ci * CH:(ci + 1) * CH]
                    Ur, Ui = U4[:, :, 0, :], U4[:, :, 1, :]
                    for i, (a_, b_) in enumerate(((Ur, Kr), (Ui, Ki), (Ur, Ki), (Ui, Kr))):
                        kb.op("dve", lambda e, i=i, a_=a_, b_=b_: e.tensor_tensor(out=m[i][:], in0=a_, in1=b_, op=ALU.mult),
                              reads=[dpsU, dK[o]], writes=[dm[i]])
                    yr = Y[yb][:, 4 * q:4 * q + 4, :]
                    yi = Y[yb][:, 32 + 4 * q:32 + 4 * q + 4, :]
                    kb.op("pool", lambda e: e.tensor_tensor(out=yr, in0=m[0][:], in1=m[1][:], op=ALU.subtract),
                          reads=[dm[0], dm[1]], writes=[dY[yb]])
                    kb.op("pool", lambda e: e.tensor_tensor(out=yi, in0=m[2][:], in1=m[3][:], op=ALU.add),
                          reads=[dm[2], dm[3]], writes=[dY[yb]])
                    y2a = Y2[yb][:, 4 * q:4 * q + 4, :]
                    y2b = Y2[yb][:, 32 + 4 * q:32 + 4 * q + 4, :]
                    kb.op("dve", lambda e: e.scalar_tensor_tensor(out=y2a, in0=m[2][:], scalar=-1.0, in1=m[3][:],
                                                                  op0=ALU.mult, op1=ALU.subtract),
                          reads=[dm[2], dm[3]], writes=[dY2[yb]])
                    kb.op("act", lambda e: e.copy(out=y2b, in_=yr), reads=[dY[yb]], writes=[dY2[yb]])
                    if q == 7:
                        if pending_inv:
                            pending_inv.pop(0)()
                        pending_inv.append(lambda ci=ci, yb=yb: emit_fft_inv(kb, HC, ci, Y[yb], dY[yb], Sy[ci * CH:(ci + 1) * CH, :],
                                                                             Y2[yb], dY2[yb]))

                pending_inv = []
                emit_fft_fwd(kb, HC, dsrc, 256 // CH, dcons, True, d_sz)
                while pending_inv:
                    pending_inv.pop(0)()
                kb.drain_scope(HC["dX"] + HC["dA"] + dY + dY2 + [HC["dZ"], HC["dys"], HC["d_sy"]] + dm_all)

        run_filter(0)
        run_conv(0)
        for step, part in ((0, 0), (1, 1)):
            for ct in range(2):
                with ExitStack() as es:
                    yt = sb(nc, es, "g_y%d" % step, [128, L], F32)
                    zt = sb(nc, es, "g_z%d" % step, [128, L], F32)
                    dyt, dzt = Dep(), Dep()
                    kb.dma("sp", yt[:], Sy[ct * 128:(ct + 1) * 128, :], reads=[HC["d_sy"]], writes=[dyt])
                    kb.dma("sp", zt[:], Sz[ct * 128:(ct + 1) * 128, :], reads=[d_sz], writes=[dzt])
                    kb.op("dve", lambda e: e.scalar_tensor_tensor(out=yt[:], in0=zt[:], scalar=hbias[:, step * 2 + ct:step * 2 + ct + 1],
                                                                  in1=yt[:], op0=ALU.mult, op1=ALU.add),
                          reads=[dzt, dyt, HC["dconst"]], writes=[dyt])
                    acc, dacc, dl = shortconv(es, part, ct, (part, ct), "g%d" % step)
                    if step == 0:
                        kb.op("dve", lambda e: e.tensor_tensor(out=zt[:], in0=acc[:], in1=yt[:], op=ALU.mult),
                              reads=[dacc, dyt], writes=[dzt])
                        kb.dma("sp", Sz[ct * 128:(ct + 1) * 128, :], zt[:], reads=[dzt], wacc=[d_sz])
                        kb.drain_scope(dl + [dyt, dzt, d_sz])
                    else:
                        ob = sb(nc, es, "g_ob", [128, L], BF16)
                        dob = Dep()
                        kb.op("dve", lambda e: e.tensor_tensor(out=ob[:], in0=acc[:], in1=yt[:], op=ALU.mult),
                              reads=[dacc, dyt], writes=[dob])
                        kb.dma("sp", I["hout"][ct * 128:(ct + 1) * 128, :], ob[:], reads=[dob], wacc=[I["d_hout"]])
                        kb.drain_scope(dl + [dyt, dzt, dob, I["d_hout"]])
            if step == 0:
                run_filter(1)
                run_conv(1)
        kb.drain_scope([HC["dconst"]] + dK)


def emit_proj(kb, C, lhsT_of, d_w, nk, msz, rhs_of, d_rhs, ntok, consume):
    for tt in range(ntok // 512):
        pb = C["rr"] % 2
        C["rr"] += 1
        pp, dpp = C["ps_" + "ab"[pb]], C["d_ps_" + "ab"[pb]]
        for k in range(nk):
            kb.op("pe", lambda e, k=k: e.matmul(pp[0:msz, :], lhsT=lhsT_of(k), rhs=rhs_of(k, tt),
                                                start=(k == 0), stop=(k == nk - 1)),
                  reads=[d_w, d_rhs], writes=[dpp], inc=(k == nk - 1))
        consume(tt, pp, dpp)


def load_x(kb, xT, dxs, src, d_src=None, per_kq=None):
    xv = src.rearrange("(c p) t -> p c t", p=128)
    rd = [d_src] if d_src is not None else []
    for k in range(8):
        for q in range(4):
            tgt = [dxs[q]] if per_kq is None else [per_kq[k][q]]
            kb.dma("sp", xT[:, k, q * 512:(q + 1) * 512], xv[:, k, q * 512:(q + 1) * 512], reads=rd, wacc=tgt)


def store_x(kb, xT, dxs, dst, d_out):
    ov = dst.rearrange("(c p) t -> p c t", p=128)
    for k in range(8):
        kb.dma("sp", ov[:, k, :], xT[:, k, :], reads=dxs, wacc=[d_out])


def declare(nc, specs, kind):
    out = {}
    for name, (shape, dt) in specs.items():
        out[name] = nc.dram_tensor(name, list(shape), dt, kind=kind).ap()
    return out


FFN_W = lambda pre: {pre + "wg": ([11, 128, 8, 256], F32), pre + "wu": ([11, 128, 8, 256], F32),
                     pre + "wd": ([2, 4, 128, 12, 256], F32)}


P1_IN = {"xT_in": ([1024, TOK], F32), "gains": ([128, 16], F32), "w_in": ([8, 128, 8, 256], F32), **FFN_W("f00_")}
P1_OUT = {"xT_out": ([1024, TOK], F32), "projT": ([2048, TOK], BF16)}


def emit_phase1(kb, nc, C, I, O):
    d_out = Dep()
    with ExitStack() as es:
        xT = sb(nc, es, "xT", [128, 8, TOK], F32)
        gt = sb(nc, es, "gt", [128, 16], F32)
        dxs = [Dep() for _ in range(4)]
        kb.dma("sp", gt[:], I["gains"][:, :], writes=[C["d_const"]])
        load_x(kb, xT, dxs, I["xT_in"])
        emit_ffn(kb, C, xT, dxs, gt[:, 0:8], I["f00_wg"], I["f00_wu"], I["f00_wd"], "a")
        store_x(kb, xT, dxs, O["xT_out"], d_out)
        with ExitStack() as es2:
            hT = sb(nc, es2, "hT1", [128, 8, TOK], BF16)
            d_h = Dep()
            emit_rmsnorm(kb, C, xT, dxs, gt[:, 8:16], hT, d_h, 0, TOK)
            wb = [sb(nc, es2, "win%d" % i, [128, 8, 256], BF16) for i in range(2)]
            dwb = [Dep(), Dep()]
            ob = [sb(nc, es2, "pob%d" % i, [128, TOK], BF16) for i in range(2)]
            dob = [Dep(), Dep()]
            kb.dma("pool", wb[0][:], I["w_in"][0], writes=[dwb[0]])
            for g in range(8):
                if g + 1 < 8:
                    kb.dma("pool", wb[(g + 1) % 2][:], I["w_in"][g + 1], writes=[dwb[(g + 1) % 2]])
                for j in range(2):
                    oi = (g * 2 + j) % 2
                    ch = g * 2 + j

                    def cons(tt, pp, dpp, oi=oi):
                        eng = "act" if tt % 2 == 0 else "dve"
                        if eng == "act":
                            kb.op("act", lambda e: e.copy(out=ob[oi][:, tt * 512:(tt + 1) * 512], in_=pp[:]), reads=[dpp], writes=[dob[oi]])
                        else:
                            kb.op("dve", lambda e: e.tensor_copy(out=ob[oi][:, tt * 512:(tt + 1) * 512], in_=pp[:]), reads=[dpp], writes=[dob[oi]])

                    emit_proj(kb, C, lambda k, g=g, j=j: wb[g % 2][:, k, j * 128:(j + 1) * 128], dwb[g % 2], 8, 128,
                              lambda k, tt: hT[:, k, tt * 512:(tt + 1) * 512], d_h, TOK, cons)
                    O["store_proj"](ch, ob[oi], dob[oi], d_out)
            kb.drain_scope([d_h] + dwb + dob)
        kb.drain_scope(dxs + [d_out])
    return d_out


P3_IN = {"xT_in": ([1024, TOK], F32), "gains": ([128, 32], F32), "apool": ([512, TOK + 32], BF16),
         "invcnt": ([4, TOK], F32), "yhyT": ([512, TOK], BF16), "pool_w": ([128, 4, 128], F32),
         "pool_scale": ([128, 4], F32), "w_mo": ([4, 128, 8, 256], F32),
         **FFN_W("f01_"), **FFN_W("f10_"),
         "w_dq": ([128, 8, 256], F32), "w_dkv": ([128, 8, 160], F32), "qkv_g": ([128, 3], F32),
         "rope_cs": ([32, 2, TOK], F32)}
P3_OUT = {"xT_out": ([1024, TOK], F32), "cqnT": ([256, TOK], BF16), "kvlatT": ([160, TOK], BF16)}


def emit_pool(kb, nc, C, I, yT, d_y):
    with ExitStack() as es:
        W = TOK + 32
        pw32 = sb(nc, es, "pw32", [128, 4, 128], F32)
        pw = sb(nc, es, "pw", [128, 4, 128], BF16)
        psc = sb(nc, es, "psc", [128, 4], F32)
        dpw = Dep()
        kb.dma("sp", pw32[:], I["pool_w"][:, :, :], writes=[dpw])
        kb.dma("sp", psc[:], I["pool_scale"][:, :], writes=[dpw])
        kb.op("act", lambda e: e.copy(out=pw[:], in_=pw32[:]), reads=[dpw], writes=[dpw])
        ab = sb(nc, es, "pl_ab", [128, W], BF16)
        A = sb(nc, es, "pl_A", [128, W], F32)
        S1 = sb(nc, es, "pl_S1", [128, W], F32)
        S2 = sb(nc, es, "pl_S2", [128, W], F32)
        inv = sb(nc, es, "pl_inv", [128, TOK], F32)
        pb = sb(nc, es, "pl_p", [128, TOK], BF16)
        dab, dA, dS1, dS2, dinv, dp = Dep(), Dep(), Dep(), Dep(), Dep(), Dep()
        for g in range(4):
            I["load_apool"](ab, g, dab)
            kb.dma("sp", inv[:], bcast_rows(I["invcnt"][g:g + 1, :], 128), writes=[dinv])
            kb.op("act", lambda e: e.copy(out=A[:], in_=ab[:]), reads=[dab], writes=[dA])
            if "halo_mask" in I:
                hm = I["halo_mask"]
                kb.op("dve", lambda e: e.tensor_scalar(out=A[:, 0:16], in0=A[:, 0:16], scalar1=hm[:, 0:1], scalar2=None, op0=ALU.mult),
                      reads=[dA, C["d_const"]], writes=[dA])
                kb.op("dve", lambda e: e.tensor_scalar(out=A[:, W - 16:W], in0=A[:, W - 16:W], scalar1=hm[:, 1:2], scalar2=None, op0=ALU.mult),
                      reads=[dA, C["d_const"]], writes=[dA])
            kb.op("dve", lambda e: e.tensor_tensor(out=S1[:, 1:W], in0=A[:, 0:W - 1], in1=A[:, 1:W], op=ALU.add),
                  reads=[dA], writes=[dS1])
            cur, dcur, oth, doth = S1, dS1, S2, dS2
            lo, hi = 1, W
            for lvl in range(1, g + 1):
                sh = 1 << (lvl - 1)
                nlo, nhi = lo + sh, hi - sh
                kb.op("dve", lambda e, cur=cur, oth=oth, sh=sh, nlo=nlo, nhi=nhi: e.tensor_tensor(
                    out=oth[:, nlo:nhi], in0=cur[:, nlo - sh:nhi - sh], in1=cur[:, nlo + sh:nhi + sh], op=ALU.add),
                    reads=[dcur], writes=[doth])
                cur, dcur, oth, doth = oth, doth, cur, dcur
                lo, hi = nlo, nhi
            assert lo <= 16 and hi >= 16 + TOK
            kb.op("dve", lambda e, cur=cur: e.tensor_tensor(out=cur[:, 16:16 + TOK], in0=cur[:, 16:16 + TOK], in1=inv[:], op=ALU.mult),
                  reads=[dcur, dinv], writes=[dcur])
            kb.op("dve", lambda e, cur=cur: e.tensor_tensor(out=pb[:], in0=cur[:, 16:16 + TOK], in1=A[:, 16:16 + TOK], op=ALU.subtract),
                  reads=[dcur, dA], writes=[dp])

            def cons(tt, pp, dpp, g=g):
                kb.op("act", lambda e: e.activation(out=yT[:, g, tt * 512:(tt + 1) * 512], in_=pp[:], func=AF.Copy,
                                                    scale=psc[:, g:g + 1]), reads=[dpp, dpw], writes=[d_y])

            emit_proj(kb, C, lambda k, g=g: pw[:, g, :], dpw, 1, 128, lambda k, tt: pb[:, tt * 512:(tt + 1) * 512], dp, TOK, cons)
        kb.drain_scope([dpw, dab, dA, dS1, dS2, dinv, dp])


def emit_phase3(kb, nc, C, I, O):
    d_out = Dep()
    with ExitStack() as es:
        xT = sb(nc, es, "xT3", [128, 8, TOK], F32)
        gt = sb(nc, es, "gt3", [128, 32], F32)
        dxs = [Dep() for _ in range(4)]
        kb.dma("sp", gt[:], I["gains"][:, :], writes=[C["d_const"]])
        load_x(kb, xT, dxs, I["xT_in"], I.get("d_x"))
        with ExitStack() as es2:
            yT = sb(nc, es2, "yT3", [128, 8, TOK], BF16)
            d_y = Dep()
            for c in range(4):
                I["load_yhy"](yT[:, 4 + c, :], c, d_y)
            emit_pool(kb, nc, C, I, yT, d_y)
            wb = [sb(nc, es2, "wmo%d" % i, [128, 8, 256], BF16) for i in range(2)]
            dwb = [Dep(), Dep()]
            kb.dma("pool", wb[0][:], I["w_mo"][0], writes=[dwb[0]])
            for g in range(4):
                if g + 1 < 4:
                    kb.dma("pool", wb[(g + 1) % 2][:], I["w_mo"][g + 1], writes=[dwb[(g + 1) % 2]])
                for j in range(2):
                    dch = g * 2 + j

                    def cons(tt, pp, dpp, dch=dch):
                        kb.op("dve", lambda e: e.tensor_tensor(out=xT[:, dch, tt * 512:(tt + 1) * 512], in0=pp[:],
                                                               in1=xT[:, dch, tt * 512:(tt + 1) * 512], op=ALU.add),
                              reads=[dpp, dxs[tt]], writes=[dxs[tt]])

                    emit_proj(kb, C, lambda k, g=g, j=j: wb[g % 2][:, k, j * 128:(j + 1) * 128], dwb[g % 2], 8, 128,
                              lambda k, tt: yT[:, k, tt * 512:(tt + 1) * 512], d_y, TOK, cons)
            kb.drain_scope([d_y] + dwb)
        if "after_mixer" in I:
            I["after_mixer"]()
        emit_ffn(kb, C, xT, dxs, gt[:, 0:8], I["f01_wg"], I["f01_wu"], I["f01_wd"], "b")
        emit_ffn(kb, C, xT, dxs, gt[:, 8:16], I["f10_wg"], I["f10_wu"], I["f10_wd"], "c")
        store_x(kb, xT, dxs, O["xT_out"], d_out)
        with ExitStack() as es2:
            hT = sb(nc, es2, "hT3", [128, 8, TOK], BF16)
            d_h = Dep()
            emit_rmsnorm(kb, C, xT, dxs, gt[:, 16:24], hT, d_h, 0, TOK)
            wdq = sb(nc, es2, "wdq", [128, 8, 256], BF16)
            wdkv = sb(nc, es2, "wdkv", [128, 8, 160], BF16)
            wrot = sb(nc, es2, "wrot", [128, 8, 32], BF16)
            qg = sb(nc, es2, "qg", [128, 3], F32)
            cs = sb(nc, es2, "ropecs", [32, 2, TOK], F32)
            dw = Dep()
            kb.dma("pool", wdq[:], I["w_dq"][:, :, :], wacc=[dw])
            kb.dma("pool", wdkv[:], I["w_dkv"][:, :, :], wacc=[dw])
            kb.dma("sp", qg[:], I["qkv_g"][:, :], wacc=[dw])
            kb.dma("sp", cs[:], I["rope_cs"][:, :, :], wacc=[dw])
            kb.op("act", lambda e: e.mul(out=wrot[:, :, 0:16], in_=wdkv[:, :, 144:160], mul=-1.0), reads=[dw], writes=[dw])
            kb.op("act", lambda e: e.copy(out=wrot[:, :, 16:32], in_=wdkv[:, :, 128:144]), reads=[dw], writes=[dw])
            cqf = sb(nc, es2, "cqf", [128, 2, TOK], F32)
            dcq = [Dep() for _ in range(4)]
            for c in range(2):
                def cons(tt, pp, dpp, c=c):
                    kb.op("act", lambda e: e.copy(out=cqf[:, c, tt * 512:(tt + 1) * 512], in_=pp[:]), reads=[dpp], writes=[dcq[tt]])
                emit_proj(kb, C, lambda k, c=c: wdq[:, k, c * 128:(c + 1) * 128], dw, 8, 128,
                          lambda k, tt: hT[:, k, tt * 512:(tt + 1) * 512], d_h, TOK, cons)
            cqn = sb(nc, es2, "cqn", [128, 2, TOK], BF16)
            dcqn = Dep()
            emit_rmsnorm(kb, C, cqf, dcq, qg[:, 0:2], cqn, dcqn, 0, TOK, nch=2, ones=C["ones256"])
            for c in range(2):
                kb.dma("sp", O["cqnT"][c * 128:(c + 1) * 128, :], cqn[:, c, :], reads=[dcqn], wacc=[d_out])
            ckf = sb(nc, es2, "ckf", [128, 1, TOK], F32)
            dck = [Dep() for _ in range(4)]

            def cons_k(tt, pp, dpp):
                kb.op("act", lambda e: e.copy(out=ckf[:, 0, tt * 512:(tt + 1) * 512], in_=pp[:]), reads=[dpp], writes=[dck[tt]])
            emit_proj(kb, C, lambda k: wdkv[:, k, 0:128], dw, 8, 128, lambda k, tt: hT[:, k, tt * 512:(tt + 1) * 512], d_h, TOK, cons_k)
            ckn = sb(nc, es2, "ckn", [128, 1, TOK], BF16)
            dckn = Dep()
            emit_rmsnorm(kb, C, ckf, dck, qg[:, 2:3], ckn, dckn, 0, TOK, nch=1, ones=C["ones128"])
            kb.dma("sp", O["kvlatT"][0:128, :], ckn[:, 0, :], reads=[dckn], wacc=[d_out])
            kr = sb(nc, es2, "krope", [32, TOK], BF16)
            ta = sb(nc, es2, "kr_ta", [32, TOK], F32)
            tb = sb(nc, es2, "kr_tb", [32, 512], F32)
            dta, dtb, dkr = Dep(), Dep(), Dep()

            def cons_a(tt, pp, dpp):
                kb.op("dve", lambda e: e.tensor_tensor(out=ta[:, tt * 512:(tt + 1) * 512], in0=pp[0:32, :],
                                                       in1=cs[:, 0, tt * 512:(tt + 1) * 512], op=ALU.mult),
                      reads=[dpp, dw], writes=[dta])
            emit_proj(kb, C, lambda k: wdkv[:, k, 128:160], dw, 8, 32, lambda k, tt: hT[:, k, tt * 512:(tt + 1) * 512], d_h, TOK, cons_a)

            def cons_b(tt, pp, dpp):
                kb.op("dve", lambda e: e.tensor_tensor(out=tb[:], in0=pp[0:32, :], in1=cs[:, 1, tt * 512:(tt + 1) * 512], op=ALU.mult),
                      reads=[dpp, dw], writes=[dtb])
                kb.op("dve", lambda e: e.tensor_tensor(out=kr[:, tt * 512:(tt + 1) * 512], in0=ta[:, tt * 512:(tt + 1) * 512],
                                                       in1=tb[:], op=ALU.add), reads=[dta, dtb], writes=[dkr])
            emit_proj(kb, C, lambda k: wrot[:, k, :], dw, 8, 32, lambda k, tt: hT[:, k, tt * 512:(tt + 1) * 512], d_h, TOK, cons_b)
            kb.dma("sp", O["kvlatT"][128:160, :], kr[:], reads=[dkr], wacc=[d_out])
            kb.drain_scope([d_h, dw, dcqn, dckn, dta, dtb, dkr] + dcq + dck)
        kb.drain_scope(dxs + [d_out])
    return d_out


NH = 16
P4_IN = {"xT_in": ([1024, TOK], F32), "gains": ([128, 16], F32), "cqnT": ([256, TOK], BF16),
         "kvlat_full": ([2, 160, TOK], BF16), "w_uq": ([128, 2, 1536], F32), "w_ukv": ([128, 2048], F32),
         "w_o": ([128, 8, 1024], F32), "rope_q": ([96, 2, TOK], F32), **FFN_W("f11_")}
P4_OUT = {"outT": ([1024, TOK], F32)}


def emit_phase4(kb, nc, C, I, O):
    d_out = Dep()
    SC = 96 ** -0.5
    with ExitStack() as es, ExitStack() as esO:
        OT = sb(nc, esO, "OT", [128, NH // 2, TOK], BF16, side="right")
        dOT = Dep()
        with ExitStack() as es2:
            dw = Dep()
            wuq = sb(nc, es2, "wuq", [128, 2, 1536], BF16)
            wukv = sb(nc, es2, "wukv", [128, 2048], BF16)
            kb.dma("pool", wuq[:], I["w_uq"][:, :, :], wacc=[dw])
            kb.dma("pool", wukv[:], I["w_ukv"][:, :], wacc=[dw])
            wqr = sb(nc, es2, "wqr", [128, 2, NH, 96], BF16)
            kb.op("pool", lambda e: e.memset(wqr[:], 0.0), writes=[dw])
            wv = wuq[:].rearrange("p k (h c) -> p k h c", c=96)
            kb.op("act", lambda e: e.mul(out=wqr[:, :, :, 64:80], in_=wv[:, :, :, 80:96], mul=-1.0), reads=[dw], writes=[dw])
            kb.op("act", lambda e: e.copy(out=wqr[:, :, :, 80:96], in_=wv[:, :, :, 64:80]), reads=[dw], writes=[dw])
            cqn = sb(nc, es2, "cqn4", [128, 2, TOK], BF16)
            ckn = sb(nc, es2, "ckn4", [128, L], BF16)
            krp = sb(nc, es2, "krp4", [96, L], BF16)
            rq = sb(nc, es2, "ropeq", [96, 2, TOK], F32)
            dl = Dep()
            for c in range(2):
                kb.dma("sp", cqn[:, c, :], I["cqnT"][c * 128:(c + 1) * 128, :], wacc=[dl])
            for rk in range(2):
                kb.dma("sp", ckn[:, rk * TOK:(rk + 1) * TOK], I["kvlat_full"][rk, 0:128, :], reads=[I["d_kv"]], wacc=[dl])
                kb.dma("sp", krp[64:96, rk * TOK:(rk + 1) * TOK], I["kvlat_full"][rk, 128:160, :], reads=[I["d_kv"]], wacc=[dl])
            kb.dma("sp", rq[64:96, :, :], I["rope_q"][64:96, :, :], wacc=[dl])
            onesf = sb(nc, es2, "onesf", [96, 64], F32)
            kb.op("pool", lambda e: e.memset(onesf[:], 1.0), writes=[dl])
            Vt = sb(nc, es2, "Vt", [128, 32, 8, 65], BF16)
            KT = [sb(nc, es2, "KT%d" % i, [96, L], BF16) for i in range(2)]
            dKT = [Dep(), Dep()]
            QT = [sb(nc, es2, "QT%d" % i, [96, TOK], BF16) for i in range(2)]
            dQT = [Dep(), Dep()]
            PT = [sb(nc, es2, "PT%d" % i, [128, 1024], BF16) for i in range(3)]
            dPT = [Dep(), Dep(), Dep()]
            qa = sb(nc, es2, "qa", [96, 512], F32)
            qb = sb(nc, es2, "qb", [96, 512], F32)
            dqa, dqb = Dep(), Dep()
            stg = [sb(nc, es2, "ostg%d" % i, [64, 512], BF16) for i in range(2)]
            dstg = [Dep(), Dep()]
            stg_rr = [0]
            den = sb(nc, es2, "den", [96, 512], F32)
            rec = sb(nc, es2, "rec", [64, 512], F32)
            dden, drec = Dep(), Dep()
            psS = [(C["ps_a"], C["ps_b"]), (C["ps_c"], C["ps_d"])]
            psS2 = [C["ps_ab"], C["ps_cd"]]
            dpsS = [Dep(), Dep()]
            psO = [C["ps_e"], C["ps_f"]]
            dpsO = [C["d_ps_e"], C["d_ps_f"]]
            psB, dpsB = C["ps_g"], C["d_ps_g"]
            psP, dpsP = C["ps_h"], C["d_ps_h"]
            sit = 0
            oit = 0
            pit = 0
            pending_norm = []

            dVs = [Dep() for _ in range(8)]

            bank_rr = [0]
            step_banks = [(C["ps_h"], C["d_ps_h"]), (C["ps_g"], C["d_ps_g"])]

            def next_bank():
                bank_rr[0] += 1
                return step_banks[bank_rr[0] % 2]

            def v_steps(h):
                hl_ = h % 8
                for k0 in range(0, 32, 8):
                    psP, dpsP = next_bank()
                    for j in range(8):
                        kt = k0 + j
                        kb.op("pe", lambda e, j=j, kt=kt: e.matmul(psP[:, j * 64:(j + 1) * 64], lhsT=ckn[:, kt * 128:(kt + 1) * 128],
                                                                   rhs=wukv[:, h * 128 + 64:h * 128 + 128], start=True, stop=True),
                              reads=[dl, dw], writes=[dpsP], inc=(j == 7))
                    kb.op("dve", lambda e: e.tensor_copy(out=Vt[:, k0:k0 + 8, hl_, 0:64], in_=psP[:].rearrange("p (j c) -> p j c", c=64)),
                          reads=[dpsP], writes=[dVs[hl_]])
                    yield

            def setup_steps(h):
                kbuf = h % 2
                K_, dK_ = KT[kbuf], dKT[kbuf]
                Q_, dQ_ = QT[kbuf], dQT[kbuf]
                kb.op("dve", lambda e: e.tensor_copy(out=K_[64:96, :], in_=krp[64:96, :]), reads=[dl], writes=[dK_])
                for kc in range(8):
                    psP, dpsP = next_bank()
                    kb.op("pe", lambda e: e.matmul(psP[0:64, :], lhsT=wukv[:, h * 128:h * 128 + 64],
                                                   rhs=ckn[:, kc * 512:(kc + 1) * 512], start=True, stop=True),
                          reads=[dl, dw], writes=[dpsP])
                    kb.op("dve", lambda e: e.tensor_copy(out=K_[0:64, kc * 512:(kc + 1) * 512], in_=psP[0:64, :]), reads=[dpsP], writes=[dK_])
                    yield
                for qt in range(4):
                    psP, dpsP = next_bank()
                    for kk in range(2):
                        kb.op("pe", lambda e: e.matmul(psP[0:96, :], lhsT=wuq[:, kk, h * 96:(h + 1) * 96],
                                                       rhs=cqn[:, kk, qt * 512:(qt + 1) * 512], start=(kk == 0), stop=(kk == 1)),
                              reads=[dl, dw], writes=[dpsP], inc=(kk == 1))
                    kb.op("dve", lambda e: e.tensor_copy(out=Q_[0:64, qt * 512:(qt + 1) * 512], in_=psP[0:64, :]), reads=[dpsP], writes=[dQ_])
                    kb.op("dve", lambda e: e.tensor_tensor(out=qa[64:96, :], in0=psP[64:96, :], in1=rq[64:96, 0, qt * 512:(qt + 1) * 512],
                                                           op=ALU.mult), reads=[dpsP, dl], writes=[dqa])
                    yield
                    psP, dpsP = next_bank()
                    for kk in range(2):
                        kb.op("pe", lambda e: e.matmul(psP[0:96, :], lhsT=wqr[:, kk, h, :],
                                                       rhs=cqn[:, kk, qt * 512:(qt + 1) * 512], start=(kk == 0), stop=(kk == 1)),
                              reads=[dl, dw], writes=[dpsP], inc=(kk == 1))
                    kb.op("dve", lambda e: e.tensor_tensor(out=qb[64:96, :], in0=psP[64:96, :], in1=rq[64:96, 1, qt * 512:(qt + 1) * 512],
                                                           op=ALU.mult), reads=[dpsP, dl], writes=[dqb])
                    kb.op("dve", lambda e: e.tensor_tensor(out=Q_[64:96, qt * 512:(qt + 1) * 512], in0=qa[64:96, :], in1=qb[64:96, :],
                                                           op=ALU.add), reads=[dqa, dqb], writes=[dQ_])
                    yield

            steps = []

            def run_step():
                while steps:
                    try:
                        next(steps[0])
                        return
                    except StopIteration:
                        steps.pop(0)

            def flush_steps():
                while steps:
                    run_step()

            def emit_norm(h, qt, ob):
                kb.op("dve", lambda e: e.tensor_copy(out=den[64:65, :], in_=psO[ob][64:65, :]), reads=[dpsO[ob]], writes=[dden])
                kb.op("pe", lambda e: e.matmul(psB[0:64, :], lhsT=onesf[64:65, :], rhs=den[64:65, :], start=True, stop=True),
                      reads=[dden, dl], writes=[dpsB])
                kb.op("dve", lambda e: e.reciprocal(out=rec[:], in_=psB[0:64, :]), reads=[dpsB], writes=[drec])
                if h % 2 == 0:
                    kb.op("dve", lambda e: e.tensor_tensor(out=OT[0:64, h // 2, qt * 512:(qt + 1) * 512], in0=psO[ob][0:64, :], in1=rec[:],
                                                           op=ALU.mult), reads=[dpsO[ob], drec], wacc=[dOT])
                else:
                    sg_i = stg_rr[0] % 2
                    stg_rr[0] += 1
                    kb.op("dve", lambda e: e.tensor_tensor(out=stg[sg_i][:], in0=psO[ob][0:64, :], in1=rec[:], op=ALU.mult),
                          reads=[dpsO[ob], drec], writes=[dstg[sg_i]])
                    kb.dma("sp", OT[64:128, h // 2, qt * 512:(qt + 1) * 512], stg[sg_i][:], reads=[dstg[sg_i]], wacc=[dOT])

            kb.op("pool", lambda e: e.memset(Vt[:, :, :, 64:65], 1.0), writes=dVs)
            steps.append(v_steps(0))
            steps.append(setup_steps(0))
            flush_steps()
            for h in range(NH):
                hg, hl = h // 8, h % 8
                kbuf = h % 2
                if h == 0:
                    steps.append(v_steps(1))
                    steps.append(setup_steps(1))
                    for h0 in range(2, 8):
                        steps.append(v_steps(h0))
                else:
                    if h + 1 < NH:
                        steps.append(setup_steps(h + 1))
                    if h + 7 < NH:
                        steps.append(v_steps(h + 7))
                K_, dK_ = KT[kbuf], dKT[kbuf]
                Q_, dQ_ = QT[kbuf], dQT[kbuf]
                for qt in range(4):
                    ob = oit % 2
                    oit += 1

                    def emit_S(kp):
                        nonlocal sit
                        sbuf_i = sit % 2
                        sit += 1
                        pA, pB = psS[sbuf_i]
                        for i, pp in enumerate((pA, pB)):
                            kt = kp * 2 + i
                            kb.op("pe", lambda e, pp=pp, kt=kt: e.matmul(pp[:], lhsT=K_[:, kt * 128:(kt + 1) * 128],
                                                                         rhs=Q_[:, qt * 512:(qt + 1) * 512], start=True, stop=True),
                                  reads=[dK_, dQ_], writes=[dpsS[sbuf_i]], inc=(i == 1))
                        return sbuf_i

                    def emit_exp(sbuf_i):
                        nonlocal pit
                        pA, pB = psS[sbuf_i]
                        pt = pit % 3
                        pit += 1
                        kb.op("act", lambda e: e.activation(out=PT[pt][:, :], in_=psS2[sbuf_i][:, :], func=AF.Exp, scale=SC),
                              reads=[dpsS[sbuf_i]], writes=[dPT[pt]])
                        return pt

                    def emit_PV(kp, pt):
                        for i in range(2):
                            kt = kp * 2 + i
                            kb.op("pe", lambda e, i=i, kt=kt: e.matmul(psO[ob][0:65, :], lhsT=Vt[:, kt, hl, :],
                                                                       rhs=PT[pt][:, i * 512:(i + 1) * 512],
                                                                       start=(kt == 0), stop=(kt == 31)),
                                  reads=[dVs[hl], dPT[pt]], writes=[dpsO[ob]], inc=(kt == 31 or i == 1))

                    sb_ = {0: emit_S(0), 1: emit_S(1)}
                    for kp in range(16):
                        pt = emit_exp(sb_[kp])
                        if kp + 2 < 16:
                            sb_[kp + 2] = emit_S(kp + 2)
                        emit_PV(kp, pt)
                        if kp == 2 and pending_norm:
                            emit_norm(*pending_norm.pop())
                        elif kp >= 3:
                            run_step()
                    pending_norm.append((h, qt, ob))
                flush_steps()
            emit_norm(*pending_norm.pop())
            kb.drain_scope([dw, dl, dqa, dqb, dden, drec] + dstg + dVs + dKT + dQT + dPT + dpsS)
        xT = sb(nc, es, "xT4", [128, 8, TOK], F32)
        gt = sb(nc, es, "gt4", [128, 16], F32)
        dxs = [Dep() for _ in range(4)]
        kb.dma("sp", gt[:], I["gains"][:, :], writes=[C["d_const"]])
        dxk = [[Dep() for _ in range(4)] for _ in range(8)]
        load_x(kb, xT, dxs, I["xT_in"], I.get("d_x"), per_kq=dxk)
        with ExitStack() as es2:
            wo = sb(nc, es2, "wo4", [128, NH // 2, 1024], BF16)
            dwo = Dep()
            for hq in range(4):
                kb.dma("pool", wo[:, hq * 2:(hq + 1) * 2, :], I["w_o"][:, hq * 2:(hq + 1) * 2, :], wacc=[dwo])
            for dch in range(8):
                def cons(tt, pp, dpp, dch=dch):
                    kb.op("dve", lambda e: e.tensor_tensor(out=xT[:, dch, tt * 512:(tt + 1) * 512], in0=pp[:],
                                                           in1=xT[:, dch, tt * 512:(tt + 1) * 512], op=ALU.add),
                          reads=[dpp, dxk[dch][tt]], writes=[dxs[tt]])
                emit_proj(kb, C, lambda k, dch=dch: wo[:, k, dch * 128:(dch + 1) * 128], dwo, NH // 2, 128,
                          lambda k, tt: OT[:, k, tt * 512:(tt + 1) * 512], dOT, TOK, cons)
            kb.drain_scope([dwo, dOT])
        esO.close()
        emit_ffn(kb, C, xT, dxs, gt[:, 0:8], I["f11_wg"], I["f11_wu"], I["f11_wd"], "d")
        with ExitStack() as es2:
            oT = sb(nc, es2, "oT4", [128, 8, TOK], F32)
            d_o = Dep()
            emit_rmsnorm(kb, C, xT, dxs, gt[:, 8:16], oT, d_o, 0, TOK)
            ov = O["outT"].rearrange("(c p) t -> p c t", p=128)
            for k in range(8):
                kb.dma("sp", ov[:, k, :], oT[:, k, :], reads=[d_o], wacc=[d_out])
            kb.drain_scope([d_o, d_out])
        kb.drain_scope(dxs)
    return d_out


BFNP = ml_dtypes.bfloat16


def tile_kxm(w, gw=256):
    K, M = w.shape
    return np.ascontiguousarray(w.reshape(K // 128, 128, M // gw, gw).transpose(2, 1, 0, 3))


def pk(v):
    return np.ascontiguousarray(v.reshape(-1, 128).T)


def tile_wd(w):
    t = w.reshape(22, 128, 4, 256).transpose(2, 1, 0, 3)
    out = np.zeros((2, 4, 128, 12, 256), w.dtype)
    out[0] = t[:, :, 0:12, :]
    out[1, :, :, 0:10, :] = t[:, :, 12:22, :]
    return out


def ffn_w(inp, l, j, pre):
    return {pre + "wg": tile_kxm(inp["ffn_w_gate"][l, j]), pre + "wu": tile_kxm(inp["ffn_w_up"][l, j]),
            pre + "wd": tile_wd(inp["ffn_w_down"][l, j])}


def rope_tables(r):
    inv_freq = (10000.0 ** (-np.arange(0, 32, 2, dtype=np.float32) / 32)).astype(np.float32)
    pos = np.arange(r * TOK, (r + 1) * TOK, dtype=np.float32)
    ang = (pos[:, None] * inv_freq[None, :]).astype(np.float32)
    c = np.cos(ang).astype(np.float32).T
    s = np.sin(ang).astype(np.float32).T
    cs = np.zeros((32, 2, TOK), np.float32)
    cs[0:16, 0] = c
    cs[16:32, 0] = c
    cs[0:16, 1] = s
    cs[16:32, 1] = s
    return cs


def invcnt_table(r):
    t = np.arange(r * TOK, (r + 1) * TOK)
    out = np.zeros((4, TOK), np.float32)
    for g, w in enumerate((2, 4, 8, 16)):
        lo = np.clip(t - w // 2, 0, L)
        hi = np.clip(t - w // 2 + w, 0, L)
        out[g] = (1.0 / (hi - lo).astype(np.float32)).astype(np.float32)
    return out


def hyena_core_inputs(P, r):
    W = 512
    d = {}
    cw = P["hyena_conv_w"][0]
    cb = P["hyena_conv_b"][0]
    convw = np.zeros((128, 6, 3), np.float32)
    convb = np.zeros((128, 6), np.float32)
    for part in range(3):
        for ct in range(2):
            cols = part * W + 256 * r + ct * 128 + np.arange(128)
            convw[:, part * 2 + ct, :] = cw[:, cols].T
            convb[:, part * 2 + ct] = cb[cols]
    d["conv_w"] = convw
    d["conv_b"] = convb
    hb = np.zeros((128, 4), np.float32)
    for o in range(2):
        for ct in range(2):
            hb[:, o * 2 + ct] = P["hyena_bias"][0][o, 256 * r + ct * 128 + np.arange(128)]
    d["hy_bias"] = hb
    d["w1"] = np.ascontiguousarray(P["hyena_ffn_w1"][0])
    d["w2"] = np.ascontiguousarray(P["hyena_ffn_w2"][0])
    d["w3"] = np.ascontiguousarray(P["hyena_ffn_w3"][0])
    d["b123"] = np.ascontiguousarray(np.stack([P["hyena_ffn_b1"][0], P["hyena_ffn_b2"][0], P["hyena_ffn_b3"][0]], 1))
    d["sin_freq"] = np.ascontiguousarray(P["hyena_sin_freq"][0].T)
    wo = P["hyena_ffn_w_out"][0].reshape(64, 2, 2, W)
    dec = P["hyena_decay"][0]
    woc = np.zeros((64, 1024), np.float32)
    decc = np.zeros((128, 8), np.float32)
    for o in range(2):
        for dr in range(2):
            for ct in range(2):
                tix = (o * 2 + dr) * 2 + ct
                cols = 256 * r + ct * 128 + np.arange(128)
                woc[:, tix * 128:(tix + 1) * 128] = wo[:, o, dr, cols]
                decc[:, tix] = dec[o, dr, cols]
    d["w_out"] = woc
    d["decay"] = decc
    return d


P2_IN = {"conv_w": ([128, 6, 3], F32), "conv_b": ([128, 6], F32), "hy_bias": ([128, 4], F32), "w1": ([33, 64], F32),
         "w2": ([64, 64], F32), "w3": ([64, 64], F32), "b123": ([64, 3], F32), "sin_freq": ([64, 3], F32),
         "w_out": ([64, 1024], F32), "decay": ([128, 8], F32), "c_zT": ([33, L], F32), "c_trow": ([1, L], F32),
         "c_d1": ([32, 64], BF16), "c_f2": ([128, 32, 3, 128], BF16), "c_i1": ([128, 2, 256], BF16),
         "c_i2": ([64, 128, 32], BF16), "hin": ([768, L], BF16)}
P2_OUT = {"hout": ([256, L], BF16)}


def build_program(phase):
    nc = bass.Bass("TRN2", target_bir_lowering=False)
    ins, outs, emit = {1: (P1_IN, P1_OUT, emit_phase1), 3: (P3_IN, P3_OUT, emit_phase3),
                       4: (P4_IN, P4_OUT, emit_phase4), 2: (P2_IN, P2_OUT, None)}[phase]
    I = declare(nc, ins, "ExternalInput")
    O = declare(nc, outs, "ExternalOutput")
    with ExitStack() as es:
        kb = KB(nc, es)
        I["d_kv"] = Dep()
        def load_u32(es_, dst, part, ct, du, du32):
            ub = sb(nc, es_, "u_b", [128, L], BF16)
            kb.dma("sp", ub[:], I["hin"][part * 256 + ct * 128:part * 256 + (ct + 1) * 128, :], writes=[du])
            kb.op("act", lambda e: e.copy(out=dst, in_=ub[:]), reads=[du], writes=[du32])
        I["load_u32"] = load_u32
        I["load_apool"] = lambda ab, g, dab: kb.dma("sp", ab[:], I["apool"][g * 128:(g + 1) * 128, :], writes=[dab])
        I["load_yhy"] = lambda dst, c, d_y: kb.dma("sp", dst, I["yhyT"][c * 128:(c + 1) * 128, :], wacc=[d_y])
        if phase == 1:
            O["store_proj"] = lambda ch, ob, dob, d_out: kb.dma("sp", O["projT"][ch * 128:(ch + 1) * 128, :], ob[:], reads=[dob], wacc=[d_out])
        if phase == 2:
            I["hout"] = O["hout"]
            I["S_f"] = nc.dram_tensor("S_f", [1024, L], BF16).ap()
            I["S_z"] = nc.dram_tensor("S_z", [256, L], F32).ap()
            I["S_y"] = nc.dram_tensor("S_y", [256, L], F32).ap()
            I["d_hout"] = Dep()
            psb = [(ps(nc, es, "psb%d" % i, [128, 512]), Dep()) for i in range(8)]
            emit_hyena(kb, nc, I, psb)
            kb.wait_all("sp", [I["d_hout"]])
        else:
            C = common_tiles(nc, es, kb)
            d_out = emit(kb, nc, C, I, O)
            kb.wait_all("sp", [d_out])
    return nc


_PROGS = {}


def get_program(phase):
    if phase not in _PROGS:
        _PROGS[phase] = build_program(phase)
    return _PROGS[phase]


def run(phase, maps):
    res = run_bass_kernel_spmd(get_program(phase), maps, core_ids=list(range(NCORES)))
    return res.results


def kernel_unfused(**inp):
    inp = {k: np.asarray(v) for k, v in inp.items()}
    x = inp["x"]
    cores = [(c // 2, c % 2) for c in range(NCORES)]
    g1 = np.concatenate([pk(inp["norm_g"][0, 0]), pk(inp["norm_g"][0, 1])], axis=1)
    w_in = tile_kxm(inp["mix_w_in"][0])
    f00 = ffn_w(inp, 0, 0, "f00_")
    maps = []
    for (b, r) in cores:
        m = {"xT_in": np.ascontiguousarray(x[b, r * TOK:(r + 1) * TOK, :].T), "gains": g1, "w_in": w_in}
        m.update(f00)
        maps.append(m)
    r1 = run(1, maps)
    consts = {**hyena_constants(), **hyena_pos_constants()}
    hy = [hyena_core_inputs(inp, r) for r in range(2)]
    maps = []
    for (b, r) in cores:
        proj = np.concatenate([np.asarray(r1[2 * b]["projT"]), np.asarray(r1[2 * b + 1]["projT"])], axis=1)
        hin = np.concatenate([proj[512 + p * 512 + 256 * r: 512 + p * 512 + 256 * r + 256] for p in range(3)], axis=0)
        m = dict(consts)
        m.update(hy[r])
        m["hin"] = np.ascontiguousarray(hin)
        maps.append(m)
    r2 = run(2, maps)
    g3 = np.zeros((128, 32), np.float32)
    g3[:, 0:8] = pk(inp["norm_g"][0, 2])
    g3[:, 8:16] = pk(inp["norm_g"][1, 0])
    g3[:, 16:24] = pk(inp["norm_g"][1, 1])
    pool_w = np.ascontiguousarray(inp["pool_w"][0].transpose(1, 0, 2))
    pool_scale = pk(inp["pool_scale"][0])
    w_mo = tile_kxm(inp["mix_w_out"][0])
    f01 = ffn_w(inp, 0, 1, "f01_")
    f10 = ffn_w(inp, 1, 0, "f10_")
    w_dq = np.ascontiguousarray(inp["mla_w_dq"][0].reshape(8, 128, 256).transpose(1, 0, 2))
    w_dkv = np.ascontiguousarray(inp["mla_w_dkv"][0].reshape(8, 128, 160).transpose(1, 0, 2))
    qkv_g = np.concatenate([pk(inp["mla_q_norm_g"][0]), pk(inp["mla_kv_norm_g"][0])], axis=1)
    maps = []
    for (b, r) in cores:
        own = np.asarray(r1[2 * b + r]["projT"])[0:512]
        oth = np.asarray(r1[2 * b + 1 - r]["projT"])[0:512]
        ap = np.zeros((512, TOK + 32), BFNP)
        ap[:, 16:16 + TOK] = own
        if r == 0:
            ap[:, 16 + TOK:16 + TOK + 16] = oth[:, 0:16]
        else:
            ap[:, 0:16] = oth[:, TOK - 16:TOK]
        yhy = np.concatenate([np.asarray(r2[2 * b]["hout"])[:, r * TOK:(r + 1) * TOK],
                              np.asarray(r2[2 * b + 1]["hout"])[:, r * TOK:(r + 1) * TOK]], axis=0)
        m = {"xT_in": np.asarray(r1[2 * b + r]["xT_out"]), "gains": g3, "apool": ap, "invcnt": invcnt_table(r),
             "yhyT": np.ascontiguousarray(yhy), "pool_w": pool_w, "pool_scale": pool_scale, "w_mo": w_mo,
             "w_dq": w_dq, "w_dkv": w_dkv, "qkv_g": qkv_g, "rope_cs": rope_tables(r)}
        m.update(f01)
        m.update(f10)
        maps.append(m)
    r3 = run(3, maps)
    g4 = np.concatenate([pk(inp["norm_g"][1, 2]), pk(inp["final_norm_g"])], axis=1)
    w_uq = np.ascontiguousarray(inp["mla_w_uq"][0].reshape(2, 128, 1536).transpose(1, 0, 2))
    w_ukv = np.ascontiguousarray(inp["mla_w_ukv"][0])
    w_o = np.ascontiguousarray(inp["mla_w_o"][0].reshape(8, 128, 1024).transpose(1, 0, 2))
    f11 = ffn_w(inp, 1, 1, "f11_")
    maps = []
    for (b, r) in cores:
        kv = np.stack([np.asarray(r3[2 * b]["kvlatT"]), np.asarray(r3[2 * b + 1]["kvlatT"])], axis=0)
        rq = np.zeros((96, 2, TOK), np.float32)
        rq[64:96] = rope_tables(r)
        m = {"xT_in": np.asarray(r3[2 * b + r]["xT_out"]), "gains": g4, "cqnT": np.asarray(r3[2 * b + r]["cqnT"]),
             "kvlat_full": np.ascontiguousarray(kv), "w_uq": w_uq, "w_ukv": w_ukv, "w_o": w_o, "rope_q": rq}
        m.update(f11)
        maps.append(m)
    r4 = run(4, maps)
    out = np.zeros((4, L, D), np.float32)
    for c, (b, r) in enumerate(cores):
        out[b, r * TOK:(r + 1) * TOK, :] = np.asarray(r4[c]["outT"]).T
    return out


I32 = mybir.dt.int32
FUSED_IN = {}
FUSED_IN.update({"xT_in": P1_IN["xT_in"], "g1": ([128, 16], F32), "w_in": P1_IN["w_in"], **FFN_W("f00_")})
FUSED_IN.update({k: v for k, v in P2_IN.items() if k != "hin"})
FUSED_IN.update({k: v for k, v in P3_IN.items() if k not in ("xT_in", "gains", "apool", "yhyT")})
FUSED_IN.update({"g3": ([128, 32], F32)})
FUSED_IN.update({k: v for k, v in P4_IN.items() if k not in ("xT_in", "gains", "cqnT", "kvlat_full")})
FUSED_IN.update({"g4": ([128, 16], F32), "idx_u": ([128, 4], I32), "idx_y": ([128, 4], I32), "halo_mask": ([128, 2], F32)})
FUSED_OUT = {"outT": ([1024, TOK], F32)}


def build_fused(upto=4):
    nc = bass.Bass("TRN2", target_bir_lowering=False)
    IN = declare(nc, FUSED_IN, "ExternalInput")
    OUT = declare(nc, FUSED_OUT, "ExternalOutput")
    xs1 = nc.dram_tensor("xs1", [1024, TOK], F32)
    projP = nc.dram_tensor("projP_i", [512, TOK], BF16)
    projH = [nc.dram_tensor("projH%d_i" % i, [512, TOK], BF16) for i in range(3)]
    GH = [nc.dram_tensor("GH%d" % i, [1024, TOK], BF16) for i in range(3)]
    halo = nc.dram_tensor("halo_i", [512, 32], BF16)
    Ghalo = nc.dram_tensor("Ghalo", [1024, 32], BF16)
    hout = nc.dram_tensor("hout_i", [256, L], BF16)
    G2 = nc.dram_tensor("G2", [512, L], BF16)
    xs3 = nc.dram_tensor("xs3", [1024, TOK], F32)
    cqnT = nc.dram_tensor("cqnT_i", [256, TOK], BF16)
    kvl = nc.dram_tensor("kvl_i", [160, TOK], BF16)
    G3 = nc.dram_tensor("G3", [320, TOK], BF16)
    S_f = nc.dram_tensor("S_f", [1024, L], BF16)
    S_z = nc.dram_tensor("S_z", [256, L], F32)
    S_y = nc.dram_tensor("S_y", [256, L], F32)
    with ExitStack() as es:
        kb = KB(nc, es)
        C = common_tiles(nc, es, kb)
        idx_u = sb(nc, es, "idx_u", [128, 4], I32)
        idx_y = sb(nc, es, "idx_y", [128, 4], I32)
        hmask = sb(nc, es, "hmask", [128, 2], F32)
        kb.dma("pool", idx_u[:], IN["idx_u"][:, :], writes=[C["d_const"]])
        kb.dma("pool", idx_y[:], IN["idx_y"][:, :], writes=[C["d_const"]])
        kb.dma("pool", hmask[:], IN["halo_mask"][:, :], writes=[C["d_const"]])
        kb.wait_all("pool", [C["d_const"]])
        I1 = {"xT_in": IN["xT_in"], "gains": IN["g1"], "w_in": IN["w_in"], "f00_wg": IN["f00_wg"], "f00_wu": IN["f00_wu"], "f00_wd": IN["f00_wd"]}
        def store_proj(ch, ob, dob, d_out):
            if ch < 4:
                kb.dma("sp", projP.ap()[ch * 128:(ch + 1) * 128, :], ob[:], reads=[dob], wacc=[d_out])
                kb.dma("sp", halo.ap()[ch * 128:(ch + 1) * 128, 0:16], ob[:, 0:16], reads=[dob], wacc=[d_out])
                kb.dma("sp", halo.ap()[ch * 128:(ch + 1) * 128, 16:32], ob[:, TOK - 16:TOK], reads=[dob], wacc=[d_out])
            else:
                part, cc = (ch - 4) // 4, (ch - 4) % 4
                kb.dma("sp", projH[part].ap()[cc * 128:(cc + 1) * 128, :], ob[:], reads=[dob], wacc=[d_out])
        d1 = emit_phase1(kb, nc, C, I1, {"xT_out": xs1.ap(), "store_proj": store_proj})
        dG1 = Dep()
        for i in range(3):
            kb.allgather_pairs(projH[i], GH[i], [d1], dG1, acc=True)
        kb.allgather_pairs(halo, Ghalo, [d1], dG1, acc=True)
        if upto == 1:
            return _finish_early(kb, nc, OUT, xs1)
        I2 = {k: IN[k] for k in P2_IN if k != "hin"}
        I2.update({"hout": hout.ap(), "S_f": S_f.ap(), "S_z": S_z.ap(), "S_y": S_y.ap(), "d_hout": Dep()})
        def load_u32(es_, dst, part, ct, du, du32):
            ua = sb(nc, es_, "u_a", [128, L], BF16)
            ubb = sb(nc, es_, "u_bb", [128, L], BF16)
            for th in range(2):
                kb.dma("sp", ua[:, th * TOK:(th + 1) * TOK], GH[part].ap()[th * 512 + ct * 128:th * 512 + (ct + 1) * 128, :], reads=[dG1], wacc=[du])
                kb.dma("sp", ubb[:, th * TOK:(th + 1) * TOK], GH[part].ap()[th * 512 + 256 + ct * 128:th * 512 + 256 + (ct + 1) * 128, :], reads=[dG1], wacc=[du])
            kb.op("act", lambda e: e.activation(out=dst, in_=ua[:], func=AF.Copy, scale=hmask[:, 1:2]), reads=[du, C["d_const"]], writes=[du32])
            kb.op("dve", lambda e: e.scalar_tensor_tensor(out=dst, in0=ubb[:], scalar=hmask[:, 0:1], in1=dst, op0=ALU.mult, op1=ALU.add),
                  reads=[du, du32], writes=[du32])
        I2["load_u32"] = load_u32
        psb = [(C["ps_" + n], C["d_ps_" + n]) for n in "abcdefgh"]
        emit_hyena(kb, nc, I2, psb)
        dG2 = Dep()
        kb.allgather_pairs(hout, G2, [I2["d_hout"]], dG2)
        if upto == 2:
            return _finish_early(kb, nc, OUT, xs1)
        I3 = {k: IN[k] for k in P3_IN if k not in ("xT_in", "gains", "apool", "yhyT")}
        I3.update({"xT_in": xs1.ap(), "gains": IN["g3"], "halo_mask": hmask, "d_x": d1})
        projap = projP.ap()
        Ghap = Ghalo.ap()

        es_y3 = ExitStack()
        ytmp = [sb(nc, es_y3, "ytmp%d" % i, [128, L], BF16, side="right") for i in range(2)]
        dyt = [Dep(), Dep()]

        def load_yhy(dst, c, d_y):
            i = c % 2
            kb.dma("act", ytmp[i][:], G2.ap()[c * 128:(c + 1) * 128, :], reads=[dG2], writes=[dyt[i]])
            kb.op("act", lambda e: e.activation(out=dst, in_=ytmp[i][:, 0:TOK], func=AF.Copy, scale=hmask[:, 1:2]),
                  reads=[dyt[i], C["d_const"]], writes=[d_y])
            kb.op("dve", lambda e: e.scalar_tensor_tensor(out=dst, in0=ytmp[i][:, TOK:L], scalar=hmask[:, 0:1], in1=dst,
                                                          op0=ALU.mult, op1=ALU.add), reads=[dyt[i], d_y], writes=[d_y])

        def load_apool(ab, g, dab):
            kb.dma("sp", ab[:, 16:16 + TOK], projap[g * 128:(g + 1) * 128, :], reads=[d1], wacc=[dab])
            kb.dma("sp", ab[:, 0:16], Ghap[g * 128:(g + 1) * 128, 16:32], reads=[dG1], wacc=[dab])
            kb.dma("sp", ab[:, 16 + TOK:32 + TOK], Ghap[512 + g * 128:512 + (g + 1) * 128, 0:16], reads=[dG1], wacc=[dab])
        I3["load_yhy"] = load_yhy

        def after_mixer():
            kb.drain_scope(dyt)
            es_y3.close()
        I3["after_mixer"] = after_mixer
        I3["load_apool"] = load_apool
        d3 = emit_phase3(kb, nc, C, I3, {"xT_out": xs3.ap(), "cqnT": cqnT.ap(), "kvlatT": kvl.ap()})
        kb.drain_scope([d3])
        dG3 = Dep()
        kb.allgather_pairs(kvl, G3, [d3], dG3)
        if upto == 3:
            return _finish_early(kb, nc, OUT, xs3)
        I4 = {k: IN[k] for k in P4_IN if k not in ("xT_in", "gains", "cqnT", "kvlat_full")}
        I4.update({"xT_in": xs3.ap(), "gains": IN["g4"], "cqnT": cqnT.ap(),
                   "kvlat_full": G3.ap().rearrange("(k c) t -> k c t", k=2), "d_kv": dG3})
        d4 = emit_phase4(kb, nc, C, I4, {"outT": OUT["outT"]})
        kb.wait_all("sp", [d4])
        kb.drain_scope([d4])
    return nc


def _finish_early(kb, nc, OUT, src):
    d = Dep()
    kb.dma("sp", OUT["outT"], src.ap(), writes=[d])
    kb.wait_all("sp", [d])
    return nc


_FUSED = []
UPTO = 4
DBG_STATIC_U = False


def kernel(**inp):
    inp = {k: np.asarray(v) for k, v in inp.items()}
    x = inp["x"]
    if not _FUSED:
        _FUSED.append(build_fused(UPTO))
    nc = _FUSED[0]
    cores = [(c // 2, c % 2) for c in range(NCORES)]
    shared = {}
    shared["g1"] = np.concatenate([pk(inp["norm_g"][0, 0]), pk(inp["norm_g"][0, 1])], axis=1)
    shared["w_in"] = tile_kxm(inp["mix_w_in"][0])
    shared.update(ffn_w(inp, 0, 0, "f00_"))
    shared.update(hyena_constants())
    shared.update(hyena_pos_constants())
    g3 = np.zeros((128, 32), np.float32)
    g3[:, 0:8] = pk(inp["norm_g"][0, 2])
    g3[:, 8:16] = pk(inp["norm_g"][1, 0])
    g3[:, 16:24] = pk(inp["norm_g"][1, 1])
    shared["g3"] = g3
    shared["pool_w"] = np.ascontiguousarray(inp["pool_w"][0].transpose(1, 0, 2))
    shared["pool_scale"] = pk(inp["pool_scale"][0])
    shared["w_mo"] = tile_kxm(inp["mix_w_out"][0])
    shared.update(ffn_w(inp, 0, 1, "f01_"))
    shared.update(ffn_w(inp, 1, 0, "f10_"))
    shared["w_dq"] = np.ascontiguousarray(inp["mla_w_dq"][0].reshape(8, 128, 256).transpose(1, 0, 2))
    shared["w_dkv"] = np.ascontiguousarray(inp["mla_w_dkv"][0].reshape(8, 128, 160).transpose(1, 0, 2))
    shared["qkv_g"] = np.concatenate([pk(inp["mla_q_norm_g"][0]), pk(inp["mla_kv_norm_g"][0])], axis=1)
    shared["g4"] = np.concatenate([pk(inp["norm_g"][1, 2]), pk(inp["final_norm_g"])], axis=1)
    shared["w_uq"] = np.ascontiguousarray(inp["mla_w_uq"][0].reshape(2, 128, 1536).transpose(1, 0, 2))
    shared["w_ukv"] = np.ascontiguousarray(inp["mla_w_ukv"][0])
    shared["w_o"] = np.ascontiguousarray(inp["mla_w_o"][0].reshape(8, 128, 1024).transpose(1, 0, 2))
    shared.update(ffn_w(inp, 1, 1, "f11_"))
    hy = [hyena_core_inputs(inp, r) for r in range(2)]
    p = np.arange(128)
    maps = []
    for (b, r) in cores:
        m = dict(shared)
        m.update(hy[r])
        m["xT_in"] = np.ascontiguousarray(x[b, r * TOK:(r + 1) * TOK, :].T)
        m["invcnt"] = invcnt_table(r)
        m["rope_cs"] = rope_tables(r)
        rq = np.zeros((96, 2, TOK), np.float32)
        rq[64:96] = m["rope_cs"]
        m["rope_q"] = rq
        iu = np.zeros((128, 4), np.int32)
        for ct in range(2):
            for th in range(2):
                iu[:, ct * 2 + th] = th * 512 + 256 * r + ct * 128 + p
        m["idx_u"] = iu
        iy = np.zeros((128, 4), np.int32)
        for c in range(4):
            iy[:, c] = (c // 2) * 512 + ((c % 2) * 128 + p) * 2 + r
        m["idx_y"] = iy
        hm = np.zeros((128, 2), np.float32)
        hm[:, 0] = 1.0 if r == 1 else 0.0
        hm[:, 1] = 1.0 if r == 0 else 0.0
        m["halo_mask"] = hm
        maps.append(m)
    res = run_bass_kernel_spmd(nc, maps, core_ids=list(range(NCORES)))
    out = np.zeros((4, L, D), np.float32)
    for c, (b, r) in enumerate(cores):
        out[b, r * TOK:(r + 1) * TOK, :] = np.asarray(res.results[c]["outT"]).T
    return out
```

```python
import math
from contextlib import ExitStack

import numpy as np
import ml_dtypes

import concourse.bass as bass
import concourse.mybir as mybir
from concourse.bass_utils import run_bass_kernel_spmd

F32 = mybir.dt.float32
BF16 = mybir.dt.bfloat16
AF = mybir.ActivationFunctionType
ALU = mybir.AluOpType
AX = mybir.AxisListType

D = 1024
DFF = 2816
NF = DFF // 128
L = 4096
TOK = 2048
EPS = 1e-6
NCORES = 8


class Dep:
    __slots__ = ("w", "r", "name")

    def __init__(self, name=""):
        self.w = []
        self.r = []
        self.name = name


class Slot:
    __slots__ = ("sem", "val")

    def __init__(self, sem):
        self.sem = sem
        self.val = 0


class KB:
    ENG = ("pe", "act", "dve", "pool", "sp")

    def __init__(self, nc, es, ring_sizes=None):
        self.nc = nc
        self.es = es
        self.engs = {"pe": nc.tensor, "act": nc.scalar, "dve": nc.vector,
                     "pool": nc.gpsimd, "sp": nc.sync}
        self.sem = {}
        self.cnt = {}
        self.waited = {e: {} for e in self.ENG}
        for e in ("pe", "act", "dve", "pool"):
            self.sem[e] = es.enter_context(nc.semaphore("s_" + e))
            self.cnt[e] = 0
        rs = {"sp": 40, "pool": 24, "act": 8}
        if ring_sizes:
            rs.update(ring_sizes)
        self.rings = {}
        self.ring_pos = {}
        for q, n in rs.items():
            self.rings[q] = [Slot(es.enter_context(nc.semaphore("d_%s%d" % (q, i)))) for i in range(n)]
            self.ring_pos[q] = 0
        self.ninstr = {e: 0 for e in self.ENG}
        self.cc = []

    def _wait(self, E, tok):
        src, val = tok
        if isinstance(src, Slot):
            key = id(src)
            if self.waited[E].get(key, 0) >= val:
                return
            self.engs[E].wait_ge(src.sem, val)
            self.waited[E][key] = val
        else:
            if src == E and src == "pe":
                return
            if self.waited[E].get(src, 0) >= val:
                return
            self.engs[E].wait_ge(self.sem[src], val)
            self.waited[E][src] = val
        self.ninstr[E] += 1

    def _deps(self, E, reads, writes):
        for d in reads:
            for t in d.w:
                self._wait(E, t)
        for d in writes:
            for t in d.w:
                self._wait(E, t)
            for t in d.r:
                self._wait(E, t)

    def _commit(self, tok, reads, writes, wacc=()):
        for d in wacc:
            d.w.append(tok)
        for d in reads:
            d.r.append(tok)
            if len(d.r) > 64:
                d.r = d.r[-48:]
        for d in writes:
            d.w = [tok]
            d.r = []

    def op(self, E, fn, reads=(), writes=(), inc=True, wacc=()):
        self._deps(E, reads, writes)
        for d in wacc:
            for t in d.r:
                self._wait(E, t)
        ins = fn(self.engs[E])
        self.ninstr[E] += 1
        if inc:
            self.cnt[E] += 1
            ins.then_inc(self.sem[E], 1)
            tok = (E, self.cnt[E])
        else:
            tok = (E, self.cnt[E] + 1)
        self._commit(tok, reads, writes, wacc)
        return tok

    def dma(self, q, out, in_, reads=(), writes=(), wacc=(), **kw):
        ring = self.rings[q]
        slot = ring[self.ring_pos[q] % len(ring)]
        self.ring_pos[q] += 1
        if slot.val > 0:
            self._wait(q, (slot, slot.val))
        self._deps(q, reads, writes)
        for d in wacc:
            for t in d.r:
                self._wait(q, t)
        ins = self.engs[q].dma_start(out=out, in_=in_, **kw)
        self.ninstr[q] += 1
        slot.val += 16
        ins.then_inc(slot.sem, 16)
        tok = (slot, slot.val)
        self._commit(tok, reads, writes, wacc)
        return tok

    def gather(self, out, in_, idx_ap, reads=(), writes=(), wacc=()):
        q = "pool"
        ring = self.rings[q]
        slot = ring[self.ring_pos[q] % len(ring)]
        self.ring_pos[q] += 1
        if slot.val > 0:
            self._wait(q, (slot, slot.val))
        self._deps(q, reads, writes)
        for d in wacc:
            for t in d.r:
                self._wait(q, t)
        ins = self.engs[q].indirect_dma_start(out=out, out_offset=None, in_=in_,
                                              in_offset=bass.IndirectOffsetOnAxis(ap=idx_ap, axis=0))
        self.ninstr[q] += 1
        slot.val += 16
        ins.then_inc(slot.sem, 16)
        tok = (slot, slot.val)
        self._commit(tok, reads, writes, wacc)
        return tok

    def allgather_pairs(self, src_t, dst_t, src_deps, d_dst, acc=False):
        for d in src_deps:
            for t in d.w:
                self._wait("pool", t)
        for t in d_dst.r + d_dst.w:
            self._wait("pool", t)
        sem = self.es.enter_context(self.nc.semaphore("cc%d" % len(self.cc)))
        slot = Slot(sem)
        self.cc.append(slot)
        self.nc.gpsimd.collective_compute("AllGather", ALU.bypass, replica_groups=[[0, 1], [2, 3], [4, 5], [6, 7]],
                                          ins=[src_t.ap().opt()], outs=[dst_t.ap().opt()]).then_inc(sem)
        slot.val = 1
        self.ninstr["pool"] += 1
        if acc:
            d_dst.w.append((slot, 1))
        else:
            d_dst.w = [(slot, 1)]
            d_dst.r = []

    def wait_all(self, E, deps):
        for d in deps:
            for t in d.w:
                self._wait(E, t)


_UNIQ = [0]


def sb(nc, es, name, shape, dt, side=None):
    _UNIQ[0] += 1
    if side is None:
        return es.enter_context(nc.sbuf_tensor("%s_%d" % (name, _UNIQ[0]), list(shape), dt))
    return es.enter_context(nc.sbuf_tensor("%s_%d" % (name, _UNIQ[0]), list(shape), dt, side=side))


def ps(nc, es, name, shape, dt=F32):
    return es.enter_context(nc.psum_tensor(name, list(shape), dt))


def emit_rmsnorm(kb, C, xT, dxs, g_ap, hT, d_h, t0, ntok, h0=0, nch=8, ones=None, np_=128):
    if ones is None:
        ones = C["ones_b"]
    for tt in range(ntok // 512):
        a = t0 + tt * 512
        d_x = dxs[a // 512]
        for k in range(nch):
            kb.op("act", lambda e, k=k: e.activation(out=C["sq"][0:np_, k, :], in_=xT[:, k, a:a + 512], func=AF.Square),
                  reads=[d_x], writes=[C["d_sq"]])
        for k in range(nch):
            kb.op("pe", lambda e, k=k: e.matmul(C["ps_ss"][0:np_, :], lhsT=ones[0:np_, 0:np_], rhs=C["sq"][0:np_, k, :],
                                                 start=(k == 0), stop=(k == nch - 1)),
                  reads=[C["d_sq"], C["d_const"]], writes=[C["d_ps_ss"]], inc=(k == nch - 1))
        kb.op("act", lambda e: e.activation(out=C["rstd"][0:np_, :], in_=C["ps_ss"][0:np_, :], func=AF.Sqrt, bias=C["eps"][0:np_, 0:1]),
              reads=[C["d_ps_ss"], C["d_const"]], writes=[C["d_rstd"]])
        kb.op("dve", lambda e: e.reciprocal(out=C["rstd"][0:np_, :], in_=C["rstd"][0:np_, :]),
              reads=[C["d_rstd"]], writes=[C["d_rstd"]])
        for k in range(nch):
            o = h0 + tt * 512
            kb.op("dve", lambda e, k=k, o=o: e.scalar_tensor_tensor(
                out=hT[:, k, o:o + 512], in0=xT[:, k, a:a + 512], scalar=g_ap[:, k:k + 1],
                in1=C["rstd"][0:np_, :], op0=ALU.mult, op1=ALU.mult),
                reads=[d_x, C["d_rstd"], C["d_const"]], writes=[d_h])


def emit_ffn(kb, C, xT, dxs, g_ap, wg_t, wu_t, wd_t, tag):
    nc = kb.nc
    with ExitStack() as es:
        hT = sb(nc, es, "hT" + tag, [128, 8, TOK], BF16)
        aT = sb(nc, es, "aT" + tag, [128, 12, TOK], BF16)
        NB = 3
        wgb = [sb(nc, es, "wg%d%s" % (i, tag), [128, 8, 256], BF16) for i in range(NB)]
        wub = [sb(nc, es, "wu%d%s" % (i, tag), [128, 8, 256], BF16) for i in range(NB)]
        wdb = [sb(nc, es, "wd%d%s" % (i, tag), [128, 12, 256], BF16) for i in range(2)]
        sg = [sb(nc, es, "sg%d%s" % (i, tag), [128, 512], F32) for i in range(2)]
        d_wg = [Dep() for _ in range(NB)]
        d_wu = [Dep() for _ in range(NB)]
        d_wd = [Dep() for _ in range(2)]
        d_sg = [Dep() for _ in range(2)]
        d_h = Dep()
        d_a = [Dep() for _ in range(12)]
        ps_g = [C["ps_a"], C["ps_b"]]
        ps_u = [C["ps_c"], C["ps_d"]]
        d_psg = [C["d_ps_a"], C["d_ps_b"]]
        d_psu = [C["d_ps_c"], C["d_ps_d"]]
        ps_y = [C["ps_e"], C["ps_f"]]
        d_psy = [C["d_ps_e"], C["d_ps_f"]]

        def load_gu(fg):
            b = fg % NB
            kb.dma("pool", wgb[b][:], wg_t[fg], writes=[d_wg[b]])
            kb.dma("pool", wub[b][:], wu_t[fg], writes=[d_wu[b]])

        def load_d(p, dg):
            b = dg % 2
            nfc = 12 if p == 0 else 10
            kb.dma("pool", wdb[b][:, 0:nfc, :], wd_t[p, dg, :, 0:nfc, :], writes=[d_wd[b]])

        load_gu(0)
        load_gu(1)
        emit_rmsnorm(kb, C, xT, dxs, g_ap, hT, d_h, 0, TOK)
        it = 0
        yit = 0
        for p in range(2):
            g0, g1 = (0, 6) if p == 0 else (6, 11)
            nfc = (g1 - g0) * 2
            for fg in range(g0, g1):
                if fg + 2 < 11:
                    load_gu(fg + 2)
                if fg == g1 - 2:
                    load_d(p, 0)
                if fg == g1 - 1:
                    load_d(p, 1)
                b = fg % NB
                for j in range(2):
                    fl = (fg - g0) * 2 + j
                    for tt in range(4):
                        pb = it % 2
                        it += 1
                        for k in range(8):
                            kb.op("pe", lambda e, k=k: e.matmul(
                                ps_g[pb][:], lhsT=wgb[b][:, k, j * 128:(j + 1) * 128],
                                rhs=hT[:, k, tt * 512:(tt + 1) * 512], start=(k == 0), stop=(k == 7)),
                                reads=[d_wg[b], d_h], writes=[d_psg[pb]], inc=(k == 7))
                        for k in range(8):
                            kb.op("pe", lambda e, k=k: e.matmul(
                                ps_u[pb][:], lhsT=wub[b][:, k, j * 128:(j + 1) * 128],
                                rhs=hT[:, k, tt * 512:(tt + 1) * 512], start=(k == 0), stop=(k == 7)),
                                reads=[d_wu[b], d_h], writes=[d_psu[pb]], inc=(k == 7))
                        kb.op("act", lambda e: e.activation(out=sg[pb][:], in_=ps_g[pb][:], func=AF.Silu),
                              reads=[d_psg[pb]], writes=[d_sg[pb]])
                        kb.op("dve", lambda e: e.tensor_tensor(
                            out=aT[:, fl, tt * 512:(tt + 1) * 512], in0=ps_u[pb][:], in1=sg[pb][:], op=ALU.mult),
                            reads=[d_psu[pb], d_sg[pb]], writes=[d_a[fl]])
            for dg in range(4):
                b = dg % 2
                for dj in range(2):
                    dch = dg * 2 + dj
                    for tt in range(4):
                        yb = yit % 2
                        yit += 1
                        for f in range(nfc):
                            kb.op("pe", lambda e, f=f: e.matmul(
                                ps_y[yb][:], lhsT=wdb[b][:, f, dj * 128:(dj + 1) * 128],
                                rhs=aT[:, f, tt * 512:(tt + 1) * 512], start=(f == 0), stop=(f == nfc - 1)),
                                reads=[d_wd[b], d_a[f]], writes=[d_psy[yb]], inc=(f == nfc - 1))
                        a = tt * 512
                        d_x = dxs[tt]
                        kb.op("dve", lambda e: e.scalar_tensor_tensor(
                            out=xT[:, dch, a:a + 512], in0=ps_y[yb][:], scalar=0.5,
                            in1=xT[:, dch, a:a + 512], op0=ALU.mult, op1=ALU.add),
                            reads=[d_psy[yb], d_x], writes=[d_x])
                if dg + 2 < 4:
                    load_d(p, dg + 2)
        kb.drain_scope([d_h] + d_a + d_wg + d_wu + d_wd + d_sg)


def _drain_scope(self, deps):
    toks = []
    for d in deps:
        toks += d.w + d.r
    for E in ("pe", "act", "dve", "pool", "sp"):
        for t in toks:
            self._wait(E, t)


KB.drain_scope = _drain_scope


def common_tiles(nc, es, kb):
    C = {}
    C["ones_b"] = sb(nc, es, "ones_b", [128, 128], BF16)
    C["sq"] = sb(nc, es, "sq", [128, 8, 512], BF16)
    C["rstd"] = sb(nc, es, "rstd", [128, 512], F32)
    for n2 in ("ab", "cd", "ef", "gh"):
        t2 = ps(nc, es, "ps_" + n2, [128, 1024])
        C["ps_" + n2] = t2
        C["ps_" + n2[0]] = t2[:, 0:512]
        C["ps_" + n2[1]] = t2[:, 512:1024]
    for n in "abcdefgh":
        C["d_ps_" + n] = Dep()
    C["ps_ss"] = C["ps_g"]
    C["d_ps_ss"] = C["d_ps_g"]
    for n in ("d_sq", "d_rstd", "d_const"):
        C[n] = Dep()
    C["rr"] = 0
    C["eps"] = sb(nc, es, "eps", [128, 1], F32)
    kb.op("pool", lambda e: e.memset(C["ones_b"][:], 1.0 / 1024.0), writes=[C["d_const"]])
    kb.op("pool", lambda e: e.memset(C["eps"][:], EPS), writes=[C["d_const"]])
    C["ones256"] = sb(nc, es, "ones256", [128, 128], BF16)
    C["ones128"] = sb(nc, es, "ones128", [128, 128], BF16)
    kb.op("pool", lambda e: e.memset(C["ones256"][:], 1.0 / 256.0), writes=[C["d_const"]])
    kb.op("pool", lambda e: e.memset(C["ones128"][:], 1.0 / 128.0), writes=[C["d_const"]])
    return C


NFFT = 8192
CH = 64
HALF = CH // 2


def hyena_constants():
    bf = ml_dtypes.bfloat16
    t1 = np.arange(32)[:, None].astype(np.float64)
    f1 = np.arange(32)[None, :].astype(np.float64)
    ang = 2 * np.pi * (f1 + 0.5) * t1 / 64.0
    d1 = np.concatenate([np.cos(ang), -np.sin(ang)], axis=1)
    t2 = np.arange(128)[:, None].astype(np.float64)
    f2 = np.arange(128)[None, :].astype(np.float64)
    w = np.zeros((128, 32, 3, 128))
    for a in range(32):
        th = 2 * np.pi * ((a + 0.5) * t2 / NFFT + f2 * t2 / 128.0)
        w[:, a, 0] = np.cos(th)
        w[:, a, 1] = -np.sin(th)
        w[:, a, 2] = np.sin(th)
    f2c = np.arange(128)[:, None].astype(np.float64)
    t2r = np.arange(128)[None, :].astype(np.float64)
    ph = 2 * np.pi * f2c * t2r / 128.0
    i1 = np.zeros((128, 2, 256))
    i1[:, 0, :128] = np.cos(ph)
    i1[:, 0, 128:] = np.sin(ph)
    i1[:, 1, :128] = -np.sin(ph)
    i1[:, 1, 128:] = np.cos(ph)
    g = np.zeros((64, 128, 32))
    f1c = np.arange(32)[:, None].astype(np.float64)
    t1r = np.arange(32)[None, :].astype(np.float64)
    for b in range(128):
        phi = 2 * np.pi * (f1c + 0.5) * (t1r / 64.0 + b / NFFT)
        g[0:32, b] = (2.0 / NFFT) * np.cos(phi)
        g[32:64, b] = -(2.0 / NFFT) * np.sin(phi)
    return {"c_d1": d1.astype(bf), "c_f2": w.astype(bf), "c_i1": i1.astype(bf), "c_i2": g.astype(bf)}


def hyena_pos_constants():
    f32 = np.float32
    t = np.linspace(0.0, 1.0, L, dtype=f32)
    bands = 16
    w = (2.0 * math.pi * np.arange(L, dtype=f32) / L).astype(f32)
    f = np.linspace(1e-4, bands - 1, bands, dtype=f32)
    phase = (w[:, None] * f[None, :]).astype(f32)
    z = np.concatenate([t[:, None], np.cos(phase), -np.sin(phase)], axis=-1).astype(f32)
    return {"c_zT": np.ascontiguousarray(z.T), "c_trow": np.ascontiguousarray(-t[None, :])}


def bcast_rows(ap_row, n):
    if hasattr(ap_row, "broadcast"):
        return ap_row.broadcast(0, n)
    return ap_row[0:1, :].to_broadcast([n, ap_row.shape[1]])


def emit_fft_fwd(kb, HC, src, ncols_chunks, consumer, src_is_f32, src_dep):
    nc = kb.nc
    for ci in range(ncols_chunks):
        xb = ci % 2
        X, dX = HC["X"][xb], HC["dX"][xb]
        for (ap, c0, n) in src(ci):
            kb.dma("pool" if src_is_f32 else "sp", X[:, c0:c0 + n, :],
                   ap.rearrange("c (t1 t2) -> t1 c t2", t2=128), reads=[src_dep], writes=[dX])
        A, dA = HC["A"][xb], HC["dA"][xb]
        for j0 in range(0, CH, 8):
            pb = (j0 // 8) % 2
            psA, dpsA = HC["psA"][pb], HC["dpsA"][pb]
            for j in range(8):
                kb.op("pe", lambda e, j=j: e.matmul(psA[:, j * 64:(j + 1) * 64], lhsT=X[:, j0 + j, :], rhs=HC["d1"][:],
                                                    start=True, stop=True),
                      reads=[dX, HC["dconst"]], writes=[dpsA], inc=(j == 7))
            eng = "act" if (j0 // 8) % 2 == 0 else "dve"
            if eng == "act":
                kb.op("act", lambda e: e.copy(out=A[:, j0:j0 + 8, :], in_=psA[:].rearrange("p (j f) -> p j f", f=64)),
                      reads=[dpsA], writes=[dA])
            else:
                kb.op("dve", lambda e: e.tensor_copy(out=A[:, j0:j0 + 8, :], in_=psA[:].rearrange("p (j f) -> p j f", f=64)),
                      reads=[dpsA], writes=[dA])
        for q in range(8):
            pb = q % 2
            psU, dpsU = HC["psU"][pb], HC["dpsU"][pb]
            U4 = psU[:, 0:8 * CH].rearrange("p (a r c) -> p a r c", a=4, r=2)
            for a in range(4):
                f1 = 4 * q + a
                Ar = A[:, :, f1]
                Ai = A[:, :, 32 + f1]
                W = HC["f2"]
                kb.op("pe", lambda e: e.matmul(U4[:, a, 0, :], lhsT=W[:, f1, 0, :], rhs=Ar, start=True, stop=False),
                      reads=[dA, HC["dconst"]], writes=[dpsU], inc=False)
                kb.op("pe", lambda e: e.matmul(U4[:, a, 0, :], lhsT=W[:, f1, 2, :], rhs=Ai, start=False, stop=True),
                      reads=[dA], writes=[dpsU], inc=False)
                kb.op("pe", lambda e: e.matmul(U4[:, a, 1, :], lhsT=W[:, f1, 1, :], rhs=Ar, start=True, stop=False),
                      reads=[dA], writes=[dpsU], inc=False)
                kb.op("pe", lambda e: e.matmul(U4[:, a, 1, :], lhsT=W[:, f1, 0, :], rhs=Ai, start=False, stop=True),
                      reads=[dA], writes=[dpsU], inc=(a == 3))
            consumer(ci, q, U4, dpsU)


def emit_fft_inv(kb, HC, ci, Y, dY, dst, Y2=None, dY2=None):
    Z, dZ = HC["Z"], HC["dZ"]
    for j0 in range(0, CH, 4):
        pb = (j0 // 4) % 2
        psZ, dpsZ = HC["psZ"][pb], HC["dpsZ"][pb]
        for j in range(4):
            kb.op("pe", lambda e, j=j: e.matmul(psZ[0:64, j * 128:(j + 1) * 128], lhsT=Y[:, :, j0 + j],
                                                rhs=HC["i1"][:, 0, 0:128], start=True, stop=False),
                  reads=[dY, HC["dconst"]], writes=[dpsZ], inc=False)
            kb.op("pe", lambda e, j=j: e.matmul(psZ[0:64, j * 128:(j + 1) * 128], lhsT=Y2[:, :, j0 + j],
                                                rhs=HC["i1"][:, 0, 128:256], start=False, stop=True),
                  reads=[dY2], writes=[dpsZ], inc=(j == 3))
        if pb == 0:
            kb.op("act", lambda e: e.copy(out=Z[:, j0:j0 + 4, :], in_=psZ[0:64, :].rearrange("p (j f) -> p j f", f=128)),
                  reads=[dpsZ], writes=[dZ])
        else:
            kb.op("dve", lambda e: e.tensor_copy(out=Z[:, j0:j0 + 4, :], in_=psZ[0:64, :].rearrange("p (j f) -> p j f", f=128)),
                  reads=[dpsZ], writes=[dZ])
    ysb, dys = HC["ysb"], HC["dys"]
    G = HC["i2"]
    for b0 in range(0, 128, 8):
        pb = (b0 // 8) % 2
        psy, dpsy = HC["psy"][pb], HC["dpsy"][pb]
        for b in range(8):
            t2 = b0 + b
            kb.op("pe", lambda e, b=b, t2=t2: e.matmul(psy[0:32, b * CH:(b + 1) * CH], lhsT=G[:, t2, :],
                                                     rhs=Z[:, :, t2], start=True, stop=True),
                  reads=[dZ, HC["dconst"]], writes=[dpsy], inc=(b == 7))
        src = psy[0:32, 0:8 * CH].rearrange("p (b c) -> p b c", c=CH)
        dstv = ysb[:, :, b0:b0 + 8].rearrange("p c b -> p b c")
        if pb == 0:
            kb.op("act", lambda e: e.copy(out=dstv, in_=src), reads=[dpsy], writes=[dys])
        else:
            kb.op("dve", lambda e: e.tensor_copy(out=dstv, in_=src), reads=[dpsy], writes=[dys])
    kb.dma("sp", dst.rearrange("c (t1 t2) -> t1 c t2", t2=128), ysb[:], reads=[dys], wacc=[HC["d_sy"]])


def emit_sin_layer(kb, nc, S, w_sb, kdim, rhs_of, fcol, bfcol, hout, d_in, d_out):
    MAGIC = 12582912.0
    for tc in range(8):
        pb = tc % 2
        pp, dpp = S["ps"][pb], S["dps"][pb]
        kb.op("pe", lambda e: e.matmul(pp[0:64, :], lhsT=w_sb, rhs=rhs_of(tc), start=True, stop=True),
              reads=[d_in, S["dw"]], writes=[dpp])
        a, k = S["arg"][pb], S["kk"][pb]
        da = S["darg"][pb]
        kb.op("act", lambda e: e.activation(out=a[:], in_=pp[0:64, :], func=AF.Identity, scale=fcol, bias=bfcol),
              reads=[dpp, S["dw"]], writes=[da])
        kb.op("dve", lambda e: e.tensor_scalar(out=k[:], in0=a[:], scalar1=1.0 / (2 * math.pi), scalar2=MAGIC,
                                               op0=ALU.mult, op1=ALU.add), reads=[da], writes=[da])
        kb.op("dve", lambda e: e.tensor_scalar(out=k[:], in0=k[:], scalar1=MAGIC, scalar2=2 * math.pi,
                                               op0=ALU.subtract, op1=ALU.mult), reads=[da], writes=[da])
        kb.op("dve", lambda e: e.tensor_tensor(out=a[:], in0=a[:], in1=k[:], op=ALU.subtract), reads=[da], writes=[da])
        kb.op("act", lambda e: e.activation(out=hout[:, tc * 512:(tc + 1) * 512], in_=a[:], func=AF.Sin,
                                            scale=1.0 - 2e-6), reads=[da], writes=[d_out])


def emit_hyena(kb, nc, I, psb):
    with ExitStack() as es0:
        HC = {}
        HC["dconst"] = Dep()
        HC["d1"] = sb(nc, es0, "h_d1", [32, 64], BF16)
        HC["f2"] = sb(nc, es0, "h_f2", [128, 32, 3, 128], BF16)
        HC["i1"] = sb(nc, es0, "h_i1", [128, 2, 256], BF16)
        HC["i2"] = sb(nc, es0, "h_i2", [64, 128, 32], BF16)
        def load_tables():
            kb.dma("sp", HC["d1"][:], I["c_d1"][:, :], wacc=[HC["dconst"]])
            kb.dma("sp", HC["f2"][:], I["c_f2"][:, :, :, :], wacc=[HC["dconst"]])
            kb.dma("sp", HC["i1"][:], I["c_i1"][:, :, :], wacc=[HC["dconst"]])
            kb.dma("sp", HC["i2"][:], I["c_i2"][:, :, :], wacc=[HC["dconst"]])
        convw = sb(nc, es0, "h_convw", [128, 6, 3], F32)
        convb = sb(nc, es0, "h_convb", [128, 6], F32)
        hbias = sb(nc, es0, "h_bias", [128, 4], F32)
        kb.dma("sp", convw[:], I["conv_w"][:, :, :], wacc=[HC["dconst"]])
        kb.dma("sp", convb[:], I["conv_b"][:, :], wacc=[HC["dconst"]])
        kb.dma("sp", hbias[:], I["hy_bias"][:, :], wacc=[HC["dconst"]])
        names = ["psA", "psU", "psZ", "psy"]
        for i, n in enumerate(names):
            HC[n] = [psb[2 * i][0], psb[2 * i + 1][0]]
            HC["d" + n] = [psb[2 * i][1], psb[2 * i + 1][1]]
        HC["d_sy"] = Dep()
        d_sf = Dep()

        with ExitStack() as es:
            S = {}
            S["ps"] = [psb[0][0], psb[1][0]]
            S["dps"] = [psb[0][1], psb[1][1]]
            S["dw"] = Dep()
            zT = sb(nc, es, "f_zT", [33, L], F32)
            w1 = sb(nc, es, "f_w1", [33, 64], F32)
            w2 = sb(nc, es, "f_w2", [64, 64], F32)
            w3 = sb(nc, es, "f_w3", [64, 64], F32)
            wo = sb(nc, es, "f_wo", [64, 1024], F32)
            bb = sb(nc, es, "f_b", [64, 3], F32)
            fq = sb(nc, es, "f_fq", [64, 3], F32)
            bf_ = sb(nc, es, "f_bf", [64, 3], F32)
            dec = sb(nc, es, "f_dec", [128, 8], F32)
            trow = sb(nc, es, "f_trow", [128, L], F32)
            for t_, s_ in ((zT, I["c_zT"]), (w1, I["w1"]), (w2, I["w2"]), (w3, I["w3"]), (wo, I["w_out"]),
                           (bb, I["b123"]), (fq, I["sin_freq"]), (dec, I["decay"])):
                kb.dma("sp", t_[:], s_[:, :], wacc=[S["dw"]])
            kb.dma("sp", trow[:], bcast_rows(I["c_trow"], 128), wacc=[S["dw"]])
            load_tables()
            kb.op("dve", lambda e: e.tensor_tensor(out=bf_[:], in0=bb[:], in1=fq[:], op=ALU.mult),
                  reads=[S["dw"]], writes=[S["dw"]])
            kb.op("act", lambda e: e.activation(out=dec[:], in_=dec[:], func=AF.Abs), reads=[S["dw"]], writes=[S["dw"]])
            S["arg"] = [sb(nc, es, "f_arg%d" % i, [64, 512], F32) for i in range(2)]
            S["kk"] = [sb(nc, es, "f_kk%d" % i, [64, 512], F32) for i in range(2)]
            S["darg"] = [Dep(), Dep()]
            hA = sb(nc, es, "f_hA", [64, L], F32)
            hB = sb(nc, es, "f_hB", [64, L], F32)
            dhA, dhB = Dep(), Dep()
            emit_sin_layer(kb, nc, S, w1[:], 33, lambda tc: zT[:, tc * 512:(tc + 1) * 512], fq[:, 0:1], bf_[:, 0:1], hA, S["dw"], dhA)
            emit_sin_layer(kb, nc, S, w2[:], 64, lambda tc: hA[:, tc * 512:(tc + 1) * 512], fq[:, 1:2], bf_[:, 1:2], hB, dhA, dhB)
            emit_sin_layer(kb, nc, S, w3[:], 64, lambda tc: hB[:, tc * 512:(tc + 1) * 512], fq[:, 2:3], bf_[:, 2:3], hA, dhB, dhA)
            wob = sb(nc, es, "f_wob", [64, 1024], BF16)
            hAb = sb(nc, es, "f_hAb", [64, L], BF16)
            dhAb = Dep()
            kb.op("act", lambda e: e.copy(out=wob[:], in_=wo[:]), reads=[S["dw"]], writes=[S["dw"]])
            kb.op("act", lambda e: e.copy(out=hAb[:], in_=hA[:]), reads=[dhA], writes=[dhAb])
            win = sb(nc, es, "f_win", [128, L], F32)
            dwin = Dep()
            filt = [sb(nc, es, "f_filt%d" % i, [128, L], F32) for i in range(2)]
            dfilt = [Dep(), Dep()]
            fo = [sb(nc, es, "f_fo%d" % i, [128, L], BF16) for i in range(2)]
            dfo = [Dep(), Dep()]
            ssq = sb(nc, es, "f_ssq", [128, 4], F32)
            dss = Dep()
            Sf = I["S_f"]
            for o in range(2):
                for ct in range(2):
                    kb.op("dve", lambda e: e.memset(ssq[:], 0.0), writes=[dss])
                    for dr in range(2):
                        tix = (o * 2 + dr) * 2 + ct
                        kb.op("act", lambda e: e.activation(out=win[:], in_=trow[:], func=AF.Exp, scale=dec[:, tix:tix + 1]),
                              reads=[S["dw"]], writes=[dwin])
                        for tc in range(8):
                            pb = tc % 2
                            pp, dpp = S["ps"][pb], S["dps"][pb]
                            kb.op("pe", lambda e: e.matmul(pp[:], lhsT=wob[:, tix * 128:(tix + 1) * 128],
                                                           rhs=hAb[:, tc * 512:(tc + 1) * 512], start=True, stop=True),
                                  reads=[dhAb, S["dw"]], writes=[dpp])
                            kb.op("dve", lambda e: e.tensor_tensor(out=filt[dr][:, tc * 512:(tc + 1) * 512], in0=pp[:],
                                                                   in1=win[:, tc * 512:(tc + 1) * 512], op=ALU.mult),
                                  reads=[dpp, dwin], writes=[dfilt[dr]])
                        if dr == 1:
                            kb.op("dve", lambda e: e.memset(filt[1][:, 0:1], 0.0), writes=[dfilt[1]])
                        kb.op("act", lambda e: e.activation(out=win[:], in_=filt[dr][:], func=AF.Square,
                                                            accum_out=ssq[:, dr:dr + 1]),
                              reads=[dfilt[dr]], writes=[dwin, dss])
                    kb.op("dve", lambda e: e.tensor_tensor(out=ssq[:, 2:3], in0=ssq[:, 0:1], in1=ssq[:, 1:2], op=ALU.add),
                          reads=[dss], writes=[dss])
                    kb.op("act", lambda e: e.activation(out=ssq[:, 2:3], in_=ssq[:, 2:3], func=AF.Sqrt), reads=[dss], writes=[dss])
                    kb.op("dve", lambda e: e.reciprocal(out=ssq[:, 3:4], in_=ssq[:, 2:3]), reads=[dss], writes=[dss])
                    kb.op("dve", lambda e: e.tensor_scalar(out=filt[1][:], in0=filt[1][:], scalar1=ssq[:, 3:4], scalar2=None,
                                                           op0=ALU.mult), reads=[dfilt[1], dss], writes=[dfilt[1]])
                    for dr, op1 in ((0, ALU.add), (1, ALU.subtract)):
                        kb.op("dve", lambda e, dr=dr, op1=op1: e.scalar_tensor_tensor(out=fo[dr][:], in0=filt[0][:], scalar=ssq[:, 3:4],
                                                                                  in1=filt[1][:], op0=ALU.mult, op1=op1),
                              reads=[dfilt[0], dfilt[1], dss], writes=[dfo[dr]])
                        row = (o * 2 + dr) * 256 + ct * 128
                        kb.dma("sp", Sf[row:row + 128, :], fo[dr][:], reads=[dfo[dr]], wacc=[d_sf])
            kb.drain_scope([S["dw"], dhA, dhB, dhAb, dwin, dss] + S["darg"] + dfilt + dfo)

        Kh1 = sb(nc, es0, "h_K", [128, 32, 2, 256], BF16)
        dK1 = Dep()
        Kh = [Kh1, Kh1]
        dK = [dK1, dK1]
        NFC = 256 // HALF

        def run_filter(o):
            with ExitStack() as es:
                Xs = [sb(nc, es, "hx%d_%d" % (i, o), [32, CH, 128], BF16) for i in range(2)]
                dXs = [Dep(), Dep()]
                As = [sb(nc, es, "ha%d_%d" % (i, o), [128, CH, 64], BF16) for i in range(2)]
                dAs = [Dep(), Dep()]
                Sf = I["S_f"]
                W = HC["f2"]
                for ci in range(8):
                    kind, cc = ci // 4, ci % 4
                    xb = ci % 2
                    X, dX, A, dA = Xs[xb], dXs[xb], As[xb], dAs[xb]
                    r0 = (o * 2 + kind) * 256 + cc * CH
                    kb.dma("sp", X[:], Sf[r0:r0 + CH, :].rearrange("c (t1 t2) -> t1 c t2", t2=128), reads=[d_sf], writes=[dX])
                    for j0 in range(0, CH, 8):
                        pb = (j0 // 8) % 2
                        psA, dpsA = HC["psA"][pb], HC["dpsA"][pb]
                        for j in range(8):
                            kb.op("pe", lambda e, j=j: e.matmul(psA[:, j * 64:(j + 1) * 64], lhsT=X[:, j0 + j, :], rhs=HC["d1"][:],
                                                                start=True, stop=True),
                                  reads=[dX, HC["dconst"]], writes=[dpsA], inc=(j == 7))
                        if pb == 0:
                            kb.op("act", lambda e: e.copy(out=A[:, j0:j0 + 8, :], in_=psA[:].rearrange("p (j f) -> p j f", f=64)),
                                  reads=[dpsA], writes=[dA])
                        else:
                            kb.op("dve", lambda e: e.tensor_copy(out=A[:, j0:j0 + 8, :], in_=psA[:].rearrange("p (j f) -> p j f", f=64)),
                                  reads=[dpsA], writes=[dA])
                    for q in range(4):
                        pb = q % 2
                        psU, dpsU = HC["psU"][pb], HC["dpsU"][pb]
                        U8 = psU[:, 0:8 * CH].rearrange("p (a c) -> p a c", a=8)
                        for a in range(8):
                            f1 = 8 * q + a
                            Ar = A[:, :, f1]
                            Ai = A[:, :, 32 + f1]
                            m0, m1 = (0, 2) if kind == 0 else (1, 0)
                            kb.op("pe", lambda e: e.matmul(U8[:, a, :], lhsT=W[:, f1, m0, :], rhs=Ar, start=True, stop=False),
                                  reads=[dA, HC["dconst"]], writes=[dpsU], inc=False)
                            kb.op("pe", lambda e: e.matmul(U8[:, a, :], lhsT=W[:, f1, m1, :], rhs=Ai, start=False, stop=True),
                                  reads=[dA], writes=[dpsU], inc=(a == 7))
                        dst = Kh1[:, 8 * q:8 * q + 8, kind, cc * CH:(cc + 1) * CH]
                        if q % 2 == 0:
                            kb.op("act", lambda e: e.copy(out=dst, in_=U8), reads=[dpsU], writes=[dK1])
                        else:
                            kb.op("dve", lambda e: e.tensor_copy(out=dst, in_=U8), reads=[dpsU], writes=[dK1])
                kb.drain_scope(dXs + dAs)

        Sz, Sy = I["S_z"], I["S_y"]
        d_sz = Dep()

        def shortconv(es, part, ct, hin_rows, name):
            u32 = sb(nc, es, "u_32" + name, [128, L + 2], F32)
            acc = sb(nc, es, "u_acc" + name, [128, L], F32)
            du, du32, dacc = Dep(), Dep(), Dep()
            kb.op("pool", lambda e: e.memset(u32[:, 0:1], 0.0), writes=[du32])
            kb.op("pool", lambda e: e.memset(u32[:, L + 1:L + 2], 0.0), writes=[du32])
            I["load_u32"](es, u32[:, 1:L + 1], hin_rows[0], hin_rows[1], du, du32)
            ix = part * 2 + ct
            kb.op("act", lambda e: e.activation(out=acc[:], in_=u32[:, 1:L + 1], func=AF.Identity, scale=convw[:, ix, 1:2],
                                                bias=convb[:, ix:ix + 1]),
                  reads=[du32, HC["dconst"]], writes=[dacc])
            kb.op("dve", lambda e: e.scalar_tensor_tensor(out=acc[:], in0=u32[:, 0:L], scalar=convw[:, ix, 0:1],
                                                          in1=acc[:], op0=ALU.mult, op1=ALU.add),
                  reads=[du32, dacc], writes=[dacc])
            kb.op("dve", lambda e: e.scalar_tensor_tensor(out=acc[:], in0=u32[:, 2:L + 2], scalar=convw[:, ix, 2:3],
                                                          in1=acc[:], op0=ALU.mult, op1=ALU.add),
                  reads=[du32, dacc], writes=[dacc])
            return acc, dacc, [du, du32, dacc]

        for ct in range(2):
            with ExitStack() as es:
                acc, dacc, dl = shortconv(es, 2, ct, (2, ct), "v")
                kb.dma("sp", Sz[ct * 128:(ct + 1) * 128, :], acc[:], reads=[dacc], wacc=[d_sz])
                kb.drain_scope(dl + [d_sz])

        def run_conv(o):
            with ExitStack() as es:
                X0 = sb(nc, es, "cx_%d" % o, [32, CH, 128], BF16)
                dX0 = Dep()
                HC["X"] = [X0, X0]
                HC["dX"] = [dX0, dX0]
                HC["A"] = [sb(nc, es, "ca%d_%d" % (i, o), [128, CH, 64], BF16) for i in range(2)]
                HC["dA"] = [Dep(), Dep()]
                Y = [sb(nc, es, "cy%d_%d" % (i, o), [128, 64, CH], BF16) for i in range(2)]
                dY = [Dep(), Dep()]
                HC["Z"] = sb(nc, es, "cz_%d" % o, [64, CH, 128], BF16)
                Y2 = [sb(nc, es, "cy2%d_%d" % (i, o), [128, 64, CH], BF16) for i in range(2)]
                dY2 = [Dep(), Dep()]
                HC["dZ"] = Dep()
                HC["ysb"] = sb(nc, es, "cys_%d" % o, [32, CH, 128], F32)
                HC["dys"] = Dep()
                m_all = [sb(nc, es, "cm%d_%d" % (i, o), [128, 4, CH], F32) for i in range(8)]
                dm_all = [Dep() for _ in range(8)]

                def dsrc(ci):
                    return [(Sz[ci * CH:(ci + 1) * CH, :], 0, CH)]

                def dcons(ci, q, U4, dpsU):
                    yb = ci % 2
                    m = m_all[(q % 2) * 4:(q % 2) * 4 + 4]
                    dm = dm_all[(q % 2) * 4:(q % 2) * 4 + 4]
                    Kr = Kh[o][:, 4 * q:4 * q + 4, 0, ci * CH:(ci + 1) * CH]
                    Ki = Kh[o][:, 4 * q:4 * q + 4, 1, ci * CH:(ci + 1) * CH]
                    Ur, Ui = U4[:, :, 0, :], U4[:, :, 1, :]
                    for i, (a_, b_) in enumerate(((Ur, Kr), (Ui, Ki), (Ur, Ki), (Ui, Kr))):
                        kb.op("dve", lambda e, i=i, a_=a_, b_=b_: e.tensor_tensor(out=m[i][:], in0=a_, in1=b_, op=ALU.mult),
                              reads=[dpsU, dK[o]], writes=[dm[i]])
                    yr = Y[yb][:, 4 * q:4 * q + 4, :]
                    yi = Y[yb][:, 32 + 4 * q:32 + 4 * q + 4, :]
                    kb.op("pool", lambda e: e.tensor_tensor(out=yr, in0=m[0][:], in1=m[1][:], op=ALU.subtract),
                          reads=[dm[0], dm[1]], writes=[dY[yb]])
                    kb.op("pool", lambda e: e.tensor_tensor(out=yi, in0=m[2][:], in1=m[3][:], op=ALU.add),
                          reads=[dm[2], dm[3]], writes=[dY[yb]])
                    y2a = Y2[yb][:, 4 * q:4 * q + 4, :]
                    y2b = Y2[yb][:, 32 + 4 * q:32 + 4 * q + 4, :]
                    kb.op("dve", lambda e: e.scalar_tensor_tensor(out=y2a, in0=m[2][:], scalar=-1.0, in1=m[3][:],
                                                                  op0=ALU.mult, op1=ALU.subtract),
                          reads=[dm[2], dm[3]], writes=[dY2[yb]])
                    kb.op("act", lambda e: e.copy(out=y2b, in_=yr), reads=[dY[yb]], writes=[dY2[yb]])
                    if q == 7:
                        if pending_inv:
                            pending_inv.pop(0)()
                        pending_inv.append(lambda ci=ci, yb=yb: emit_fft_inv(kb, HC, ci, Y[yb], dY[yb], Sy[ci * CH:(ci + 1) * CH, :],
                                                                             Y2[yb], dY2[yb]))

                pending_inv = []
                emit_fft_fwd(kb, HC, dsrc, 256 // CH, dcons, True, d_sz)
                while pending_inv:
                    pending_inv.pop(0)()
                kb.drain_scope(HC["dX"] + HC["dA"] + dY + dY2 + [HC["dZ"], HC["dys"], HC["d_sy"]] + dm_all)

        run_filter(0)
        run_conv(0)
        for step, part in ((0, 0), (1, 1)):
            for ct in range(2):
                with ExitStack() as es:
                    yt = sb(nc, es, "g_y%d" % step, [128, L], F32)
                    zt = sb(nc, es, "g_z%d" % step, [128, L], F32)
                    dyt, dzt = Dep(), Dep()
                    kb.dma("sp", yt[:], Sy[ct * 128:(ct + 1) * 128, :], reads=[HC["d_sy"]], writes=[dyt])
                    kb.dma("sp", zt[:], Sz[ct * 128:(ct + 1) * 128, :], reads=[d_sz], writes=[dzt])
                    kb.op("dve", lambda e: e.scalar_tensor_tensor(out=yt[:], in0=zt[:], scalar=hbias[:, step * 2 + ct:step * 2 + ct + 1],
                                                                  in1=yt[:], op0=ALU.mult, op1=ALU.add),
                          reads=[dzt, dyt, HC["dconst"]], writes=[dyt])
                    acc, dacc, dl = shortconv(es, part, ct, (part, ct), "g%d" % step)
                    if step == 0:
                        kb.op("dve", lambda e: e.tensor_tensor(out=zt[:], in0=acc[:], in1=yt[:], op=ALU.mult),
                              reads=[dacc, dyt], writes=[dzt])
                        kb.dma("sp", Sz[ct * 128:(ct + 1) * 128, :], zt[:], reads=[dzt], wacc=[d_sz])
                        kb.drain_scope(dl + [dyt, dzt, d_sz])
                    else:
                        ob = sb(nc, es, "g_ob", [128, L], BF16)
                        dob = Dep()
                        kb.op("dve", lambda e: e.tensor_tensor(out=ob[:], in0=acc[:], in1=yt[:], op=ALU.mult),
                              reads=[dacc, dyt], writes=[dob])
                        kb.dma("sp", I["hout"][ct * 128:(ct + 1) * 128, :], ob[:], reads=[dob], wacc=[I["d_hout"]])
                        kb.drain_scope(dl + [dyt, dzt, dob, I["d_hout"]])
            if step == 0:
                run_filter(1)
                run_conv(1)
        kb.drain_scope([HC["dconst"]] + dK)


def emit_proj(kb, C, lhsT_of, d_w, nk, msz, rhs_of, d_rhs, ntok, consume):
    for tt in range(ntok // 512):
        pb = C["rr"] % 2
        C["rr"] += 1
        pp, dpp = C["ps_" + "ab"[pb]], C["d_ps_" + "ab"[pb]]
        for k in range(nk):
            kb.op("pe", lambda e, k=k: e.matmul(pp[0:msz, :], lhsT=lhsT_of(k), rhs=rhs_of(k, tt),
                                                start=(k == 0), stop=(k == nk - 1)),
                  reads=[d_w, d_rhs], writes=[dpp], inc=(k == nk - 1))
        consume(tt, pp, dpp)


def load_x(kb, xT, dxs, src, d_src=None, per_kq=None):
    xv = src.rearrange("(c p) t -> p c t", p=128)
    rd = [d_src] if d_src is not None else []
    for k in range(8):
        for q in range(4):
            tgt = [dxs[q]] if per_kq is None else [per_kq[k][q]]
            kb.dma("sp", xT[:, k, q * 512:(q + 1) * 512], xv[:, k, q * 512:(q + 1) * 512], reads=rd, wacc=tgt)


def store_x(kb, xT, dxs, dst, d_out):
    ov = dst.rearrange("(c p) t -> p c t", p=128)
    for k in range(8):
        kb.dma("sp", ov[:, k, :], xT[:, k, :], reads=dxs, wacc=[d_out])


def declare(nc, specs, kind):
    out = {}
    for name, (shape, dt) in specs.items():
        out[name] = nc.dram_tensor(name, list(shape), dt, kind=kind).ap()
    return out


FFN_W = lambda pre: {pre + "wg": ([11, 128, 8, 256], F32), pre + "wu": ([11, 128, 8, 256], F32),
                     pre + "wd": ([2, 4, 128, 12, 256], F32)}


P1_IN = {"xT_in": ([1024, TOK], F32), "gains": ([128, 16], F32), "w_in": ([8, 128, 8, 256], F32), **FFN_W("f00_")}
P1_OUT = {"xT_out": ([1024, TOK], F32), "projT": ([2048, TOK], BF16)}


def emit_phase1(kb, nc, C, I, O):
    d_out = Dep()
    with ExitStack() as es:
        xT = sb(nc, es, "xT", [128, 8, TOK], F32)
        gt = sb(nc, es, "gt", [128, 16], F32)
        dxs = [Dep() for _ in range(4)]
        kb.dma("sp", gt[:], I["gains"][:, :], writes=[C["d_const"]])
        load_x(kb, xT, dxs, I["xT_in"])
        emit_ffn(kb, C, xT, dxs, gt[:, 0:8], I["f00_wg"], I["f00_wu"], I["f00_wd"], "a")
        store_x(kb, xT, dxs, O["xT_out"], d_out)
        with ExitStack() as es2:
            hT = sb(nc, es2, "hT1", [128, 8, TOK], BF16)
            d_h = Dep()
            emit_rmsnorm(kb, C, xT, dxs, gt[:, 8:16], hT, d_h, 0, TOK)
            wb = [sb(nc, es2, "win%d" % i, [128, 8, 256], BF16) for i in range(2)]
            dwb = [Dep(), Dep()]
            ob = [sb(nc, es2, "pob%d" % i, [128, TOK], BF16) for i in range(2)]
            dob = [Dep(), Dep()]
            kb.dma("pool", wb[0][:], I["w_in"][0], writes=[dwb[0]])
            for g in range(8):
                if g + 1 < 8:
                    kb.dma("pool", wb[(g + 1) % 2][:], I["w_in"][g + 1], writes=[dwb[(g + 1) % 2]])
                for j in range(2):
                    oi = (g * 2 + j) % 2
                    ch = g * 2 + j

                    def cons(tt, pp, dpp, oi=oi):
                        eng = "act" if tt % 2 == 0 else "dve"
                        if eng == "act":
                            kb.op("act", lambda e: e.copy(out=ob[oi][:, tt * 512:(tt + 1) * 512], in_=pp[:]), reads=[dpp], writes=[dob[oi]])
                        else:
                            kb.op("dve", lambda e: e.tensor_copy(out=ob[oi][:, tt * 512:(tt + 1) * 512], in_=pp[:]), reads=[dpp], writes=[dob[oi]])

                    emit_proj(kb, C, lambda k, g=g, j=j: wb[g % 2][:, k, j * 128:(j + 1) * 128], dwb[g % 2], 8, 128,
                              lambda k, tt: hT[:, k, tt * 512:(tt + 1) * 512], d_h, TOK, cons)
                    O["store_proj"](ch, ob[oi], dob[oi], d_out)
            kb.drain_scope([d_h] + dwb + dob)
        kb.drain_scope(dxs + [d_out])
    return d_out


P3_IN = {"xT_in": ([1024, TOK], F32), "gains": ([128, 32], F32), "apool": ([512, TOK + 32], BF16),
         "invcnt": ([4, TOK], F32), "yhyT": ([512, TOK], BF16), "pool_w": ([128, 4, 128], F32),
         "pool_scale": ([128, 4], F32), "w_mo": ([4, 128, 8, 256], F32),
         **FFN_W("f01_"), **FFN_W("f10_"),
         "w_dq": ([128, 8, 256], F32), "w_dkv": ([128, 8, 160], F32), "qkv_g": ([128, 3], F32),
         "rope_cs": ([32, 2, TOK], F32)}
P3_OUT = {"xT_out": ([1024, TOK], F32), "cqnT": ([256, TOK], BF16), "kvlatT": ([160, TOK], BF16)}


def emit_pool(kb, nc, C, I, yT, d_y):
    with ExitStack() as es:
        W = TOK + 32
        pw32 = sb(nc, es, "pw32", [128, 4, 128], F32)
        pw = sb(nc, es, "pw", [128, 4, 128], BF16)
        psc = sb(nc, es, "psc", [128, 4], F32)
        dpw = Dep()
        kb.dma("sp", pw32[:], I["pool_w"][:, :, :], writes=[dpw])
        kb.dma("sp", psc[:], I["pool_scale"][:, :], writes=[dpw])
        kb.op("act", lambda e: e.copy(out=pw[:], in_=pw32[:]), reads=[dpw], writes=[dpw])
        ab = sb(nc, es, "pl_ab", [128, W], BF16)
        A = sb(nc, es, "pl_A", [128, W], F32)
        S1 = sb(nc, es, "pl_S1", [128, W], F32)
        S2 = sb(nc, es, "pl_S2", [128, W], F32)
        inv = sb(nc, es, "pl_inv", [128, TOK], F32)
        pb = sb(nc, es, "pl_p", [128, TOK], BF16)
        dab, dA, dS1, dS2, dinv, dp = Dep(), Dep(), Dep(), Dep(), Dep(), Dep()
        for g in range(4):
            I["load_apool"](ab, g, dab)
            kb.dma("sp", inv[:], bcast_rows(I["invcnt"][g:g + 1, :], 128), writes=[dinv])
            kb.op("act", lambda e: e.copy(out=A[:], in_=ab[:]), reads=[dab], writes=[dA])
            if "halo_mask" in I:
                hm = I["halo_mask"]
                kb.op("dve", lambda e: e.tensor_scalar(out=A[:, 0:16], in0=A[:, 0:16], scalar1=hm[:, 0:1], scalar2=None, op0=ALU.mult),
                      reads=[dA, C["d_const"]], writes=[dA])
                kb.op("dve", lambda e: e.tensor_scalar(out=A[:, W - 16:W], in0=A[:, W - 16:W], scalar1=hm[:, 1:2], scalar2=None, op0=ALU.mult),
                      reads=[dA, C["d_const"]], writes=[dA])
            kb.op("dve", lambda e: e.tensor_tensor(out=S1[:, 1:W], in0=A[:, 0:W - 1], in1=A[:, 1:W], op=ALU.add),
                  reads=[dA], writes=[dS1])
            cur, dcur, oth, doth = S1, dS1, S2, dS2
            lo, hi = 1, W
            for lvl in range(1, g + 1):
                sh = 1 << (lvl - 1)
                nlo, nhi = lo + sh, hi - sh
                kb.op("dve", lambda e, cur=cur, oth=oth, sh=sh, nlo=nlo, nhi=nhi: e.tensor_tensor(
                    out=oth[:, nlo:nhi], in0=cur[:, nlo - sh:nhi - sh], in1=cur[:, nlo + sh:nhi + sh], op=ALU.add),
                    reads=[dcur], writes=[doth])
                cur, dcur, oth, doth = oth, doth, cur, dcur
                lo, hi = nlo, nhi
            assert lo <= 16 and hi >= 16 + TOK
            kb.op("dve", lambda e, cur=cur: e.tensor_tensor(out=cur[:, 16:16 + TOK], in0=cur[:, 16:16 + TOK], in1=inv[:], op=ALU.mult),
                  reads=[dcur, dinv], writes=[dcur])
            kb.op("dve", lambda e, cur=cur: e.tensor_tensor(out=pb[:], in0=cur[:, 16:16 + TOK], in1=A[:, 16:16 + TOK], op=ALU.subtract),
                  reads=[dcur, dA], writes=[dp])

            def cons(tt, pp, dpp, g=g):
                kb.op("act", lambda e: e.activation(out=yT[:, g, tt * 512:(tt + 1) * 512], in_=pp[:], func=AF.Copy,
                                                    scale=psc[:, g:g + 1]), reads=[dpp, dpw], writes=[d_y])

            emit_proj(kb, C, lambda k, g=g: pw[:, g, :], dpw, 1, 128, lambda k, tt: pb[:, tt * 512:(tt + 1) * 512], dp, TOK, cons)
        kb.drain_scope([dpw, dab, dA, dS1, dS2, dinv, dp])


def emit_phase3(kb, nc, C, I, O):
    d_out = Dep()
    with ExitStack() as es:
        xT = sb(nc, es, "xT3", [128, 8, TOK], F32)
        gt = sb(nc, es, "gt3", [128, 32], F32)
        dxs = [Dep() for _ in range(4)]
        kb.dma("sp", gt[:], I["gains"][:, :], writes=[C["d_const"]])
        load_x(kb, xT, dxs, I["xT_in"], I.get("d_x"))
        with ExitStack() as es2:
            yT = sb(nc, es2, "yT3", [128, 8, TOK], BF16)
            d_y = Dep()
            for c in range(4):
                I["load_yhy"](yT[:, 4 + c, :], c, d_y)
            emit_pool(kb, nc, C, I, yT, d_y)
            wb = [sb(nc, es2, "wmo%d" % i, [128, 8, 256], BF16) for i in range(2)]
            dwb = [Dep(), Dep()]
            kb.dma("pool", wb[0][:], I["w_mo"][0], writes=[dwb[0]])
            for g in range(4):
                if g + 1 < 4:
                    kb.dma("pool", wb[(g + 1) % 2][:], I["w_mo"][g + 1], writes=[dwb[(g + 1) % 2]])
                for j in range(2):
                    dch = g * 2 + j

                    def cons(tt, pp, dpp, dch=dch):
                        kb.op("dve", lambda e: e.tensor_tensor(out=xT[:, dch, tt * 512:(tt + 1) * 512], in0=pp[:],
                                                               in1=xT[:, dch, tt * 512:(tt + 1) * 512], op=ALU.add),
                              reads=[dpp, dxs[tt]], writes=[dxs[tt]])

                    emit_proj(kb, C, lambda k, g=g, j=j: wb[g % 2][:, k, j * 128:(j + 1) * 128], dwb[g % 2], 8, 128,
                              lambda k, tt: yT[:, k, tt * 512:(tt + 1) * 512], d_y, TOK, cons)
            kb.drain_scope([d_y] + dwb)
        if "after_mixer" in I:
            I["after_mixer"]()
        emit_ffn(kb, C, xT, dxs, gt[:, 0:8], I["f01_wg"], I["f01_wu"], I["f01_wd"], "b")
        emit_ffn(kb, C, xT, dxs, gt[:, 8:16], I["f10_wg"], I["f10_wu"], I["f10_wd"], "c")
        store_x(kb, xT, dxs, O["xT_out"], d_out)
        with ExitStack() as es2:
            hT = sb(nc, es2, "hT3", [128, 8, TOK], BF16)
            d_h = Dep()
            emit_rmsnorm(kb, C, xT, dxs, gt[:, 16:24], hT, d_h, 0, TOK)
            wdq = sb(nc, es2, "wdq", [128, 8, 256], BF16)
            wdkv = sb(nc, es2, "wdkv", [128, 8, 160], BF16)
            wrot = sb(nc, es2, "wrot", [128, 8, 32], BF16)
            qg = sb(nc, es2, "qg", [128, 3], F32)
            cs = sb(nc, es2, "ropecs", [32, 2, TOK], F32)
            dw = Dep()
            kb.dma("pool", wdq[:], I["w_dq"][:, :, :], wacc=[dw])
            kb.dma("pool", wdkv[:], I["w_dkv"][:, :, :], wacc=[dw])
            kb.dma("sp", qg[:], I["qkv_g"][:, :], wacc=[dw])
            kb.dma("sp", cs[:], I["rope_cs"][:, :, :], wacc=[dw])
            kb.op("act", lambda e: e.mul(out=wrot[:, :, 0:16], in_=wdkv[:, :, 144:160], mul=-1.0), reads=[dw], writes=[dw])
            kb.op("act", lambda e: e.copy(out=wrot[:, :, 16:32], in_=wdkv[:, :, 128:144]), reads=[dw], writes=[dw])
            cqf = sb(nc, es2, "cqf", [128, 2, TOK], F32)
            dcq = [Dep() for _ in range(4)]
            for c in range(2):
                def cons(tt, pp, dpp, c=c):
                    kb.op("act", lambda e: e.copy(out=cqf[:, c, tt * 512:(tt + 1) * 512], in_=pp[:]), reads=[dpp], writes=[dcq[tt]])
                emit_proj(kb, C, lambda k, c=c: wdq[:, k, c * 128:(c + 1) * 128], dw, 8, 128,
                          lambda k, tt: hT[:, k, tt * 512:(tt + 1) * 512], d_h, TOK, cons)
            cqn = sb(nc, es2, "cqn", [128, 2, TOK], BF16)
            dcqn = Dep()
            emit_rmsnorm(kb, C, cqf, dcq, qg[:, 0:2], cqn, dcqn, 0, TOK, nch=2, ones=C["ones256"])
            for c in range(2):
                kb.dma("sp", O["cqnT"][c * 128:(c + 1) * 128, :], cqn[:, c, :], reads=[dcqn], wacc=[d_out])
            ckf = sb(nc, es2, "ckf", [128, 1, TOK], F32)
            dck = [Dep() for _ in range(4)]

            def cons_k(tt, pp, dpp):
                kb.op("act", lambda e: e.copy(out=ckf[:, 0, tt * 512:(tt + 1) * 512], in_=pp[:]), reads=[dpp], writes=[dck[tt]])
            emit_proj(kb, C, lambda k: wdkv[:, k, 0:128], dw, 8, 128, lambda k, tt: hT[:, k, tt * 512:(tt + 1) * 512], d_h, TOK, cons_k)
            ckn = sb(nc, es2, "ckn", [128, 1, TOK], BF16)
            dckn = Dep()
            emit_rmsnorm(kb, C, ckf, dck, qg[:, 2:3], ckn, dckn, 0, TOK, nch=1, ones=C["ones128"])
            kb.dma("sp", O["kvlatT"][0:128, :], ckn[:, 0, :], reads=[dckn], wacc=[d_out])
            kr = sb(nc, es2, "krope", [32, TOK], BF16)
            ta = sb(nc, es2, "kr_ta", [32, TOK], F32)
            tb = sb(nc, es2, "kr_tb", [32, 512], F32)
            dta, dtb, dkr = Dep(), Dep(), Dep()

            def cons_a(tt, pp, dpp):
                kb.op("dve", lambda e: e.tensor_tensor(out=ta[:, tt * 512:(tt + 1) * 512], in0=pp[0:32, :],
                                                       in1=cs[:, 0, tt * 512:(tt + 1) * 512], op=ALU.mult),
                      reads=[dpp, dw], writes=[dta])
            emit_proj(kb, C, lambda k: wdkv[:, k, 128:160], dw, 8, 32, lambda k, tt: hT[:, k, tt * 512:(tt + 1) * 512], d_h, TOK, cons_a)

            def cons_b(tt, pp, dpp):
                kb.op("dve", lambda e: e.tensor_tensor(out=tb[:], in0=pp[0:32, :], in1=cs[:, 1, tt * 512:(tt + 1) * 512], op=ALU.mult),
                      reads=[dpp, dw], writes=[dtb])
                kb.op("dve", lambda e: e.tensor_tensor(out=kr[:, tt * 512:(tt + 1) * 512], in0=ta[:, tt * 512:(tt + 1) * 512],
                                                       in1=tb[:], op=ALU.add), reads=[dta, dtb], writes=[dkr])
            emit_proj(kb, C, lambda k: wrot[:, k, :], dw, 8, 32, lambda k, tt: hT[:, k, tt * 512:(tt + 1) * 512], d_h, TOK, cons_b)
            kb.dma("sp", O["kvlatT"][128:160, :], kr[:], reads=[dkr], wacc=[d_out])
            kb.drain_scope([d_h, dw, dcqn, dckn, dta, dtb, dkr] + dcq + dck)
        kb.drain_scope(dxs + [d_out])
    return d_out


NH = 16
P4_IN = {"xT_in": ([1024, TOK], F32), "gains": ([128, 16], F32), "cqnT": ([256, TOK], BF16),
         "kvlat_full": ([2, 160, TOK], BF16), "w_uq": ([128, 2, 1536], F32), "w_ukv": ([128, 2048], F32),
         "w_o": ([128, 8, 1024], F32), "rope_q": ([96, 2, TOK], F32), **FFN_W("f11_")}
P4_OUT = {"outT": ([1024, TOK], F32)}


def emit_phase4(kb, nc, C, I, O):
    d_out = Dep()
    SC = 96 ** -0.5
    with ExitStack() as es, ExitStack() as esO:
        OT = sb(nc, esO, "OT", [128, NH // 2, TOK], BF16, side="right")
        dOT = Dep()
        with ExitStack() as es2:
            dw = Dep()
            wuq = sb(nc, es2, "wuq", [128, 2, 1536], BF16)
            wukv = sb(nc, es2, "wukv", [128, 2048], BF16)
            kb.dma("pool", wuq[:], I["w_uq"][:, :, :], wacc=[dw])
            kb.dma("pool", wukv[:], I["w_ukv"][:, :], wacc=[dw])
            wqr = sb(nc, es2, "wqr", [128, 2, NH, 96], BF16)
            kb.op("pool", lambda e: e.memset(wqr[:], 0.0), writes=[dw])
            wv = wuq[:].rearrange("p k (h c) -> p k h c", c=96)
            kb.op("act", lambda e: e.mul(out=wqr[:, :, :, 64:80], in_=wv[:, :, :, 80:96], mul=-1.0), reads=[dw], writes=[dw])
            kb.op("act", lambda e: e.copy(out=wqr[:, :, :, 80:96], in_=wv[:, :, :, 64:80]), reads=[dw], writes=[dw])
            cqn = sb(nc, es2, "cqn4", [128, 2, TOK], BF16)
            ckn = sb(nc, es2, "ckn4", [128, L], BF16)
            krp = sb(nc, es2, "krp4", [96, L], BF16)
            rq = sb(nc, es2, "ropeq", [96, 2, TOK], F32)
            dl = Dep()
            for c in range(2):
                kb.dma("sp", cqn[:, c, :], I["cqnT"][c * 128:(c + 1) * 128, :], wacc=[dl])
            for rk in range(2):
                kb.dma("sp", ckn[:, rk * TOK:(rk + 1) * TOK], I["kvlat_full"][rk, 0:128, :], reads=[I["d_kv"]], wacc=[dl])
                kb.dma("sp", krp[64:96, rk * TOK:(rk + 1) * TOK], I["kvlat_full"][rk, 128:160, :], reads=[I["d_kv"]], wacc=[dl])
            kb.dma("sp", rq[64:96, :, :], I["rope_q"][64:96, :, :], wacc=[dl])
            onesf = sb(nc, es2, "onesf", [96, 64], F32)
            kb.op("pool", lambda e: e.memset(onesf[:], 1.0), writes=[dl])
            Vt = sb(nc, es2, "Vt", [128, 32, 8, 65], BF16)
            KT = [sb(nc, es2, "KT%d" % i, [96, L], BF16) for i in range(2)]
            dKT = [Dep(), Dep()]
            QT = [sb(nc, es2, "QT%d" % i, [96, TOK], BF16) for i in range(2)]
            dQT = [Dep(), Dep()]
            PT = [sb(nc, es2, "PT%d" % i, [128, 1024], BF16) for i in range(3)]
            dPT = [Dep(), Dep(), Dep()]
            qa = sb(nc, es2, "qa", [96, 512], F32)
            qb = sb(nc, es2, "qb", [96, 512], F32)
            dqa, dqb = Dep(), Dep()
            stg = [sb(nc, es2, "ostg%d" % i, [64, 512], BF16) for i in range(2)]
            dstg = [Dep(), Dep()]
            stg_rr = [0]
            den = sb(nc, es2, "den", [96, 512], F32)
            rec = sb(nc, es2, "rec", [64, 512], F32)
            dden, drec = Dep(), Dep()
            psS = [(C["ps_a"], C["ps_b"]), (C["ps_c"], C["ps_d"])]
            psS2 = [C["ps_ab"], C["ps_cd"]]
            dpsS = [Dep(), Dep()]
            psO = [C["ps_e"], C["ps_f"]]
            dpsO = [C["d_ps_e"], C["d_ps_f"]]
            psB, dpsB = C["ps_g"], C["d_ps_g"]
            psP, dpsP = C["ps_h"], C["d_ps_h"]
            sit = 0
            oit = 0
            pit = 0
            pending_norm = []

            dVs = [Dep() for _ in range(8)]

            bank_rr = [0]
            step_banks = [(C["ps_h"], C["d_ps_h"]), (C["ps_g"], C["d_ps_g"])]

            def next_bank():
                bank_rr[0] += 1
                return step_banks[bank_rr[0] % 2]

            def v_steps(h):
                hl_ = h % 8
                for k0 in range(0, 32, 8):
                    psP, dpsP = next_bank()
                    for j in range(8):
                        kt = k0 + j
                        kb.op("pe", lambda e, j=j, kt=kt: e.matmul(psP[:, j * 64:(j + 1) * 64], lhsT=ckn[:, kt * 128:(kt + 1) * 128],
                                                                   rhs=wukv[:, h * 128 + 64:h * 128 + 128], start=True, stop=True),
                              reads=[dl, dw], writes=[dpsP], inc=(j == 7))
                    kb.op("dve", lambda e: e.tensor_copy(out=Vt[:, k0:k0 + 8, hl_, 0:64], in_=psP[:].rearrange("p (j c) -> p j c", c=64)),
                          reads=[dpsP], writes=[dVs[hl_]])
                    yield

            def setup_steps(h):
                kbuf = h % 2
                K_, dK_ = KT[kbuf], dKT[kbuf]
                Q_, dQ_ = QT[kbuf], dQT[kbuf]
                kb.op("dve", lambda e: e.tensor_copy(out=K_[64:96, :], in_=krp[64:96, :]), reads=[dl], writes=[dK_])
                for kc in range(8):
                    psP, dpsP = next_bank()
                    kb.op("pe", lambda e: e.matmul(psP[0:64, :], lhsT=wukv[:, h * 128:h * 128 + 64],
                                                   rhs=ckn[:, kc * 512:(kc + 1) * 512], start=True, stop=True),
                          reads=[dl, dw], writes=[dpsP])
                    kb.op("dve", lambda e: e.tensor_copy(out=K_[0:64, kc * 512:(kc + 1) * 512], in_=psP[0:64, :]), reads=[dpsP], writes=[dK_])
                    yield
                for qt in range(4):
                    psP, dpsP = next_bank()
                    for kk in range(2):
                        kb.op("pe", lambda e: e.matmul(psP[0:96, :], lhsT=wuq[:, kk, h * 96:(h + 1) * 96],
                                                       rhs=cqn[:, kk, qt * 512:(qt + 1) * 512], start=(kk == 0), stop=(kk == 1)),
                              reads=[dl, dw], writes=[dpsP], inc=(kk == 1))
                    kb.op("dve", lambda e: e.tensor_copy(out=Q_[0:64, qt * 512:(qt + 1) * 512], in_=psP[0:64, :]), reads=[dpsP], writes=[dQ_])
                    kb.op("dve", lambda e: e.tensor_tensor(out=qa[64:96, :], in0=psP[64:96, :], in1=rq[64:96, 0, qt * 512:(qt + 1) * 512],
                                                           op=ALU.mult), reads=[dpsP, dl], writes=[dqa])
                    yield
                    psP, dpsP = next_bank()
                    for kk in range(2):
                        kb.op("pe", lambda e: e.matmul(psP[0:96, :], lhsT=wqr[:, kk, h, :],
                                                       rhs=cqn[:, kk, qt * 512:(qt + 1) * 512], start=(kk == 0), stop=(kk == 1)),
                              reads=[dl, dw], writes=[dpsP], inc=(kk == 1))
                    kb.op("dve", lambda e: e.tensor_tensor(out=qb[64:96, :], in0=psP[64:96, :], in1=rq[64:96, 1, qt * 512:(qt + 1) * 512],
                                                           op=ALU.mult), reads=[dpsP, dl], writes=[dqb])
                    kb.op("dve", lambda e: e.tensor_tensor(out=Q_[64:96, qt * 512:(qt + 1) * 512], in0=qa[64:96, :], in1=qb[64:96, :],
                                                           op=ALU.add), reads=[dqa, dqb], writes=[dQ_])
                    yield

            steps = []

            def run_step():
                while steps:
                    try:
                        next(steps[0])
                        return
                    except StopIteration:
                        steps.pop(0)

            def flush_steps():
                while steps:
                    run_step()

            def emit_norm(h, qt, ob):
                kb.op("dve", lambda e: e.tensor_copy(out=den[64:65, :], in_=psO[ob][64:65, :]), reads=[dpsO[ob]], writes=[dden])
                kb.op("pe", lambda e: e.matmul(psB[0:64, :], lhsT=onesf[64:65, :], rhs=den[64:65, :], start=True, stop=True),
                      reads=[dden, dl], writes=[dpsB])
                kb.op("dve", lambda e: e.reciprocal(out=rec[:], in_=psB[0:64, :]), reads=[dpsB], writes=[drec])
                if h % 2 == 0:
                    kb.op("dve", lambda e: e.tensor_tensor(out=OT[0:64, h // 2, qt * 512:(qt + 1) * 512], in0=psO[ob][0:64, :], in1=rec[:],
                                                           op=ALU.mult), reads=[dpsO[ob], drec], wacc=[dOT])
                else:
                    sg_i = stg_rr[0] % 2
                    stg_rr[0] += 1
                    kb.op("dve", lambda e: e.tensor_tensor(out=stg[sg_i][:], in0=psO[ob][0:64, :], in1=rec[:], op=ALU.mult),
                          reads=[dpsO[ob], drec], writes=[dstg[sg_i]])
                    kb.dma("sp", OT[64:128, h // 2, qt * 512:(qt + 1) * 512], stg[sg_i][:], reads=[dstg[sg_i]], wacc=[dOT])

            kb.op("pool", lambda e: e.memset(Vt[:, :, :, 64:65], 1.0), writes=dVs)
            steps.append(v_steps(0))
            steps.append(setup_steps(0))
            flush_steps()
            for h in range(NH):
                hg, hl = h // 8, h % 8
                kbuf = h % 2
                if h == 0:
                    steps.append(v_steps(1))
                    steps.append(setup_steps(1))
                    for h0 in range(2, 8):
                        steps.append(v_steps(h0))
                else:
                    if h + 1 < NH:
                        steps.append(setup_steps(h + 1))
                    if h + 7 < NH:
                        steps.append(v_steps(h + 7))
                K_, dK_ = KT[kbuf], dKT[kbuf]
                Q_, dQ_ = QT[kbuf], dQT[kbuf]
                for qt in range(4):
                    ob = oit % 2
                    oit += 1

                    def emit_S(kp):
                        nonlocal sit
                        sbuf_i = sit % 2
                        sit += 1
                        pA, pB = psS[sbuf_i]
                        for i, pp in enumerate((pA, pB)):
                            kt = kp * 2 + i
                            kb.op("pe", lambda e, pp=pp, kt=kt: e.matmul(pp[:], lhsT=K_[:, kt * 128:(kt + 1) * 128],
                                                                         rhs=Q_[:, qt * 512:(qt + 1) * 512], start=True, stop=True),
                                  reads=[dK_, dQ_], writes=[dpsS[sbuf_i]], inc=(i == 1))
                        return sbuf_i

                    def emit_exp(sbuf_i):
                        nonlocal pit
                        pA, pB = psS[sbuf_i]
                        pt = pit % 3
                        pit += 1
                        kb.op("act", lambda e: e.activation(out=PT[pt][:, :], in_=psS2[sbuf_i][:, :], func=AF.Exp, scale=SC),
                              reads=[dpsS[sbuf_i]], writes=[dPT[pt]])
                        return pt

                    def emit_PV(kp, pt):
                        for i in range(2):
                            kt = kp * 2 + i
                            kb.op("pe", lambda e, i=i, kt=kt: e.matmul(psO[ob][0:65, :], lhsT=Vt[:, kt, hl, :],
                                                                       rhs=PT[pt][:, i * 512:(i + 1) * 512],
                                                                       start=(kt == 0), stop=(kt == 31)),
                                  reads=[dVs[hl], dPT[pt]], writes=[dpsO[ob]], inc=(kt == 31 or i == 1))

                    sb_ = {0: emit_S(0), 1: emit_S(1)}
                    for kp in range(16):
                        pt = emit_exp(sb_[kp])
                        if kp + 2 < 16:
                            sb_[kp + 2] = emit_S(kp + 2)
                        emit_PV(kp, pt)
                        if kp == 2 and pending_norm:
                            emit_norm(*pending_norm.pop())
                        elif kp >= 3:
                            run_step()
                    pending_norm.append((h, qt, ob))
                flush_steps()
            emit_norm(*pending_norm.pop())
            kb.drain_scope([dw, dl, dqa, dqb, dden, drec] + dstg + dVs + dKT + dQT + dPT + dpsS)
        xT = sb(nc, es, "xT4", [128, 8, TOK], F32)
        gt = sb(nc, es, "gt4", [128, 16], F32)
        dxs = [Dep() for _ in range(4)]
        kb.dma("sp", gt[:], I["gains"][:, :], writes=[C["d_const"]])
        dxk = [[Dep() for _ in range(4)] for _ in range(8)]
        load_x(kb, xT, dxs, I["xT_in"], I.get("d_x"), per_kq=dxk)
        with ExitStack() as es2:
            wo = sb(nc, es2, "wo4", [128, NH // 2, 1024], BF16)
            dwo = Dep()
            for hq in range(4):
                kb.dma("pool", wo[:, hq * 2:(hq + 1) * 2, :], I["w_o"][:, hq * 2:(hq + 1) * 2, :], wacc=[dwo])
            for dch in range(8):
                def cons(tt, pp, dpp, dch=dch):
                    kb.op("dve", lambda e: e.tensor_tensor(out=xT[:, dch, tt * 512:(tt + 1) * 512], in0=pp[:],
                                                           in1=xT[:, dch, tt * 512:(tt + 1) * 512], op=ALU.add),
                          reads=[dpp, dxk[dch][tt]], writes=[dxs[tt]])
                emit_proj(kb, C, lambda k, dch=dch: wo[:, k, dch * 128:(dch + 1) * 128], dwo, NH // 2, 128,
                          lambda k, tt: OT[:, k, tt * 512:(tt + 1) * 512], dOT, TOK, cons)
            kb.drain_scope([dwo, dOT])
        esO.close()
        emit_ffn(kb, C, xT, dxs, gt[:, 0:8], I["f11_wg"], I["f11_wu"], I["f11_wd"], "d")
        with ExitStack() as es2:
            oT = sb(nc, es2, "oT4", [128, 8, TOK], F32)
            d_os = [Dep() for _ in range(4)]
            ov = O["outT"].rearrange("(c p) t -> p c t", p=128)
            for tt in range(4):
                emit_rmsnorm(kb, C, xT, dxs, gt[:, 8:16], oT, d_os[tt], tt * 512, 512, h0=tt * 512)
                for k in range(8):
                    kb.dma("sp", ov[:, k, tt * 512:(tt + 1) * 512], oT[:, k, tt * 512:(tt + 1) * 512], reads=[d_os[tt]], wacc=[d_out])
            kb.drain_scope(d_os + [d_out])
        kb.drain_scope(dxs)
    return d_out


BFNP = ml_dtypes.bfloat16


def tile_kxm(w, gw=256):
    K, M = w.shape
    return np.ascontiguousarray(w.reshape(K // 128, 128, M // gw, gw).transpose(2, 1, 0, 3))


def pk(v):
    return np.ascontiguousarray(v.reshape(-1, 128).T)


def tile_wd(w):
    t = w.reshape(22, 128, 4, 256).transpose(2, 1, 0, 3)
    out = np.zeros((2, 4, 128, 12, 256), w.dtype)
    out[0] = t[:, :, 0:12, :]
    out[1, :, :, 0:10, :] = t[:, :, 12:22, :]
    return out


def ffn_w(inp, l, j, pre):
    return {pre + "wg": tile_kxm(inp["ffn_w_gate"][l, j]), pre + "wu": tile_kxm(inp["ffn_w_up"][l, j]),
            pre + "wd": tile_wd(inp["ffn_w_down"][l, j])}


def rope_tables(r):
    inv_freq = (10000.0 ** (-np.arange(0, 32, 2, dtype=np.float32) / 32)).astype(np.float32)
    pos = np.arange(r * TOK, (r + 1) * TOK, dtype=np.float32)
    ang = (pos[:, None] * inv_freq[None, :]).astype(np.float32)
    c = np.cos(ang).astype(np.float32).T
    s = np.sin(ang).astype(np.float32).T
    cs = np.zeros((32, 2, TOK), np.float32)
    cs[0:16, 0] = c
    cs[16:32, 0] = c
    cs[0:16, 1] = s
    cs[16:32, 1] = s
    return cs


def invcnt_table(r):
    t = np.arange(r * TOK, (r + 1) * TOK)
    out = np.zeros((4, TOK), np.float32)
    for g, w in enumerate((2, 4, 8, 16)):
        lo = np.clip(t - w // 2, 0, L)
        hi = np.clip(t - w // 2 + w, 0, L)
        out[g] = (1.0 / (hi - lo).astype(np.float32)).astype(np.float32)
    return out


def hyena_core_inputs(P, r):
    W = 512
    d = {}
    cw = P["hyena_conv_w"][0]
    cb = P["hyena_conv_b"][0]
    convw = np.zeros((128, 6, 3), np.float32)
    convb = np.zeros((128, 6), np.float32)
    for part in range(3):
        for ct in range(2):
            cols = part * W + 256 * r + ct * 128 + np.arange(128)
            convw[:, part * 2 + ct, :] = cw[:, cols].T
            convb[:, part * 2 + ct] = cb[cols]
    d["conv_w"] = convw
    d["conv_b"] = convb
    hb = np.zeros((128, 4), np.float32)
    for o in range(2):
        for ct in range(2):
            hb[:, o * 2 + ct] = P["hyena_bias"][0][o, 256 * r + ct * 128 + np.arange(128)]
    d["hy_bias"] = hb
    d["w1"] = np.ascontiguousarray(P["hyena_ffn_w1"][0])
    d["w2"] = np.ascontiguousarray(P["hyena_ffn_w2"][0])
    d["w3"] = np.ascontiguousarray(P["hyena_ffn_w3"][0])
    d["b123"] = np.ascontiguousarray(np.stack([P["hyena_ffn_b1"][0], P["hyena_ffn_b2"][0], P["hyena_ffn_b3"][0]], 1))
    d["sin_freq"] = np.ascontiguousarray(P["hyena_sin_freq"][0].T)
    wo = P["hyena_ffn_w_out"][0].reshape(64, 2, 2, W)
    dec = P["hyena_decay"][0]
    woc = np.zeros((64, 1024), np.float32)
    decc = np.zeros((128, 8), np.float32)
    for o in range(2):
        for dr in range(2):
            for ct in range(2):
                tix = (o * 2 + dr) * 2 + ct
                cols = 256 * r + ct * 128 + np.arange(128)
                woc[:, tix * 128:(tix + 1) * 128] = wo[:, o, dr, cols]
                decc[:, tix] = dec[o, dr, cols]
    d["w_out"] = woc
    d["decay"] = decc
    return d


P2_IN = {"conv_w": ([128, 6, 3], F32), "conv_b": ([128, 6], F32), "hy_bias": ([128, 4], F32), "w1": ([33, 64], F32),
         "w2": ([64, 64], F32), "w3": ([64, 64], F32), "b123": ([64, 3], F32), "sin_freq": ([64, 3], F32),
         "w_out": ([64, 1024], F32), "decay": ([128, 8], F32), "c_zT": ([33, L], F32), "c_trow": ([1, L], F32),
         "c_d1": ([32, 64], BF16), "c_f2": ([128, 32, 3, 128], BF16), "c_i1": ([128, 2, 256], BF16),
         "c_i2": ([64, 128, 32], BF16), "hin": ([768, L], BF16)}
P2_OUT = {"hout": ([256, L], BF16)}


def build_program(phase):
    nc = bass.Bass("TRN2", target_bir_lowering=False)
    ins, outs, emit = {1: (P1_IN, P1_OUT, emit_phase1), 3: (P3_IN, P3_OUT, emit_phase3),
                       4: (P4_IN, P4_OUT, emit_phase4), 2: (P2_IN, P2_OUT, None)}[phase]
    I = declare(nc, ins, "ExternalInput")
    O = declare(nc, outs, "ExternalOutput")
    with ExitStack() as es:
        kb = KB(nc, es)
        I["d_kv"] = Dep()
        def load_u32(es_, dst, part, ct, du, du32):
            ub = sb(nc, es_, "u_b", [128, L], BF16)
            kb.dma("sp", ub[:], I["hin"][part * 256 + ct * 128:part * 256 + (ct + 1) * 128, :], writes=[du])
            kb.op("act", lambda e: e.copy(out=dst, in_=ub[:]), reads=[du], writes=[du32])
        I["load_u32"] = load_u32
        I["load_apool"] = lambda ab, g, dab: kb.dma("sp", ab[:], I["apool"][g * 128:(g + 1) * 128, :], writes=[dab])
        I["load_yhy"] = lambda dst, c, d_y: kb.dma("sp", dst, I["yhyT"][c * 128:(c + 1) * 128, :], wacc=[d_y])
        if phase == 1:
            O["store_proj"] = lambda ch, ob, dob, d_out: kb.dma("sp", O["projT"][ch * 128:(ch + 1) * 128, :], ob[:], reads=[dob], wacc=[d_out])
        if phase == 2:
            I["hout"] = O["hout"]
            I["S_f"] = nc.dram_tensor("S_f", [1024, L], BF16).ap()
            I["S_z"] = nc.dram_tensor("S_z", [256, L], F32).ap()
            I["S_y"] = nc.dram_tensor("S_y", [256, L], F32).ap()
            I["d_hout"] = Dep()
            psb = [(ps(nc, es, "psb%d" % i, [128, 512]), Dep()) for i in range(8)]
            emit_hyena(kb, nc, I, psb)
            kb.wait_all("sp", [I["d_hout"]])
        else:
            C = common_tiles(nc, es, kb)
            d_out = emit(kb, nc, C, I, O)
            kb.wait_all("sp", [d_out])
    return nc


_PROGS = {}


def get_program(phase):
    if phase not in _PROGS:
        _PROGS[phase] = build_program(phase)
    return _PROGS[phase]


def run(phase, maps):
    res = run_bass_kernel_spmd(get_program(phase), maps, core_ids=list(range(NCORES)))
    return res.results


def kernel_unfused(**inp):
    inp = {k: np.asarray(v) for k, v in inp.items()}
    x = inp["x"]
    cores = [(c // 2, c % 2) for c in range(NCORES)]
    g1 = np.concatenate([pk(inp["norm_g"][0, 0]), pk(inp["norm_g"][0, 1])], axis=1)
    w_in = tile_kxm(inp["mix_w_in"][0])
    f00 = ffn_w(inp, 0, 0, "f00_")
    maps = []
    for (b, r) in cores:
        m = {"xT_in": np.ascontiguousarray(x[b, r * TOK:(r + 1) * TOK, :].T), "gains": g1, "w_in": w_in}
        m.update(f00)
        maps.append(m)
    r1 = run(1, maps)
    consts = {**hyena_constants(), **hyena_pos_constants()}
    hy = [hyena_core_inputs(inp, r) for r in range(2)]
    maps = []
    for (b, r) in cores:
        proj = np.concatenate([np.asarray(r1[2 * b]["projT"]), np.asarray(r1[2 * b + 1]["projT"])], axis=1)
        hin = np.concatenate([proj[512 + p * 512 + 256 * r: 512 + p * 512 + 256 * r + 256] for p in range(3)], axis=0)
        m = dict(consts)
        m.update(hy[r])
        m["hin"] = np.ascontiguousarray(hin)
        maps.append(m)
    r2 = run(2, maps)
    g3 = np.zeros((128, 32), np.float32)
    g3[:, 0:8] = pk(inp["norm_g"][0, 2])
    g3[:, 8:16] = pk(inp["norm_g"][1, 0])
    g3[:, 16:24] = pk(inp["norm_g"][1, 1])
    pool_w = np.ascontiguousarray(inp["pool_w"][0].transpose(1, 0, 2))
    pool_scale = pk(inp["pool_scale"][0])
    w_mo = tile_kxm(inp["mix_w_out"][0])
    f01 = ffn_w(inp, 0, 1, "f01_")
    f10 = ffn_w(inp, 1, 0, "f10_")
    w_dq = np.ascontiguousarray(inp["mla_w_dq"][0].reshape(8, 128, 256).transpose(1, 0, 2))
    w_dkv = np.ascontiguousarray(inp["mla_w_dkv"][0].reshape(8, 128, 160).transpose(1, 0, 2))
    qkv_g = np.concatenate([pk(inp["mla_q_norm_g"][0]), pk(inp["mla_kv_norm_g"][0])], axis=1)
    maps = []
    for (b, r) in cores:
        own = np.asarray(r1[2 * b + r]["projT"])[0:512]
        oth = np.asarray(r1[2 * b + 1 - r]["projT"])[0:512]
        ap = np.zeros((512, TOK + 32), BFNP)
        ap[:, 16:16 + TOK] = own
        if r == 0:
            ap[:, 16 + TOK:16 + TOK + 16] = oth[:, 0:16]
        else:
            ap[:, 0:16] = oth[:, TOK - 16:TOK]
        yhy = np.concatenate([np.asarray(r2[2 * b]["hout"])[:, r * TOK:(r + 1) * TOK],
                              np.asarray(r2[2 * b + 1]["hout"])[:, r * TOK:(r + 1) * TOK]], axis=0)
        m = {"xT_in": np.asarray(r1[2 * b + r]["xT_out"]), "gains": g3, "apool": ap, "invcnt": invcnt_table(r),
             "yhyT": np.ascontiguousarray(yhy), "pool_w": pool_w, "pool_scale": pool_scale, "w_mo": w_mo,
             "w_dq": w_dq, "w_dkv": w_dkv, "qkv_g": qkv_g, "rope_cs": rope_tables(r)}
        m.update(f01)
        m.update(f10)
        maps.append(m)
    r3 = run(3, maps)
    g4 = np.concatenate([pk(inp["norm_g"][1, 2]), pk(inp["final_norm_g"])], axis=1)
    w_uq = np.ascontiguousarray(inp["mla_w_uq"][0].reshape(2, 128, 1536).transpose(1, 0, 2))
    w_ukv = np.ascontiguousarray(inp["mla_w_ukv"][0])
    w_o = np.ascontiguousarray(inp["mla_w_o"][0].reshape(8, 128, 1024).transpose(1, 0, 2))
    f11 = ffn_w(inp, 1, 1, "f11_")
    maps = []
    for (b, r) in cores:
        kv = np.stack([np.asarray(r3[2 * b]["kvlatT"]), np.asarray(r3[2 * b + 1]["kvlatT"])], axis=0)
        rq = np.zeros((96, 2, TOK), np.float32)
        rq[64:96] = rope_tables(r)
        m = {"xT_in": np.asarray(r3[2 * b + r]["xT_out"]), "gains": g4, "cqnT": np.asarray(r3[2 * b + r]["cqnT"]),
             "kvlat_full": np.ascontiguousarray(kv), "w_uq": w_uq, "w_ukv": w_ukv, "w_o": w_o, "rope_q": rq}
        m.update(f11)
        maps.append(m)
    r4 = run(4, maps)
    out = np.zeros((4, L, D), np.float32)
    for c, (b, r) in enumerate(cores):
        out[b, r * TOK:(r + 1) * TOK, :] = np.asarray(r4[c]["outT"]).T
    return out


I32 = mybir.dt.int32
FUSED_IN = {}
FUSED_IN.update({"xT_in": P1_IN["xT_in"], "g1": ([128, 16], F32), "w_in": P1_IN["w_in"], **FFN_W("f00_")})
FUSED_IN.update({k: v for k, v in P2_IN.items() if k != "hin"})
FUSED_IN.update({k: v for k, v in P3_IN.items() if k not in ("xT_in", "gains", "apool", "yhyT")})
FUSED_IN.update({"g3": ([128, 32], F32)})
FUSED_IN.update({k: v for k, v in P4_IN.items() if k not in ("xT_in", "gains", "cqnT", "kvlat_full")})
FUSED_IN.update({"g4": ([128, 16], F32), "idx_u": ([128, 4], I32), "idx_y": ([128, 4], I32), "halo_mask": ([128, 2], F32)})
FUSED_OUT = {"outT": ([1024, TOK], F32)}


def build_fused(upto=4):
    nc = bass.Bass("TRN2", target_bir_lowering=False)
    IN = declare(nc, FUSED_IN, "ExternalInput")
    OUT = declare(nc, FUSED_OUT, "ExternalOutput")
    xs1 = nc.dram_tensor("xs1", [1024, TOK], F32)
    projP = nc.dram_tensor("projP_i", [512, TOK], BF16)
    projH = [nc.dram_tensor("projH%d_i" % i, [512, TOK], BF16) for i in range(3)]
    GH = [nc.dram_tensor("GH%d" % i, [1024, TOK], BF16) for i in range(3)]
    halo = nc.dram_tensor("halo_i", [512, 32], BF16)
    Ghalo = nc.dram_tensor("Ghalo", [1024, 32], BF16)
    hout = nc.dram_tensor("hout_i", [256, L], BF16)
    G2 = nc.dram_tensor("G2", [512, L], BF16)
    xs3 = nc.dram_tensor("xs3", [1024, TOK], F32)
    cqnT = nc.dram_tensor("cqnT_i", [256, TOK], BF16)
    kvl = nc.dram_tensor("kvl_i", [160, TOK], BF16)
    G3 = nc.dram_tensor("G3", [320, TOK], BF16)
    S_f = nc.dram_tensor("S_f", [1024, L], BF16)
    S_z = nc.dram_tensor("S_z", [256, L], F32)
    S_y = nc.dram_tensor("S_y", [256, L], F32)
    with ExitStack() as es:
        kb = KB(nc, es)
        C = common_tiles(nc, es, kb)
        idx_u = sb(nc, es, "idx_u", [128, 4], I32)
        idx_y = sb(nc, es, "idx_y", [128, 4], I32)
        hmask = sb(nc, es, "hmask", [128, 2], F32)
        kb.dma("pool", idx_u[:], IN["idx_u"][:, :], writes=[C["d_const"]])
        kb.dma("pool", idx_y[:], IN["idx_y"][:, :], writes=[C["d_const"]])
        kb.dma("pool", hmask[:], IN["halo_mask"][:, :], writes=[C["d_const"]])
        kb.wait_all("pool", [C["d_const"]])
        I1 = {"xT_in": IN["xT_in"], "gains": IN["g1"], "w_in": IN["w_in"], "f00_wg": IN["f00_wg"], "f00_wu": IN["f00_wu"], "f00_wd": IN["f00_wd"]}
        def store_proj(ch, ob, dob, d_out):
            if ch < 4:
                kb.dma("sp", projP.ap()[ch * 128:(ch + 1) * 128, :], ob[:], reads=[dob], wacc=[d_out])
                kb.dma("sp", halo.ap()[ch * 128:(ch + 1) * 128, 0:16], ob[:, 0:16], reads=[dob], wacc=[d_out])
                kb.dma("sp", halo.ap()[ch * 128:(ch + 1) * 128, 16:32], ob[:, TOK - 16:TOK], reads=[dob], wacc=[d_out])
            else:
                part, cc = (ch - 4) // 4, (ch - 4) % 4
                kb.dma("sp", projH[part].ap()[cc * 128:(cc + 1) * 128, :], ob[:], reads=[dob], wacc=[d_out])
        d1 = emit_phase1(kb, nc, C, I1, {"xT_out": xs1.ap(), "store_proj": store_proj})
        dG1 = Dep()
        for i in range(3):
            kb.allgather_pairs(projH[i], GH[i], [d1], dG1, acc=True)
        kb.allgather_pairs(halo, Ghalo, [d1], dG1, acc=True)
        if upto == 1:
            return _finish_early(kb, nc, OUT, xs1)
        I2 = {k: IN[k] for k in P2_IN if k != "hin"}
        I2.update({"hout": hout.ap(), "S_f": S_f.ap(), "S_z": S_z.ap(), "S_y": S_y.ap(), "d_hout": Dep()})
        def load_u32(es_, dst, part, ct, du, du32):
            ua = sb(nc, es_, "u_a", [128, L], BF16)
            ubb = sb(nc, es_, "u_bb", [128, L], BF16)
            for th in range(2):
                kb.dma("sp", ua[:, th * TOK:(th + 1) * TOK], GH[part].ap()[th * 512 + ct * 128:th * 512 + (ct + 1) * 128, :], reads=[dG1], wacc=[du])
                kb.dma("sp", ubb[:, th * TOK:(th + 1) * TOK], GH[part].ap()[th * 512 + 256 + ct * 128:th * 512 + 256 + (ct + 1) * 128, :], reads=[dG1], wacc=[du])
            kb.op("act", lambda e: e.activation(out=dst, in_=ua[:], func=AF.Copy, scale=hmask[:, 1:2]), reads=[du, C["d_const"]], writes=[du32])
            kb.op("dve", lambda e: e.scalar_tensor_tensor(out=dst, in0=ubb[:], scalar=hmask[:, 0:1], in1=dst, op0=ALU.mult, op1=ALU.add),
                  reads=[du, du32], writes=[du32])
        I2["load_u32"] = load_u32
        psb = [(C["ps_" + n], C["d_ps_" + n]) for n in "abcdefgh"]
        emit_hyena(kb, nc, I2, psb)
        dG2 = Dep()
        kb.allgather_pairs(hout, G2, [I2["d_hout"]], dG2)
        if upto == 2:
            return _finish_early(kb, nc, OUT, xs1)
        I3 = {k: IN[k] for k in P3_IN if k not in ("xT_in", "gains", "apool", "yhyT")}
        I3.update({"xT_in": xs1.ap(), "gains": IN["g3"], "halo_mask": hmask, "d_x": d1})
        projap = projP.ap()
        Ghap = Ghalo.ap()

        es_y3 = ExitStack()
        ytmp = [sb(nc, es_y3, "ytmp%d" % i, [128, L], BF16, side="right") for i in range(2)]
        dyt = [Dep(), Dep()]

        def load_yhy(dst, c, d_y):
            i = c % 2
            kb.dma("sp", ytmp[i][:], G2.ap()[c * 128:(c + 1) * 128, :], reads=[dG2], writes=[dyt[i]])
            kb.op("act", lambda e: e.activation(out=dst, in_=ytmp[i][:, 0:TOK], func=AF.Copy, scale=hmask[:, 1:2]),
                  reads=[dyt[i], C["d_const"]], writes=[d_y])
            kb.op("dve", lambda e: e.scalar_tensor_tensor(out=dst, in0=ytmp[i][:, TOK:L], scalar=hmask[:, 0:1], in1=dst,
                                                          op0=ALU.mult, op1=ALU.add), reads=[dyt[i], d_y], writes=[d_y])

        def load_apool(ab, g, dab):
            kb.dma("sp", ab[:, 16:16 + TOK], projap[g * 128:(g + 1) * 128, :], reads=[d1], wacc=[dab])
            kb.dma("sp", ab[:, 0:16], Ghap[g * 128:(g + 1) * 128, 16:32], reads=[dG1], wacc=[dab])
            kb.dma("sp", ab[:, 16 + TOK:32 + TOK], Ghap[512 + g * 128:512 + (g + 1) * 128, 0:16], reads=[dG1], wacc=[dab])
        I3["load_yhy"] = load_yhy

        def after_mixer():
            kb.drain_scope(dyt)
            es_y3.close()
        I3["after_mixer"] = after_mixer
        I3["load_apool"] = load_apool
        d3 = emit_phase3(kb, nc, C, I3, {"xT_out": xs3.ap(), "cqnT": cqnT.ap(), "kvlatT": kvl.ap()})
        kb.drain_scope([d3])
        dG3 = Dep()
        kb.allgather_pairs(kvl, G3, [d3], dG3)
        if upto == 3:
            return _finish_early(kb, nc, OUT, xs3)
        I4 = {k: IN[k] for k in P4_IN if k not in ("xT_in", "gains", "cqnT", "kvlat_full")}
        I4.update({"xT_in": xs3.ap(), "gains": IN["g4"], "cqnT": cqnT.ap(),
                   "kvlat_full": G3.ap().rearrange("(k c) t -> k c t", k=2), "d_kv": dG3})
        d4 = emit_phase4(kb, nc, C, I4, {"outT": OUT["outT"]})
        kb.wait_all("sp", [d4])
        kb.drain_scope([d4])
    return nc


def _finish_early(kb, nc, OUT, src):
    d = Dep()
    kb.dma("sp", OUT["outT"], src.ap(), writes=[d])
    kb.wait_all("sp", [d])
    return nc


_FUSED = []
UPTO = 4
DBG_STATIC_U = False


def kernel(**inp):
    inp = {k: np.asarray(v) for k, v in inp.items()}
    x = inp["x"]
    if not _FUSED:
        _FUSED.append(build_fused(UPTO))
    nc = _FUSED[0]
    cores = [(c // 2, c % 2) for c in range(NCORES)]
    shared = {}
    shared["g1"] = np.concatenate([pk(inp["norm_g"][0, 0]), pk(inp["norm_g"][0, 1])], axis=1)
    shared["w_in"] = tile_kxm(inp["mix_w_in"][0])
    shared.update(ffn_w(inp, 0, 0, "f00_"))
    shared.update(hyena_constants())
    shared.update(hyena_pos_constants())
    g3 = np.zeros((128, 32), np.float32)
    g3[:, 0:8] = pk(inp["norm_g"][0, 2])
    g3[:, 8:16] = pk(inp["norm_g"][1, 0])
    g3[:, 16:24] = pk(inp["norm_g"][1, 1])
    shared["g3"] = g3
    shared["pool_w"] = np.ascontiguousarray(inp["pool_w"][0].transpose(1, 0, 2))
    shared["pool_scale"] = pk(inp["pool_scale"][0])
    shared["w_mo"] = tile_kxm(inp["mix_w_out"][0])
    shared.update(ffn_w(inp, 0, 1, "f01_"))
    shared.update(ffn_w(inp, 1, 0, "f10_"))
    shared["w_dq"] = np.ascontiguousarray(inp["mla_w_dq"][0].reshape(8, 128, 256).transpose(1, 0, 2))
    shared["w_dkv"] = np.ascontiguousarray(inp["mla_w_dkv"][0].reshape(8, 128, 160).transpose(1, 0, 2))
    shared["qkv_g"] = np.concatenate([pk(inp["mla_q_norm_g"][0]), pk(inp["mla_kv_norm_g"][0])], axis=1)
    shared["g4"] = np.concatenate([pk(inp["norm_g"][1, 2]), pk(inp["final_norm_g"])], axis=1)
    shared["w_uq"] = np.ascontiguousarray(inp["mla_w_uq"][0].reshape(2, 128, 1536).transpose(1, 0, 2))
    shared["w_ukv"] = np.ascontiguousarray(inp["mla_w_ukv"][0])
    shared["w_o"] = np.ascontiguousarray(inp["mla_w_o"][0].reshape(8, 128, 1024).transpose(1, 0, 2))
    shared.update(ffn_w(inp, 1, 1, "f11_"))
    hy = [hyena_core_inputs(inp, r) for r in range(2)]
    p = np.arange(128)
    maps = []
    for (b, r) in cores:
        m = dict(shared)
        m.update(hy[r])
        m["xT_in"] = np.ascontiguousarray(x[b, r * TOK:(r + 1) * TOK, :].T)
        m["invcnt"] = invcnt_table(r)
        m["rope_cs"] = rope_tables(r)
        rq = np.zeros((96, 2, TOK), np.float32)
        rq[64:96] = m["rope_cs"]
        m["rope_q"] = rq
        iu = np.zeros((128, 4), np.int32)
        for ct in range(2):
            for th in range(2):
                iu[:, ct * 2 + th] = th * 512 + 256 * r + ct * 128 + p
        m["idx_u"] = iu
        iy = np.zeros((128, 4), np.int32)
        for c in range(4):
            iy[:, c] = (c // 2) * 512 + ((c % 2) * 128 + p) * 2 + r
        m["idx_y"] = iy
        hm = np.zeros((128, 2), np.float32)
        hm[:, 0] = 1.0 if r == 1 else 0.0
        hm[:, 1] = 1.0 if r == 0 else 0.0
        m["halo_mask"] = hm
        maps.append(m)
    res = run_bass_kernel_spmd(nc, maps, core_ids=list(range(NCORES)))
    out = np.zeros((4, L, D), np.float32)
    for c, (b, r) in enumerate(cores):
        out[b, r * TOK:(r + 1) * TOK, :] = np.asarray(res.results[c]["outT"]).T
    return out
```

```python
import math
from contextlib import ExitStack

import numpy as np
import ml_dtypes

import concourse.bass as bass
import concourse.mybir as mybir
from concourse.bass_utils import run_bass_kernel_spmd

F32 = mybir.dt.float32
BF16 = mybir.dt.bfloat16
AF = mybir.ActivationFunctionType
ALU = mybir.AluOpType
AX = mybir.AxisListType

D = 1024
DFF = 2816
NF = DFF // 128
L = 4096
TOK = 2048
EPS = 1e-6
NCORES = 8


class Dep:
    __slots__ = ("w", "r", "name")

    def __init__(self, name=""):
        self.w = []
        self.r = []
        self.name = name


class Slot:
    __slots__ = ("sem", "val")

    def __init__(self, sem):
        self.sem = sem
        self.val = 0


class KB:
    ENG = ("pe", "act", "dve", "pool", "sp")

    def __init__(self, nc, es, ring_sizes=None):
        self.nc = nc
        self.es = es
        self.engs = {"pe": nc.tensor, "act": nc.scalar, "dve": nc.vector,
                     "pool": nc.gpsimd, "sp": nc.sync}
        self.sem = {}
        self.cnt = {}
        self.waited = {e: {} for e in self.ENG}
        for e in ("pe", "act", "dve", "pool"):
            self.sem[e] = es.enter_context(nc.semaphore("s_" + e))
            self.cnt[e] = 0
        rs = {"sp": 40, "pool": 24, "act": 8}
        if ring_sizes:
            rs.update(ring_sizes)
        self.rings = {}
        self.ring_pos = {}
        for q, n in rs.items():
            self.rings[q] = [Slot(es.enter_context(nc.semaphore("d_%s%d" % (q, i)))) for i in range(n)]
            self.ring_pos[q] = 0
        self.ninstr = {e: 0 for e in self.ENG}
        self.cc = []

    def _wait(self, E, tok):
        src, val = tok
        if isinstance(src, Slot):
            key = id(src)
            if self.waited[E].get(key, 0) >= val:
                return
            self.engs[E].wait_ge(src.sem, val)
            self.waited[E][key] = val
        else:
            if src == E and src == "pe":
                return
            if self.waited[E].get(src, 0) >= val:
                return
            self.engs[E].wait_ge(self.sem[src], val)
            self.waited[E][src] = val
        self.ninstr[E] += 1

    def _deps(self, E, reads, writes):
        for d in reads:
            for t in d.w:
                self._wait(E, t)
        for d in writes:
            for t in d.w:
                self._wait(E, t)
            for t in d.r:
                self._wait(E, t)

    def _commit(self, tok, reads, writes, wacc=()):
        for d in wacc:
            d.w.append(tok)
        for d in reads:
            d.r.append(tok)
            if len(d.r) > 64:
                d.r = d.r[-48:]
        for d in writes:
            d.w = [tok]
            d.r = []

    def op(self, E, fn, reads=(), writes=(), inc=True, wacc=()):
        self._deps(E, reads, writes)
        for d in wacc:
            for t in d.r:
                self._wait(E, t)
        ins = fn(self.engs[E])
        self.ninstr[E] += 1
        if inc:
            self.cnt[E] += 1
            ins.then_inc(self.sem[E], 1)
            tok = (E, self.cnt[E])
        else:
            tok = (E, self.cnt[E] + 1)
        self._commit(tok, reads, writes, wacc)
        return tok

    def dma(self, q, out, in_, reads=(), writes=(), wacc=(), **kw):
        ring = self.rings[q]
        slot = ring[self.ring_pos[q] % len(ring)]
        self.ring_pos[q] += 1
        if slot.val > 0:
            self._wait(q, (slot, slot.val))
        self._deps(q, reads, writes)
        for d in wacc:
            for t in d.r:
                self._wait(q, t)
        ins = self.engs[q].dma_start(out=out, in_=in_, **kw)
        self.ninstr[q] += 1
        slot.val += 16
        ins.then_inc(slot.sem, 16)
        tok = (slot, slot.val)
        self._commit(tok, reads, writes, wacc)
        return tok

    def gather(self, out, in_, idx_ap, reads=(), writes=(), wacc=()):
        q = "pool"
        ring = self.rings[q]
        slot = ring[self.ring_pos[q] % len(ring)]
        self.ring_pos[q] += 1
        if slot.val > 0:
            self._wait(q, (slot, slot.val))
        self._deps(q, reads, writes)
        for d in wacc:
            for t in d.r:
                self._wait(q, t)
        ins = self.engs[q].indirect_dma_start(out=out, out_offset=None, in_=in_,
                                              in_offset=bass.IndirectOffsetOnAxis(ap=idx_ap, axis=0))
        self.ninstr[q] += 1
        slot.val += 16
        ins.then_inc(slot.sem, 16)
        tok = (slot, slot.val)
        self._commit(tok, reads, writes, wacc)
        return tok

    def allgather_pairs(self, src_t, dst_t, src_deps, d_dst, acc=False):
        for d in src_deps:
            for t in d.w:
                self._wait("pool", t)
        for t in d_dst.r + d_dst.w:
            self._wait("pool", t)
        sem = self.es.enter_context(self.nc.semaphore("cc%d" % len(self.cc)))
        slot = Slot(sem)
        self.cc.append(slot)
        self.nc.gpsimd.collective_compute("AllGather", ALU.bypass, replica_groups=[[0, 1], [2, 3], [4, 5], [6, 7]],
                                          ins=[src_t.ap().opt()], outs=[dst_t.ap().opt()]).then_inc(sem)
        slot.val = 1
        self.ninstr["pool"] += 1
        if acc:
            d_dst.w.append((slot, 1))
        else:
            d_dst.w = [(slot, 1)]
            d_dst.r = []

    def wait_all(self, E, deps):
        for d in deps:
            for t in d.w:
                self._wait(E, t)


_UNIQ = [0]


def sb(nc, es, name, shape, dt, side=None):
    _UNIQ[0] += 1
    if side is None:
        return es.enter_context(nc.sbuf_tensor("%s_%d" % (name, _UNIQ[0]), list(shape), dt))
    return es.enter_context(nc.sbuf_tensor("%s_%d" % (name, _UNIQ[0]), list(shape), dt, side=side))


def ps(nc, es, name, shape, dt=F32):
    return es.enter_context(nc.psum_tensor(name, list(shape), dt))


def emit_rmsnorm(kb, C, xT, dxs, g_ap, hT, d_h, t0, ntok, h0=0, nch=8, ones=None, np_=128):
    if ones is None:
        ones = C["ones_b"]
    for tt in range(ntok // 512):
        a = t0 + tt * 512
        d_x = dxs[a // 512]
        for k in range(nch):
            kb.op("act", lambda e, k=k: e.activation(out=C["sq"][0:np_, k, :], in_=xT[:, k, a:a + 512], func=AF.Square),
                  reads=[d_x], writes=[C["d_sq"]])
        for k in range(nch):
            kb.op("pe", lambda e, k=k: e.matmul(C["ps_ss"][0:np_, :], lhsT=ones[0:np_, 0:np_], rhs=C["sq"][0:np_, k, :],
                                                 start=(k == 0), stop=(k == nch - 1)),
                  reads=[C["d_sq"], C["d_const"]], writes=[C["d_ps_ss"]], inc=(k == nch - 1))
        kb.op("act", lambda e: e.activation(out=C["rstd"][0:np_, :], in_=C["ps_ss"][0:np_, :], func=AF.Sqrt, bias=C["eps"][0:np_, 0:1]),
              reads=[C["d_ps_ss"], C["d_const"]], writes=[C["d_rstd"]])
        kb.op("dve", lambda e: e.reciprocal(out=C["rstd"][0:np_, :], in_=C["rstd"][0:np_, :]),
              reads=[C["d_rstd"]], writes=[C["d_rstd"]])
        for k in range(nch):
            o = h0 + tt * 512
            kb.op("dve", lambda e, k=k, o=o: e.scalar_tensor_tensor(
                out=hT[:, k, o:o + 512], in0=xT[:, k, a:a + 512], scalar=g_ap[:, k:k + 1],
                in1=C["rstd"][0:np_, :], op0=ALU.mult, op1=ALU.mult),
                reads=[d_x, C["d_rstd"], C["d_const"]], writes=[d_h])


def emit_ffn(kb, C, xT, dxs, g_ap, wg_t, wu_t, wd_t, tag):
    nc = kb.nc
    with ExitStack() as es:
        hT = sb(nc, es, "hT" + tag, [128, 8, TOK], BF16)
        aT = sb(nc, es, "aT" + tag, [128, 12, TOK], BF16)
        NB = 3
        wgb = [sb(nc, es, "wg%d%s" % (i, tag), [128, 8, 256], BF16) for i in range(NB)]
        wub = [sb(nc, es, "wu%d%s" % (i, tag), [128, 8, 256], BF16) for i in range(NB)]
        wdb = [sb(nc, es, "wd%d%s" % (i, tag), [128, 12, 256], BF16) for i in range(2)]
        sg = [sb(nc, es, "sg%d%s" % (i, tag), [128, 512], F32) for i in range(2)]
        d_wg = [Dep() for _ in range(NB)]
        d_wu = [Dep() for _ in range(NB)]
        d_wd = [Dep() for _ in range(2)]
        d_sg = [Dep() for _ in range(2)]
        d_h = Dep()
        d_a = [Dep() for _ in range(12)]
        ps_g = [C["ps_a"], C["ps_b"]]
        ps_u = [C["ps_c"], C["ps_d"]]
        d_psg = [C["d_ps_a"], C["d_ps_b"]]
        d_psu = [C["d_ps_c"], C["d_ps_d"]]
        ps_y = [C["ps_e"], C["ps_f"]]
        d_psy = [C["d_ps_e"], C["d_ps_f"]]

        def load_gu(fg):
            b = fg % NB
            kb.dma("pool", wgb[b][:], wg_t[fg], writes=[d_wg[b]])
            kb.dma("pool", wub[b][:], wu_t[fg], writes=[d_wu[b]])

        def load_d(p, dg):
            b = dg % 2
            nfc = 12 if p == 0 else 10
            kb.dma("pool", wdb[b][:, 0:nfc, :], wd_t[p, dg, :, 0:nfc, :], writes=[d_wd[b]])

        load_gu(0)
        load_gu(1)
        emit_rmsnorm(kb, C, xT, dxs, g_ap, hT, d_h, 0, TOK)
        it = 0
        yit = 0
        for p in range(2):
            g0, g1 = (0, 6) if p == 0 else (6, 11)
            nfc = (g1 - g0) * 2
            for fg in range(g0, g1):
                if fg + 2 < 11:
                    load_gu(fg + 2)
                if fg == g1 - 2:
                    load_d(p, 0)
                if fg == g1 - 1:
                    load_d(p, 1)
                b = fg % NB
                for j in range(2):
                    fl = (fg - g0) * 2 + j
                    for tt in range(4):
                        pb = it % 2
                        it += 1
                        for k in range(8):
                            kb.op("pe", lambda e, k=k: e.matmul(
                                ps_g[pb][:], lhsT=wgb[b][:, k, j * 128:(j + 1) * 128],
                                rhs=hT[:, k, tt * 512:(tt + 1) * 512], start=(k == 0), stop=(k == 7)),
                                reads=[d_wg[b], d_h], writes=[d_psg[pb]], inc=(k == 7))
                        for k in range(8):
                            kb.op("pe", lambda e, k=k: e.matmul(
                                ps_u[pb][:], lhsT=wub[b][:, k, j * 128:(j + 1) * 128],
                                rhs=hT[:, k, tt * 512:(tt + 1) * 512], start=(k == 0), stop=(k == 7)),
                                reads=[d_wu[b], d_h], writes=[d_psu[pb]], inc=(k == 7))
                        kb.op("act", lambda e: e.activation(out=sg[pb][:], in_=ps_g[pb][:], func=AF.Silu),
                              reads=[d_psg[pb]], writes=[d_sg[pb]])
                        kb.op("dve", lambda e: e.tensor_tensor(
                            out=aT[:, fl, tt * 512:(tt + 1) * 512], in0=ps_u[pb][:], in1=sg[pb][:], op=ALU.mult),
                            reads=[d_psu[pb], d_sg[pb]], writes=[d_a[fl]])
            for dg in range(4):
                b = dg % 2
                for dj in range(2):
                    dch = dg * 2 + dj
                    for tt in range(4):
                        yb = yit % 2
                        yit += 1
                        for f in range(nfc):
                            kb.op("pe", lambda e, f=f: e.matmul(
                                ps_y[yb][:], lhsT=wdb[b][:, f, dj * 128:(dj + 1) * 128],
                                rhs=aT[:, f, tt * 512:(tt + 1) * 512], start=(f == 0), stop=(f == nfc - 1)),
                                reads=[d_wd[b], d_a[f]], writes=[d_psy[yb]], inc=(f == nfc - 1))
                        a = tt * 512
                        d_x = dxs[tt]
                        kb.op("dve", lambda e: e.scalar_tensor_tensor(
                            out=xT[:, dch, a:a + 512], in0=ps_y[yb][:], scalar=0.5,
                            in1=xT[:, dch, a:a + 512], op0=ALU.mult, op1=ALU.add),
                            reads=[d_psy[yb], d_x], writes=[d_x])
                if dg + 2 < 4:
                    load_d(p, dg + 2)
        kb.drain_scope([d_h] + d_a + d_wg + d_wu + d_wd + d_sg)


def _drain_scope(self, deps):
    toks = []
    for d in deps:
        toks += d.w + d.r
    for E in ("pe", "act", "dve", "pool", "sp"):
        for t in toks:
            self._wait(E, t)


KB.drain_scope = _drain_scope


def common_tiles(nc, es, kb):
    C = {}
    C["ones_b"] = sb(nc, es, "ones_b", [128, 128], BF16)
    C["sq"] = sb(nc, es, "sq", [128, 8, 512], BF16)
    C["rstd"] = sb(nc, es, "rstd", [128, 512], F32)
    for n2 in ("ab", "cd", "ef", "gh"):
        t2 = ps(nc, es, "ps_" + n2, [128, 1024])
        C["ps_" + n2] = t2
        C["ps_" + n2[0]] = t2[:, 0:512]
        C["ps_" + n2[1]] = t2[:, 512:1024]
    for n in "abcdefgh":
        C["d_ps_" + n] = Dep()
    C["ps_ss"] = C["ps_g"]
    C["d_ps_ss"] = C["d_ps_g"]
    for n in ("d_sq", "d_rstd", "d_const"):
        C[n] = Dep()
    C["rr"] = 0
    C["eps"] = sb(nc, es, "eps", [128, 1], F32)
    kb.op("pool", lambda e: e.memset(C["ones_b"][:], 1.0 / 1024.0), writes=[C["d_const"]])
    kb.op("pool", lambda e: e.memset(C["eps"][:], EPS), writes=[C["d_const"]])
    C["ones256"] = sb(nc, es, "ones256", [128, 128], BF16)
    C["ones128"] = sb(nc, es, "ones128", [128, 128], BF16)
    kb.op("pool", lambda e: e.memset(C["ones256"][:], 1.0 / 256.0), writes=[C["d_const"]])
    kb.op("pool", lambda e: e.memset(C["ones128"][:], 1.0 / 128.0), writes=[C["d_const"]])
    return C


NFFT = 8192
CH = 64
HALF = CH // 2


def hyena_constants():
    bf = ml_dtypes.bfloat16
    t1 = np.arange(32)[:, None].astype(np.float64)
    f1 = np.arange(32)[None, :].astype(np.float64)
    ang = 2 * np.pi * (f1 + 0.5) * t1 / 64.0
    d1 = np.concatenate([np.cos(ang), -np.sin(ang)], axis=1)
    t2 = np.arange(128)[:, None].astype(np.float64)
    f2 = np.arange(128)[None, :].astype(np.float64)
    w = np.zeros((128, 32, 3, 128))
    for a in range(32):
        th = 2 * np.pi * ((a + 0.5) * t2 / NFFT + f2 * t2 / 128.0)
        w[:, a, 0] = np.cos(th)
        w[:, a, 1] = -np.sin(th)
        w[:, a, 2] = np.sin(th)
    f2c = np.arange(128)[:, None].astype(np.float64)
    t2r = np.arange(128)[None, :].astype(np.float64)
    ph = 2 * np.pi * f2c * t2r / 128.0
    i1 = np.zeros((128, 2, 256))
    i1[:, 0, :128] = np.cos(ph)
    i1[:, 0, 128:] = np.sin(ph)
    i1[:, 1, :128] = -np.sin(ph)
    i1[:, 1, 128:] = np.cos(ph)
    g = np.zeros((64, 128, 32))
    f1c = np.arange(32)[:, None].astype(np.float64)
    t1r = np.arange(32)[None, :].astype(np.float64)
    for b in range(128):
        phi = 2 * np.pi * (f1c + 0.5) * (t1r / 64.0 + b / NFFT)
        g[0:32, b] = (2.0 / NFFT) * np.cos(phi)
        g[32:64, b] = -(2.0 / NFFT) * np.sin(phi)
    return {"c_d1": d1.astype(bf), "c_f2": w.astype(bf), "c_i1": i1.astype(bf), "c_i2": g.astype(bf)}


def hyena_pos_constants():
    f32 = np.float32
    t = np.linspace(0.0, 1.0, L, dtype=f32)
    bands = 16
    w = (2.0 * math.pi * np.arange(L, dtype=f32) / L).astype(f32)
    f = np.linspace(1e-4, bands - 1, bands, dtype=f32)
    phase = (w[:, None] * f[None, :]).astype(f32)
    z = np.concatenate([t[:, None], np.cos(phase), -np.sin(phase)], axis=-1).astype(f32)
    return {"c_zT": np.ascontiguousarray(z.T), "c_trow": np.ascontiguousarray(-t[None, :])}


def bcast_rows(ap_row, n):
    if hasattr(ap_row, "broadcast"):
        return ap_row.broadcast(0, n)
    return ap_row[0:1, :].to_broadcast([n, ap_row.shape[1]])


def emit_fft_fwd(kb, HC, src, ncols_chunks, consumer, src_is_f32, src_dep):
    nc = kb.nc
    for ci in range(ncols_chunks):
        xb = ci % 2
        X, dX = HC["X"][xb], HC["dX"][xb]
        for (ap, c0, n) in src(ci):
            kb.dma("pool" if src_is_f32 else "sp", X[:, c0:c0 + n, :],
                   ap.rearrange("c (t1 t2) -> t1 c t2", t2=128), reads=[src_dep], writes=[dX])
        A, dA = HC["A"][xb], HC["dA"][xb]
        for j0 in range(0, CH, 8):
            pb = (j0 // 8) % 2
            psA, dpsA = HC["psA"][pb], HC["dpsA"][pb]
            for j in range(8):
                kb.op("pe", lambda e, j=j: e.matmul(psA[:, j * 64:(j + 1) * 64], lhsT=X[:, j0 + j, :], rhs=HC["d1"][:],
                                                    start=True, stop=True),
                      reads=[dX, HC["dconst"]], writes=[dpsA], inc=(j == 7))
            eng = "act" if (j0 // 8) % 2 == 0 else "dve"
            if eng == "act":
                kb.op("act", lambda e: e.copy(out=A[:, j0:j0 + 8, :], in_=psA[:].rearrange("p (j f) -> p j f", f=64)),
                      reads=[dpsA], writes=[dA])
            else:
                kb.op("dve", lambda e: e.tensor_copy(out=A[:, j0:j0 + 8, :], in_=psA[:].rearrange("p (j f) -> p j f", f=64)),
                      reads=[dpsA], writes=[dA])
        for q in range(8):
            pb = q % 2
            psU, dpsU = HC["psU"][pb], HC["dpsU"][pb]
            U4 = psU[:, 0:8 * CH].rearrange("p (a r c) -> p a r c", a=4, r=2)
            for a in range(4):
                f1 = 4 * q + a
                Ar = A[:, :, f1]
                Ai = A[:, :, 32 + f1]
                W = HC["f2"]
                kb.op("pe", lambda e: e.matmul(U4[:, a, 0, :], lhsT=W[:, f1, 0, :], rhs=Ar, start=True, stop=False),
                      reads=[dA, HC["dconst"]], writes=[dpsU], inc=False)
                kb.op("pe", lambda e: e.matmul(U4[:, a, 0, :], lhsT=W[:, f1, 2, :], rhs=Ai, start=False, stop=True),
                      reads=[dA], writes=[dpsU], inc=False)
                kb.op("pe", lambda e: e.matmul(U4[:, a, 1, :], lhsT=W[:, f1, 1, :], rhs=Ar, start=True, stop=False),
                      reads=[dA], writes=[dpsU], inc=False)
                kb.op("pe", lambda e: e.matmul(U4[:, a, 1, :], lhsT=W[:, f1, 0, :], rhs=Ai, start=False, stop=True),
                      reads=[dA], writes=[dpsU], inc=(a == 3))
            consumer(ci, q, U4, dpsU)


def emit_fft_inv(kb, HC, ci, Y, dY, dst, Y2=None, dY2=None):
    Z, dZ = HC["Z"], HC["dZ"]
    for j0 in range(0, CH, 4):
        pb = (j0 // 4) % 2
        psZ, dpsZ = HC["psZ"][pb], HC["dpsZ"][pb]
        for j in range(4):
            kb.op("pe", lambda e, j=j: e.matmul(psZ[0:64, j * 128:(j + 1) * 128], lhsT=Y[:, :, j0 + j],
                                                rhs=HC["i1"][:, 0, 0:128], start=True, stop=False),
                  reads=[dY, HC["dconst"]], writes=[dpsZ], inc=False)
            kb.op("pe", lambda e, j=j: e.matmul(psZ[0:64, j * 128:(j + 1) * 128], lhsT=Y2[:, :, j0 + j],
                                                rhs=HC["i1"][:, 0, 128:256], start=False, stop=True),
                  reads=[dY2], writes=[dpsZ], inc=(j == 3))
        if pb == 0:
            kb.op("act", lambda e: e.copy(out=Z[:, j0:j0 + 4, :], in_=psZ[0:64, :].rearrange("p (j f) -> p j f", f=128)),
                  reads=[dpsZ], writes=[dZ])
        else:
            kb.op("dve", lambda e: e.tensor_copy(out=Z[:, j0:j0 + 4, :], in_=psZ[0:64, :].rearrange("p (j f) -> p j f", f=128)),
                  reads=[dpsZ], writes=[dZ])
    ysb, dys = HC["ysb"], HC["dys"]
    G = HC["i2"]
    for b0 in range(0, 128, 8):
        pb = (b0 // 8) % 2
        psy, dpsy = HC["psy"][pb], HC["dpsy"][pb]
        for b in range(8):
            t2 = b0 + b
            kb.op("pe", lambda e, b=b, t2=t2: e.matmul(psy[0:32, b * CH:(b + 1) * CH], lhsT=G[:, t2, :],
                                                     rhs=Z[:, :, t2], start=True, stop=True),
                  reads=[dZ, HC["dconst"]], writes=[dpsy], inc=(b == 7))
        src = psy[0:32, 0:8 * CH].rearrange("p (b c) -> p b c", c=CH)
        dstv = ysb[:, :, b0:b0 + 8].rearrange("p c b -> p b c")
        if pb == 0:
            kb.op("act", lambda e: e.copy(out=dstv, in_=src), reads=[dpsy], writes=[dys])
        else:
            kb.op("dve", lambda e: e.tensor_copy(out=dstv, in_=src), reads=[dpsy], writes=[dys])
    kb.dma("sp", dst.rearrange("c (t1 t2) -> t1 c t2", t2=128), ysb[:], reads=[dys], wacc=[HC["d_sy"]])


def emit_sin_layer(kb, nc, S, w_sb, kdim, rhs_of, fcol, bfcol, hout, d_in, d_out):
    MAGIC = 12582912.0
    for tc in range(8):
        pb = tc % 2
        pp, dpp = S["ps"][pb], S["dps"][pb]
        kb.op("pe", lambda e: e.matmul(pp[0:64, :], lhsT=w_sb, rhs=rhs_of(tc), start=True, stop=True),
              reads=[d_in, S["dw"]], writes=[dpp])
        a, k = S["arg"][pb], S["kk"][pb]
        da = S["darg"][pb]
        kb.op("act", lambda e: e.activation(out=a[:], in_=pp[0:64, :], func=AF.Identity, scale=fcol, bias=bfcol),
              reads=[dpp, S["dw"]], writes=[da])
        kb.op("dve", lambda e: e.tensor_scalar(out=k[:], in0=a[:], scalar1=1.0 / (2 * math.pi), scalar2=MAGIC,
                                               op0=ALU.mult, op1=ALU.add), reads=[da], writes=[da])
        kb.op("dve", lambda e: e.tensor_scalar(out=k[:], in0=k[:], scalar1=MAGIC, scalar2=2 * math.pi,
                                               op0=ALU.subtract, op1=ALU.mult), reads=[da], writes=[da])
        kb.op("dve", lambda e: e.tensor_tensor(out=a[:], in0=a[:], in1=k[:], op=ALU.subtract), reads=[da], writes=[da])
        kb.op("act", lambda e: e.activation(out=hout[:, tc * 512:(tc + 1) * 512], in_=a[:], func=AF.Sin,
                                            scale=1.0 - 2e-6), reads=[da], writes=[d_out])


def emit_hyena(kb, nc, I, psb):
    with ExitStack() as es0:
        HC = {}
        HC["dconst"] = Dep()
        HC["d1"] = sb(nc, es0, "h_d1", [32, 64], BF16)
        HC["f2"] = sb(nc, es0, "h_f2", [128, 32, 3, 128], BF16)
        HC["i1"] = sb(nc, es0, "h_i1", [128, 2, 256], BF16)
        HC["i2"] = sb(nc, es0, "h_i2", [64, 128, 32], BF16)
        def load_tables():
            kb.dma("sp", HC["d1"][:], I["c_d1"][:, :], wacc=[HC["dconst"]])
            kb.dma("sp", HC["f2"][:], I["c_f2"][:, :, :, :], wacc=[HC["dconst"]])
            kb.dma("sp", HC["i1"][:], I["c_i1"][:, :, :], wacc=[HC["dconst"]])
            kb.dma("sp", HC["i2"][:], I["c_i2"][:, :, :], wacc=[HC["dconst"]])
        convw = sb(nc, es0, "h_convw", [128, 6, 3], F32)
        convb = sb(nc, es0, "h_convb", [128, 6], F32)
        hbias = sb(nc, es0, "h_bias", [128, 4], F32)
        kb.dma("sp", convw[:], I["conv_w"][:, :, :], wacc=[HC["dconst"]])
        kb.dma("sp", convb[:], I["conv_b"][:, :], wacc=[HC["dconst"]])
        kb.dma("sp", hbias[:], I["hy_bias"][:, :], wacc=[HC["dconst"]])
        names = ["psA", "psU", "psZ", "psy"]
        for i, n in enumerate(names):
            HC[n] = [psb[2 * i][0], psb[2 * i + 1][0]]
            HC["d" + n] = [psb[2 * i][1], psb[2 * i + 1][1]]
        HC["d_sy"] = Dep()
        d_sf = Dep()

        with ExitStack() as es:
            S = {}
            S["ps"] = [psb[0][0], psb[1][0]]
            S["dps"] = [psb[0][1], psb[1][1]]
            S["dw"] = Dep()
            zT = sb(nc, es, "f_zT", [33, L], F32)
            w1 = sb(nc, es, "f_w1", [33, 64], F32)
            w2 = sb(nc, es, "f_w2", [64, 64], F32)
            w3 = sb(nc, es, "f_w3", [64, 64], F32)
            wo = sb(nc, es, "f_wo", [64, 1024], F32)
            bb = sb(nc, es, "f_b", [64, 3], F32)
            fq = sb(nc, es, "f_fq", [64, 3], F32)
            bf_ = sb(nc, es, "f_bf", [64, 3], F32)
            dec = sb(nc, es, "f_dec", [128, 8], F32)
            trow = sb(nc, es, "f_trow", [128, L], F32)
            for t_, s_ in ((zT, I["c_zT"]), (w1, I["w1"]), (w2, I["w2"]), (w3, I["w3"]), (wo, I["w_out"]),
                           (bb, I["b123"]), (fq, I["sin_freq"]), (dec, I["decay"])):
                kb.dma("sp", t_[:], s_[:, :], wacc=[S["dw"]])
            kb.dma("sp", trow[:], bcast_rows(I["c_trow"], 128), wacc=[S["dw"]])
            load_tables()
            kb.op("dve", lambda e: e.tensor_tensor(out=bf_[:], in0=bb[:], in1=fq[:], op=ALU.mult),
                  reads=[S["dw"]], writes=[S["dw"]])
            kb.op("act", lambda e: e.activation(out=dec[:], in_=dec[:], func=AF.Abs), reads=[S["dw"]], writes=[S["dw"]])
            S["arg"] = [sb(nc, es, "f_arg%d" % i, [64, 512], F32) for i in range(2)]
            S["kk"] = [sb(nc, es, "f_kk%d" % i, [64, 512], F32) for i in range(2)]
            S["darg"] = [Dep(), Dep()]
            hA = sb(nc, es, "f_hA", [64, L], F32)
            hB = sb(nc, es, "f_hB", [64, L], F32)
            dhA, dhB = Dep(), Dep()
            emit_sin_layer(kb, nc, S, w1[:], 33, lambda tc: zT[:, tc * 512:(tc + 1) * 512], fq[:, 0:1], bf_[:, 0:1], hA, S["dw"], dhA)
            emit_sin_layer(kb, nc, S, w2[:], 64, lambda tc: hA[:, tc * 512:(tc + 1) * 512], fq[:, 1:2], bf_[:, 1:2], hB, dhA, dhB)
            emit_sin_layer(kb, nc, S, w3[:], 64, lambda tc: hB[:, tc * 512:(tc + 1) * 512], fq[:, 2:3], bf_[:, 2:3], hA, dhB, dhA)
            wob = sb(nc, es, "f_wob", [64, 1024], BF16)
            hAb = sb(nc, es, "f_hAb", [64, L], BF16)
            dhAb = Dep()
            kb.op("act", lambda e: e.copy(out=wob[:], in_=wo[:]), reads=[S["dw"]], writes=[S["dw"]])
            kb.op("act", lambda e: e.copy(out=hAb[:], in_=hA[:]), reads=[dhA], writes=[dhAb])
            win = sb(nc, es, "f_win", [128, L], F32)
            dwin = Dep()
            filt = [sb(nc, es, "f_filt%d" % i, [128, L], F32) for i in range(2)]
            dfilt = [Dep(), Dep()]
            fo = [sb(nc, es, "f_fo%d" % i, [128, L], BF16) for i in range(2)]
            dfo = [Dep(), Dep()]
            ssq = sb(nc, es, "f_ssq", [128, 4], F32)
            dss = Dep()
            Sf = I["S_f"]
            for o in range(2):
                for ct in range(2):
                    kb.op("dve", lambda e: e.memset(ssq[:], 0.0), writes=[dss])
                    for dr in range(2):
                        tix = (o * 2 + dr) * 2 + ct
                        kb.op("act", lambda e: e.activation(out=win[:], in_=trow[:], func=AF.Exp, scale=dec[:, tix:tix + 1]),
                              reads=[S["dw"]], writes=[dwin])
                        for tc in range(8):
                            pb = tc % 2
                            pp, dpp = S["ps"][pb], S["dps"][pb]
                            kb.op("pe", lambda e: e.matmul(pp[:], lhsT=wob[:, tix * 128:(tix + 1) * 128],
                                                           rhs=hAb[:, tc * 512:(tc + 1) * 512], start=True, stop=True),
                                  reads=[dhAb, S["dw"]], writes=[dpp])
                            kb.op("dve", lambda e: e.tensor_tensor(out=filt[dr][:, tc * 512:(tc + 1) * 512], in0=pp[:],
                                                                   in1=win[:, tc * 512:(tc + 1) * 512], op=ALU.mult),
                                  reads=[dpp, dwin], writes=[dfilt[dr]])
                        if dr == 1:
                            kb.op("dve", lambda e: e.memset(filt[1][:, 0:1], 0.0), writes=[dfilt[1]])
                        kb.op("act", lambda e: e.activation(out=win[:], in_=filt[dr][:], func=AF.Square,
                                                            accum_out=ssq[:, dr:dr + 1]),
                              reads=[dfilt[dr]], writes=[dwin, dss])
                    kb.op("dve", lambda e: e.tensor_tensor(out=ssq[:, 2:3], in0=ssq[:, 0:1], in1=ssq[:, 1:2], op=ALU.add),
                          reads=[dss], writes=[dss])
                    kb.op("act", lambda e: e.activation(out=ssq[:, 2:3], in_=ssq[:, 2:3], func=AF.Sqrt), reads=[dss], writes=[dss])
                    kb.op("dve", lambda e: e.reciprocal(out=ssq[:, 3:4], in_=ssq[:, 2:3]), reads=[dss], writes=[dss])
                    kb.op("dve", lambda e: e.tensor_scalar(out=filt[1][:], in0=filt[1][:], scalar1=ssq[:, 3:4], scalar2=None,
                                                           op0=ALU.mult), reads=[dfilt[1], dss], writes=[dfilt[1]])
                    for dr, op1 in ((0, ALU.add), (1, ALU.subtract)):
                        kb.op("dve", lambda e, dr=dr, op1=op1: e.scalar_tensor_tensor(out=fo[dr][:], in0=filt[0][:], scalar=ssq[:, 3:4],
                                                                                  in1=filt[1][:], op0=ALU.mult, op1=op1),
                              reads=[dfilt[0], dfilt[1], dss], writes=[dfo[dr]])
                        row = (o * 2 + dr) * 256 + ct * 128
                        kb.dma("sp", Sf[row:row + 128, :], fo[dr][:], reads=[dfo[dr]], wacc=[d_sf])
            kb.drain_scope([S["dw"], dhA, dhB, dhAb, dwin, dss] + S["darg"] + dfilt + dfo)

        Kh1 = sb(nc, es0, "h_K", [128, 32, 2, 256], BF16)
        dK1 = Dep()
        Kh = [Kh1, Kh1]
        dK = [dK1, dK1]
        NFC = 256 // HALF

        def run_filter(o):
            with ExitStack() as es:
                Xs = [sb(nc, es, "hx%d_%d" % (i, o), [32, CH, 128], BF16) for i in range(2)]
                dXs = [Dep(), Dep()]
                As = [sb(nc, es, "ha%d_%d" % (i, o), [128, CH, 64], BF16) for i in range(2)]
                dAs = [Dep(), Dep()]
                Sf = I["S_f"]
                W = HC["f2"]
                for ci in range(8):
                    kind, cc = ci // 4, ci % 4
                    xb = ci % 2
                    X, dX, A, dA = Xs[xb], dXs[xb], As[xb], dAs[xb]
                    r0 = (o * 2 + kind) * 256 + cc * CH
                    kb.dma("sp", X[:], Sf[r0:r0 + CH, :].rearrange("c (t1 t2) -> t1 c t2", t2=128), reads=[d_sf], writes=[dX])
                    for j0 in range(0, CH, 8):
                        pb = (j0 // 8) % 2
                        psA, dpsA = HC["psA"][pb], HC["dpsA"][pb]
                        for j in range(8):
                            kb.op("pe", lambda e, j=j: e.matmul(psA[:, j * 64:(j + 1) * 64], lhsT=X[:, j0 + j, :], rhs=HC["d1"][:],
                                                                start=True, stop=True),
                                  reads=[dX, HC["dconst"]], writes=[dpsA], inc=(j == 7))
                        if pb == 0:
                            kb.op("act", lambda e: e.copy(out=A[:, j0:j0 + 8, :], in_=psA[:].rearrange("p (j f) -> p j f", f=64)),
                                  reads=[dpsA], writes=[dA])
                        else:
                            kb.op("dve", lambda e: e.tensor_copy(out=A[:, j0:j0 + 8, :], in_=psA[:].rearrange("p (j f) -> p j f", f=64)),
                                  reads=[dpsA], writes=[dA])
                    for q in range(4):
                        pb = q % 2
                        psU, dpsU = HC["psU"][pb], HC["dpsU"][pb]
                        U8 = psU[:, 0:8 * CH].rearrange("p (a c) -> p a c", a=8)
                        for a in range(8):
                            f1 = 8 * q + a
                            Ar = A[:, :, f1]
                            Ai = A[:, :, 32 + f1]
                            m0, m1 = (0, 2) if kind == 0 else (1, 0)
                            kb.op("pe", lambda e: e.matmul(U8[:, a, :], lhsT=W[:, f1, m0, :], rhs=Ar, start=True, stop=False),
                                  reads=[dA, HC["dconst"]], writes=[dpsU], inc=False)
                            kb.op("pe", lambda e: e.matmul(U8[:, a, :], lhsT=W[:, f1, m1, :], rhs=Ai, start=False, stop=True),
                                  reads=[dA], writes=[dpsU], inc=(a == 7))
                        dst = Kh1[:, 8 * q:8 * q + 8, kind, cc * CH:(cc + 1) * CH]
                        if q % 2 == 0:
                            kb.op("act", lambda e: e.copy(out=dst, in_=U8), reads=[dpsU], writes=[dK1])
                        else:
                            kb.op("dve", lambda e: e.tensor_copy(out=dst, in_=U8), reads=[dpsU], writes=[dK1])
                kb.drain_scope(dXs + dAs)

        Sz, Sy = I["S_z"], I["S_y"]
        d_sz = Dep()

        def shortconv(es, part, ct, hin_rows, name):
            u32 = sb(nc, es, "u_32" + name, [128, L + 2], F32)
            acc = sb(nc, es, "u_acc" + name, [128, L], F32)
            du, du32, dacc = Dep(), Dep(), Dep()
            kb.op("pool", lambda e: e.memset(u32[:, 0:1], 0.0), writes=[du32])
            kb.op("pool", lambda e: e.memset(u32[:, L + 1:L + 2], 0.0), writes=[du32])
            I["load_u32"](es, u32[:, 1:L + 1], hin_rows[0], hin_rows[1], du, du32)
            ix = part * 2 + ct
            kb.op("act", lambda e: e.activation(out=acc[:], in_=u32[:, 1:L + 1], func=AF.Identity, scale=convw[:, ix, 1:2],
                                                bias=convb[:, ix:ix + 1]),
                  reads=[du32, HC["dconst"]], writes=[dacc])
            kb.op("dve", lambda e: e.scalar_tensor_tensor(out=acc[:], in0=u32[:, 0:L], scalar=convw[:, ix, 0:1],
                                                          in1=acc[:], op0=ALU.mult, op1=ALU.add),
                  reads=[du32, dacc], writes=[dacc])
            kb.op("dve", lambda e: e.scalar_tensor_tensor(out=acc[:], in0=u32[:, 2:L + 2], scalar=convw[:, ix, 2:3],
                                                          in1=acc[:], op0=ALU.mult, op1=ALU.add),
                  reads=[du32, dacc], writes=[dacc])
            return acc, dacc, [du, du32, dacc]

        for ct in range(2):
            with ExitStack() as es:
                acc, dacc, dl = shortconv(es, 2, ct, (2, ct), "v")
                kb.dma("sp", Sz[ct * 128:(ct + 1) * 128, :], acc[:], reads=[dacc], wacc=[d_sz])
                kb.drain_scope(dl + [d_sz])

        def run_conv(o):
            with ExitStack() as es:
                X0 = sb(nc, es, "cx_%d" % o, [32, CH, 128], BF16)
                dX0 = Dep()
                HC["X"] = [X0, X0]
                HC["dX"] = [dX0, dX0]
                HC["A"] = [sb(nc, es, "ca%d_%d" % (i, o), [128, CH, 64], BF16) for i in range(2)]
                HC["dA"] = [Dep(), Dep()]
                Y = [sb(nc, es, "cy%d_%d" % (i, o), [128, 64, CH], BF16) for i in range(2)]
                dY = [Dep(), Dep()]
                HC["Z"] = sb(nc, es, "cz_%d" % o, [64, CH, 128], BF16)
                Y2 = [sb(nc, es, "cy2%d_%d" % (i, o), [128, 64, CH], BF16) for i in range(2)]
                dY2 = [Dep(), Dep()]
                HC["dZ"] = Dep()
                HC["ysb"] = sb(nc, es, "cys_%d" % o, [32, CH, 128], F32)
                HC["dys"] = Dep()
                m_all = [sb(nc, es, "cm%d_%d" % (i, o), [128, 4, CH], F32) for i in range(8)]
                dm_all = [Dep() for _ in range(8)]

                def dsrc(ci):
                    return [(Sz[ci * CH:(ci + 1) * CH, :], 0, CH)]

                def dcons(ci, q, U4, dpsU):
                    yb = ci % 2
                    m = m_all[(q % 2) * 4:(q % 2) * 4 + 4]
                    dm = dm_all[(q % 2) * 4:(q % 2) * 4 + 4]
                    Kr = Kh[o][:, 4 * q:4 * q + 4, 0, ci * CH:(ci + 1) * CH]
                    Ki = Kh[o][:, 4 * q:4 * q + 4, 1, ci * CH:(ci + 1) * CH]
                    Ur, Ui = U4[:, :, 0, :], U4[:, :, 1, :]
                    for i, (a_, b_) in enumerate(((Ur, Kr), (Ui, Ki), (Ur, Ki), (Ui, Kr))):
                        kb.op("dve", lambda e, i=i, a_=a_, b_=b_: e.tensor_tensor(out=m[i][:], in0=a_, in1=b_, op=ALU.mult),
                              reads=[dpsU, dK[o]], writes=[dm[i]])
                    yr = Y[yb][:, 4 * q:4 * q + 4, :]
                    yi = Y[yb][:, 32 + 4 * q:32 + 4 * q + 4, :]
                    kb.op("pool", lambda e: e.tensor_tensor(out=yr, in0=m[0][:], in1=m[1][:], op=ALU.subtract),
                          reads=[dm[0], dm[1]], writes=[dY[yb]])
                    kb.op("pool", lambda e: e.tensor_tensor(out=yi, in0=m[2][:], in1=m[3][:], op=ALU.add),
                          reads=[dm[2], dm[3]], writes=[dY[yb]])
                    y2a = Y2[yb][:, 4 * q:4 * q + 4, :]
                    y2b = Y2[yb][:, 32 + 4 * q:32 + 4 * q + 4, :]
                    kb.op("dve", lambda e: e.scalar_tensor_tensor(out=y2a, in0=m[2][:], scalar=-1.0, in1=m[3][:],
                                                                  op0=ALU.mult, op1=ALU.subtract),
                          reads=[dm[2], dm[3]], writes=[dY2[yb]])
                    kb.op("act", lambda e: e.copy(out=y2b, in_=yr), reads=[dY[yb]], writes=[dY2[yb]])
                    if q == 7:
                        if pending_inv:
                            pending_inv.pop(0)()
                        pending_inv.append(lambda ci=ci, yb=yb: emit_fft_inv(kb, HC, ci, Y[yb], dY[yb], Sy[ci * CH:(ci + 1) * CH, :],
                                                                             Y2[yb], dY2[yb]))

                pending_inv = []
                emit_fft_fwd(kb, HC, dsrc, 256 // CH, dcons, True, d_sz)
                while pending_inv:
                    pending_inv.pop(0)()
                kb.drain_scope(HC["dX"] + HC["dA"] + dY + dY2 + [HC["dZ"], HC["dys"], HC["d_sy"]] + dm_all)

        run_filter(0)
        run_conv(0)
        for step, part in ((0, 0), (1, 1)):
            for ct in range(2):
                with ExitStack() as es:
                    yt = sb(nc, es, "g_y%d" % step, [128, L], F32)
                    zt = sb(nc, es, "g_z%d" % step, [128, L], F32)
                    dyt, dzt = Dep(), Dep()
                    kb.dma("sp", yt[:], Sy[ct * 128:(ct + 1) * 128, :], reads=[HC["d_sy"]], writes=[dyt])
                    kb.dma("sp", zt[:], Sz[ct * 128:(ct + 1) * 128, :], reads=[d_sz], writes=[dzt])
                    kb.op("dve", lambda e: e.scalar_tensor_tensor(out=yt[:], in0=zt[:], scalar=hbias[:, step * 2 + ct:step * 2 + ct + 1],
                                                                  in1=yt[:], op0=ALU.mult, op1=ALU.add),
                          reads=[dzt, dyt, HC["dconst"]], writes=[dyt])
                    acc, dacc, dl = shortconv(es, part, ct, (part, ct), "g%d" % step)
                    if step == 0:
                        kb.op("dve", lambda e: e.tensor_tensor(out=zt[:], in0=acc[:], in1=yt[:], op=ALU.mult),
                              reads=[dacc, dyt], writes=[dzt])
                        kb.dma("sp", Sz[ct * 128:(ct + 1) * 128, :], zt[:], reads=[dzt], wacc=[d_sz])
                        kb.drain_scope(dl + [dyt, dzt, d_sz])
                    else:
                        ob = sb(nc, es, "g_ob", [128, L], BF16)
                        dob = Dep()
                        kb.op("dve", lambda e: e.tensor_tensor(out=ob[:], in0=acc[:], in1=yt[:], op=ALU.mult),
                              reads=[dacc, dyt], writes=[dob])
                        kb.dma("sp", I["hout"][ct * 128:(ct + 1) * 128, :], ob[:], reads=[dob], wacc=[I["d_hout"]])
                        kb.drain_scope(dl + [dyt, dzt, dob, I["d_hout"]])
            if step == 0:
                run_filter(1)
                run_conv(1)
        kb.drain_scope([HC["dconst"]] + dK)


def emit_proj(kb, C, lhsT_of, d_w, nk, msz, rhs_of, d_rhs, ntok, consume):
    for tt in range(ntok // 512):
        pb = C["rr"] % 2
        C["rr"] += 1
        pp, dpp = C["ps_" + "ab"[pb]], C["d_ps_" + "ab"[pb]]
        for k in range(nk):
            kb.op("pe", lambda e, k=k: e.matmul(pp[0:msz, :], lhsT=lhsT_of(k), rhs=rhs_of(k, tt),
                                                start=(k == 0), stop=(k == nk - 1)),
                  reads=[d_w, d_rhs], writes=[dpp], inc=(k == nk - 1))
        consume(tt, pp, dpp)


def load_x(kb, xT, dxs, src, d_src=None, per_kq=None):
    xv = src.rearrange("(c p) t -> p c t", p=128)
    rd = [d_src] if d_src is not None else []
    for k in range(8):
        for q in range(4):
            tgt = [dxs[q]] if per_kq is None else [per_kq[k][q]]
            kb.dma("sp", xT[:, k, q * 512:(q + 1) * 512], xv[:, k, q * 512:(q + 1) * 512], reads=rd, wacc=tgt)


def store_x(kb, xT, dxs, dst, d_out):
    ov = dst.rearrange("(c p) t -> p c t", p=128)
    for k in range(8):
        kb.dma("sp", ov[:, k, :], xT[:, k, :], reads=dxs, wacc=[d_out])


def declare(nc, specs, kind):
    out = {}
    for name, (shape, dt) in specs.items():
        out[name] = nc.dram_tensor(name, list(shape), dt, kind=kind).ap()
    return out


FFN_W = lambda pre: {pre + "wg": ([11, 128, 8, 256], F32), pre + "wu": ([11, 128, 8, 256], F32),
                     pre + "wd": ([2, 4, 128, 12, 256], F32)}


P1_IN = {"xT_in": ([1024, TOK], F32), "gains": ([128, 16], F32), "w_in": ([8, 128, 8, 256], F32), **FFN_W("f00_")}
P1_OUT = {"xT_out": ([1024, TOK], F32), "projT": ([2048, TOK], BF16)}


def emit_phase1(kb, nc, C, I, O):
    d_out = Dep()
    with ExitStack() as es:
        xT = sb(nc, es, "xT", [128, 8, TOK], F32)
        gt = sb(nc, es, "gt", [128, 16], F32)
        dxs = [Dep() for _ in range(4)]
        kb.dma("sp", gt[:], I["gains"][:, :], writes=[C["d_const"]])
        load_x(kb, xT, dxs, I["xT_in"])
        emit_ffn(kb, C, xT, dxs, gt[:, 0:8], I["f00_wg"], I["f00_wu"], I["f00_wd"], "a")
        store_x(kb, xT, dxs, O["xT_out"], d_out)
        with ExitStack() as es2:
            hT = sb(nc, es2, "hT1", [128, 8, TOK], BF16)
            d_h = Dep()
            emit_rmsnorm(kb, C, xT, dxs, gt[:, 8:16], hT, d_h, 0, TOK)
            wb = [sb(nc, es2, "win%d" % i, [128, 8, 256], BF16) for i in range(2)]
            dwb = [Dep(), Dep()]
            ob = [sb(nc, es2, "pob%d" % i, [128, TOK], BF16) for i in range(2)]
            dob = [Dep(), Dep()]
            kb.dma("pool", wb[0][:], I["w_in"][0], writes=[dwb[0]])
            for g in range(8):
                if g + 1 < 8:
                    kb.dma("pool", wb[(g + 1) % 2][:], I["w_in"][g + 1], writes=[dwb[(g + 1) % 2]])
                for j in range(2):
                    oi = (g * 2 + j) % 2
                    ch = g * 2 + j

                    def cons(tt, pp, dpp, oi=oi):
                        eng = "act" if tt % 2 == 0 else "dve"
                        if eng == "act":
                            kb.op("act", lambda e: e.copy(out=ob[oi][:, tt * 512:(tt + 1) * 512], in_=pp[:]), reads=[dpp], writes=[dob[oi]])
                        else:
                            kb.op("dve", lambda e: e.tensor_copy(out=ob[oi][:, tt * 512:(tt + 1) * 512], in_=pp[:]), reads=[dpp], writes=[dob[oi]])

                    emit_proj(kb, C, lambda k, g=g, j=j: wb[g % 2][:, k, j * 128:(j + 1) * 128], dwb[g % 2], 8, 128,
                              lambda k, tt: hT[:, k, tt * 512:(tt + 1) * 512], d_h, TOK, cons)
                    O["store_proj"](ch, ob[oi], dob[oi], d_out)
            kb.drain_scope([d_h] + dwb + dob)
        kb.drain_scope(dxs + [d_out])
    return d_out


P3_IN = {"xT_in": ([1024, TOK], F32), "gains": ([128, 32], F32), "apool": ([512, TOK + 32], BF16),
         "invcnt": ([4, TOK], F32), "yhyT": ([512, TOK], BF16), "pool_w": ([128, 4, 128], F32),
         "pool_scale": ([128, 4], F32), "w_mo": ([4, 128, 8, 256], F32),
         **FFN_W("f01_"), **FFN_W("f10_"),
         "w_dq": ([128, 8, 256], F32), "w_dkv": ([128, 8, 160], F32), "qkv_g": ([128, 3], F32),
         "rope_cs": ([32, 2, TOK], F32)}
P3_OUT = {"xT_out": ([1024, TOK], F32), "cqnT": ([256, TOK], BF16), "kvlatT": ([160, TOK], BF16)}


def emit_pool(kb, nc, C, I, yT, d_y):
    with ExitStack() as es:
        W = TOK + 32
        pw32 = sb(nc, es, "pw32", [128, 4, 128], F32)
        pw = sb(nc, es, "pw", [128, 4, 128], BF16)
        psc = sb(nc, es, "psc", [128, 4], F32)
        dpw = Dep()
        kb.dma("sp", pw32[:], I["pool_w"][:, :, :], writes=[dpw])
        kb.dma("sp", psc[:], I["pool_scale"][:, :], writes=[dpw])
        kb.op("act", lambda e: e.copy(out=pw[:], in_=pw32[:]), reads=[dpw], writes=[dpw])
        ab = sb(nc, es, "pl_ab", [128, W], BF16)
        A = sb(nc, es, "pl_A", [128, W], F32)
        S1 = sb(nc, es, "pl_S1", [128, W], F32)
        S2 = sb(nc, es, "pl_S2", [128, W], F32)
        inv = sb(nc, es, "pl_inv", [128, TOK], F32)
        pb = sb(nc, es, "pl_p", [128, TOK], BF16)
        dab, dA, dS1, dS2, dinv, dp = Dep(), Dep(), Dep(), Dep(), Dep(), Dep()
        for g in range(4):
            I["load_apool"](ab, g, dab)
            kb.dma("sp", inv[:], bcast_rows(I["invcnt"][g:g + 1, :], 128), writes=[dinv])
            kb.op("act", lambda e: e.copy(out=A[:], in_=ab[:]), reads=[dab], writes=[dA])
            if "halo_mask" in I:
                hm = I["halo_mask"]
                kb.op("dve", lambda e: e.tensor_scalar(out=A[:, 0:16], in0=A[:, 0:16], scalar1=hm[:, 0:1], scalar2=None, op0=ALU.mult),
                      reads=[dA, C["d_const"]], writes=[dA])
                kb.op("dve", lambda e: e.tensor_scalar(out=A[:, W - 16:W], in0=A[:, W - 16:W], scalar1=hm[:, 1:2], scalar2=None, op0=ALU.mult),
                      reads=[dA, C["d_const"]], writes=[dA])
            kb.op("dve", lambda e: e.tensor_tensor(out=S1[:, 1:W], in0=A[:, 0:W - 1], in1=A[:, 1:W], op=ALU.add),
                  reads=[dA], writes=[dS1])
            cur, dcur, oth, doth = S1, dS1, S2, dS2
            lo, hi = 1, W
            for lvl in range(1, g + 1):
                sh = 1 << (lvl - 1)
                nlo, nhi = lo + sh, hi - sh
                kb.op("dve", lambda e, cur=cur, oth=oth, sh=sh, nlo=nlo, nhi=nhi: e.tensor_tensor(
                    out=oth[:, nlo:nhi], in0=cur[:, nlo - sh:nhi - sh], in1=cur[:, nlo + sh:nhi + sh], op=ALU.add),
                    reads=[dcur], writes=[doth])
                cur, dcur, oth, doth = oth, doth, cur, dcur
                lo, hi = nlo, nhi
            assert lo <= 16 and hi >= 16 + TOK
            kb.op("dve", lambda e, cur=cur: e.tensor_tensor(out=cur[:, 16:16 + TOK], in0=cur[:, 16:16 + TOK], in1=inv[:], op=ALU.mult),
                  reads=[dcur, dinv], writes=[dcur])
            kb.op("dve", lambda e, cur=cur: e.tensor_tensor(out=pb[:], in0=cur[:, 16:16 + TOK], in1=A[:, 16:16 + TOK], op=ALU.subtract),
                  reads=[dcur, dA], writes=[dp])

            def cons(tt, pp, dpp, g=g):
                kb.op("act", lambda e: e.activation(out=yT[:, g, tt * 512:(tt + 1) * 512], in_=pp[:], func=AF.Copy,
                                                    scale=psc[:, g:g + 1]), reads=[dpp, dpw], writes=[d_y])

            emit_proj(kb, C, lambda k, g=g: pw[:, g, :], dpw, 1, 128, lambda k, tt: pb[:, tt * 512:(tt + 1) * 512], dp, TOK, cons)
        kb.drain_scope([dpw, dab, dA, dS1, dS2, dinv, dp])


def emit_phase3(kb, nc, C, I, O):
    d_out = Dep()
    with ExitStack() as es:
        xT = sb(nc, es, "xT3", [128, 8, TOK], F32)
        gt = sb(nc, es, "gt3", [128, 32], F32)
        dxs = [Dep() for _ in range(4)]
        kb.dma("sp", gt[:], I["gains"][:, :], writes=[C["d_const"]])
        load_x(kb, xT, dxs, I["xT_in"], I.get("d_x"))
        with ExitStack() as es2:
            yT = sb(nc, es2, "yT3", [128, 8, TOK], BF16)
            d_y = Dep()
            for c in range(4):
                I["load_yhy"](yT[:, 4 + c, :], c, d_y)
            emit_pool(kb, nc, C, I, yT, d_y)
            wb = [sb(nc, es2, "wmo%d" % i, [128, 8, 256], BF16) for i in range(2)]
            dwb = [Dep(), Dep()]
            kb.dma("pool", wb[0][:], I["w_mo"][0], writes=[dwb[0]])
            for g in range(4):
                if g + 1 < 4:
                    kb.dma("pool", wb[(g + 1) % 2][:], I["w_mo"][g + 1], writes=[dwb[(g + 1) % 2]])
                for j in range(2):
                    dch = g * 2 + j

                    def cons(tt, pp, dpp, dch=dch):
                        kb.op("dve", lambda e: e.tensor_tensor(out=xT[:, dch, tt * 512:(tt + 1) * 512], in0=pp[:],
                                                               in1=xT[:, dch, tt * 512:(tt + 1) * 512], op=ALU.add),
                              reads=[dpp, dxs[tt]], writes=[dxs[tt]])

                    emit_proj(kb, C, lambda k, g=g, j=j: wb[g % 2][:, k, j * 128:(j + 1) * 128], dwb[g % 2], 8, 128,
                              lambda k, tt: yT[:, k, tt * 512:(tt + 1) * 512], d_y, TOK, cons)
            kb.drain_scope([d_y] + dwb)
        if "after_mixer" in I:
            I["after_mixer"]()
        emit_ffn(kb, C, xT, dxs, gt[:, 0:8], I["f01_wg"], I["f01_wu"], I["f01_wd"], "b")
        emit_ffn(kb, C, xT, dxs, gt[:, 8:16], I["f10_wg"], I["f10_wu"], I["f10_wd"], "c")
        store_x(kb, xT, dxs, O["xT_out"], d_out)
        with ExitStack() as es2:
            hT = sb(nc, es2, "hT3", [128, 8, TOK], BF16)
            d_h = Dep()
            emit_rmsnorm(kb, C, xT, dxs, gt[:, 16:24], hT, d_h, 0, TOK)
            wdq = sb(nc, es2, "wdq", [128, 8, 256], BF16)
            wdkv = sb(nc, es2, "wdkv", [128, 8, 160], BF16)
            wrot = sb(nc, es2, "wrot", [128, 8, 32], BF16)
            qg = sb(nc, es2, "qg", [128, 3], F32)
            cs = sb(nc, es2, "ropecs", [32, 2, TOK], F32)
            dw = Dep()
            kb.dma("pool", wdq[:], I["w_dq"][:, :, :], wacc=[dw])
            kb.dma("pool", wdkv[:], I["w_dkv"][:, :, :], wacc=[dw])
            kb.dma("sp", qg[:], I["qkv_g"][:, :], wacc=[dw])
            kb.dma("sp", cs[:], I["rope_cs"][:, :, :], wacc=[dw])
            kb.op("act", lambda e: e.mul(out=wrot[:, :, 0:16], in_=wdkv[:, :, 144:160], mul=-1.0), reads=[dw], writes=[dw])
            kb.op("act", lambda e: e.copy(out=wrot[:, :, 16:32], in_=wdkv[:, :, 128:144]), reads=[dw], writes=[dw])
            cqf = sb(nc, es2, "cqf", [128, 2, TOK], F32)
            dcq = [Dep() for _ in range(4)]
            for c in range(2):
                def cons(tt, pp, dpp, c=c):
                    kb.op("act", lambda e: e.copy(out=cqf[:, c, tt * 512:(tt + 1) * 512], in_=pp[:]), reads=[dpp], writes=[dcq[tt]])
                emit_proj(kb, C, lambda k, c=c: wdq[:, k, c * 128:(c + 1) * 128], dw, 8, 128,
                          lambda k, tt: hT[:, k, tt * 512:(tt + 1) * 512], d_h, TOK, cons)
            cqn = sb(nc, es2, "cqn", [128, 2, TOK], BF16)
            dcqn = Dep()
            emit_rmsnorm(kb, C, cqf, dcq, qg[:, 0:2], cqn, dcqn, 0, TOK, nch=2, ones=C["ones256"])
            for c in range(2):
                kb.dma("sp", O["cqnT"][c * 128:(c + 1) * 128, :], cqn[:, c, :], reads=[dcqn], wacc=[d_out])
            ckf = sb(nc, es2, "ckf", [128, 1, TOK], F32)
            dck = [Dep() for _ in range(4)]

            def cons_k(tt, pp, dpp):
                kb.op("act", lambda e: e.copy(out=ckf[:, 0, tt * 512:(tt + 1) * 512], in_=pp[:]), reads=[dpp], writes=[dck[tt]])
            emit_proj(kb, C, lambda k: wdkv[:, k, 0:128], dw, 8, 128, lambda k, tt: hT[:, k, tt * 512:(tt + 1) * 512], d_h, TOK, cons_k)
            ckn = sb(nc, es2, "ckn", [128, 1, TOK], BF16)
            dckn = Dep()
            emit_rmsnorm(kb, C, ckf, dck, qg[:, 2:3], ckn, dckn, 0, TOK, nch=1, ones=C["ones128"])
            kb.dma("sp", O["kvlatT"][0:128, :], ckn[:, 0, :], reads=[dckn], wacc=[d_out])
            kr = sb(nc, es2, "krope", [32, TOK], BF16)
            ta = sb(nc, es2, "kr_ta", [32, TOK], F32)
            tb = sb(nc, es2, "kr_tb", [32, 512], F32)
            dta, dtb, dkr = Dep(), Dep(), Dep()

            def cons_a(tt, pp, dpp):
                kb.op("dve", lambda e: e.tensor_tensor(out=ta[:, tt * 512:(tt + 1) * 512], in0=pp[0:32, :],
                                                       in1=cs[:, 0, tt * 512:(tt + 1) * 512], op=ALU.mult),
                      reads=[dpp, dw], writes=[dta])
            emit_proj(kb, C, lambda k: wdkv[:, k, 128:160], dw, 8, 32, lambda k, tt: hT[:, k, tt * 512:(tt + 1) * 512], d_h, TOK, cons_a)

            def cons_b(tt, pp, dpp):
                kb.op("dve", lambda e: e.tensor_tensor(out=tb[:], in0=pp[0:32, :], in1=cs[:, 1, tt * 512:(tt + 1) * 512], op=ALU.mult),
                      reads=[dpp, dw], writes=[dtb])
                kb.op("dve", lambda e: e.tensor_tensor(out=kr[:, tt * 512:(tt + 1) * 512], in0=ta[:, tt * 512:(tt + 1) * 512],
                                                       in1=tb[:], op=ALU.add), reads=[dta, dtb], writes=[dkr])
            emit_proj(kb, C, lambda k: wrot[:, k, :], dw, 8, 32, lambda k, tt: hT[:, k, tt * 512:(tt + 1) * 512], d_h, TOK, cons_b)
            kb.dma("sp", O["kvlatT"][128:160, :], kr[:], reads=[dkr], wacc=[d_out])
            kb.drain_scope([d_h, dw, dcqn, dckn, dta, dtb, dkr] + dcq + dck)
        kb.drain_scope(dxs + [d_out])
    return d_out


NH = 16
P4_IN = {"xT_in": ([1024, TOK], F32), "gains": ([128, 16], F32), "cqnT": ([256, TOK], BF16),
         "kvlat_full": ([2, 160, TOK], BF16), "w_uq": ([128, 2, 1536], F32), "w_ukv": ([128, 2048], F32),
         "w_o": ([128, 8, 1024], F32), "rope_q": ([96, 2, TOK], F32), **FFN_W("f11_")}
P4_OUT = {"outT": ([1024, TOK], F32)}


def emit_phase4(kb, nc, C, I, O):
    d_out = Dep()
    SC = 96 ** -0.5
    with ExitStack() as es, ExitStack() as esO:
        OT = sb(nc, esO, "OT", [128, NH // 2, TOK], BF16, side="right")
        dOT = Dep()
        with ExitStack() as es2:
            dw = Dep()
            wuq = sb(nc, es2, "wuq", [128, 2, 1536], BF16)
            wukv = sb(nc, es2, "wukv", [128, 2048], BF16)
            kb.dma("pool", wuq[:], I["w_uq"][:, :, :], wacc=[dw])
            kb.dma("pool", wukv[:], I["w_ukv"][:, :], wacc=[dw])
            wqr = sb(nc, es2, "wqr", [128, 2, NH, 96], BF16)
            kb.op("pool", lambda e: e.memset(wqr[:], 0.0), writes=[dw])
            wv = wuq[:].rearrange("p k (h c) -> p k h c", c=96)
            kb.op("act", lambda e: e.mul(out=wqr[:, :, :, 64:80], in_=wv[:, :, :, 80:96], mul=-1.0), reads=[dw], writes=[dw])
            kb.op("act", lambda e: e.copy(out=wqr[:, :, :, 80:96], in_=wv[:, :, :, 64:80]), reads=[dw], writes=[dw])
            cqn = sb(nc, es2, "cqn4", [128, 2, TOK], BF16)
            ckn = sb(nc, es2, "ckn4", [128, L], BF16)
            krp = sb(nc, es2, "krp4", [96, L], BF16)
            rq = sb(nc, es2, "ropeq", [96, 2, TOK], F32)
            dl = Dep()
            for c in range(2):
                kb.dma("sp", cqn[:, c, :], I["cqnT"][c * 128:(c + 1) * 128, :], wacc=[dl])
            for rk in range(2):
                kb.dma("sp", ckn[:, rk * TOK:(rk + 1) * TOK], I["kvlat_full"][rk, 0:128, :], reads=[I["d_kv"]], wacc=[dl])
                kb.dma("sp", krp[64:96, rk * TOK:(rk + 1) * TOK], I["kvlat_full"][rk, 128:160, :], reads=[I["d_kv"]], wacc=[dl])
            kb.dma("sp", rq[64:96, :, :], I["rope_q"][64:96, :, :], wacc=[dl])
            onesf = sb(nc, es2, "onesf", [96, 64], F32)
            kb.op("pool", lambda e: e.memset(onesf[:], 1.0), writes=[dl])
            Vt = sb(nc, es2, "Vt", [128, 32, 8, 65], BF16)
            KT = [sb(nc, es2, "KT%d" % i, [96, L], BF16) for i in range(2)]
            dKT = [Dep(), Dep()]
            QT = [sb(nc, es2, "QT%d" % i, [96, TOK], BF16) for i in range(2)]
            dQT = [Dep(), Dep()]
            PT = [sb(nc, es2, "PT%d" % i, [128, 1024], BF16) for i in range(3)]
            dPT = [Dep(), Dep(), Dep()]
            qa = sb(nc, es2, "qa", [96, 512], F32)
            qb = sb(nc, es2, "qb", [96, 512], F32)
            dqa, dqb = Dep(), Dep()
            stg = [sb(nc, es2, "ostg%d" % i, [64, 512], BF16) for i in range(2)]
            dstg = [Dep(), Dep()]
            stg_rr = [0]
            den = sb(nc, es2, "den", [96, 512], F32)
            dhi = sb(nc, es2, "den_hi", [96, 512], BF16)
            dlo = sb(nc, es2, "den_lo", [96, 512], BF16)
            onesb16 = sb(nc, es2, "onesb16", [96, 64], BF16)
            kb.op("pool", lambda e: e.memset(onesb16[:], 1.0), writes=[dl])
            rec = sb(nc, es2, "rec", [64, 512], F32)
            dden, drec = Dep(), Dep()
            psS = [(C["ps_a"], C["ps_b"]), (C["ps_c"], C["ps_d"])]
            psS2 = [C["ps_ab"], C["ps_cd"]]
            dpsS = [Dep(), Dep()]
            psO = [C["ps_e"], C["ps_f"]]
            dpsO = [C["d_ps_e"], C["d_ps_f"]]
            psB, dpsB = C["ps_g"], C["d_ps_g"]
            psP, dpsP = C["ps_h"], C["d_ps_h"]
            sit = 0
            oit = 0
            pit = 0
            pending_norm = []

            dVs = [Dep() for _ in range(8)]

            bank_rr = [0]
            step_banks = [(C["ps_h"], C["d_ps_h"]), (C["ps_g"], C["d_ps_g"])]

            def next_bank():
                bank_rr[0] += 1
                return step_banks[bank_rr[0] % 2]

            def v_steps(h):
                hl_ = h % 8
                for k0 in range(0, 32, 8):
                    psP, dpsP = next_bank()
                    for j in range(8):
                        kt = k0 + j
                        kb.op("pe", lambda e, j=j, kt=kt: e.matmul(psP[:, j * 64:(j + 1) * 64], lhsT=ckn[:, kt * 128:(kt + 1) * 128],
                                                                   rhs=wukv[:, h * 128 + 64:h * 128 + 128], start=True, stop=True),
                              reads=[dl, dw], writes=[dpsP], inc=(j == 7))
                    kb.op("dve", lambda e: e.tensor_copy(out=Vt[:, k0:k0 + 8, hl_, 0:64], in_=psP[:].rearrange("p (j c) -> p j c", c=64)),
                          reads=[dpsP], writes=[dVs[hl_]])
                    yield

            def setup_steps(h):
                kbuf = h % 2
                K_, dK_ = KT[kbuf], dKT[kbuf]
                Q_, dQ_ = QT[kbuf], dQT[kbuf]
                kb.op("dve", lambda e: e.tensor_copy(out=K_[64:96, :], in_=krp[64:96, :]), reads=[dl], writes=[dK_])
                for kc in range(8):
                    psP, dpsP = next_bank()
                    kb.op("pe", lambda e: e.matmul(psP[0:64, :], lhsT=wukv[:, h * 128:h * 128 + 64],
                                                   rhs=ckn[:, kc * 512:(kc + 1) * 512], start=True, stop=True),
                          reads=[dl, dw], writes=[dpsP])
                    kb.op("dve", lambda e: e.tensor_copy(out=K_[0:64, kc * 512:(kc + 1) * 512], in_=psP[0:64, :]), reads=[dpsP], writes=[dK_])
                    yield
                for qt in range(4):
                    psP, dpsP = next_bank()
                    for kk in range(2):
                        kb.op("pe", lambda e: e.matmul(psP[0:96, :], lhsT=wuq[:, kk, h * 96:(h + 1) * 96],
                                                       rhs=cqn[:, kk, qt * 512:(qt + 1) * 512], start=(kk == 0), stop=(kk == 1)),
                              reads=[dl, dw], writes=[dpsP], inc=(kk == 1))
                    kb.op("dve", lambda e: e.tensor_copy(out=Q_[0:64, qt * 512:(qt + 1) * 512], in_=psP[0:64, :]), reads=[dpsP], writes=[dQ_])
                    kb.op("dve", lambda e: e.tensor_tensor(out=qa[64:96, :], in0=psP[64:96, :], in1=rq[64:96, 0, qt * 512:(qt + 1) * 512],
                                                           op=ALU.mult), reads=[dpsP, dl], writes=[dqa])
                    yield
                    psP, dpsP = next_bank()
                    for kk in range(2):
                        kb.op("pe", lambda e: e.matmul(psP[0:96, :], lhsT=wqr[:, kk, h, :],
                                                       rhs=cqn[:, kk, qt * 512:(qt + 1) * 512], start=(kk == 0), stop=(kk == 1)),
                              reads=[dl, dw], writes=[dpsP], inc=(kk == 1))
                    kb.op("dve", lambda e: e.tensor_tensor(out=qb[64:96, :], in0=psP[64:96, :], in1=rq[64:96, 1, qt * 512:(qt + 1) * 512],
                                                           op=ALU.mult), reads=[dpsP, dl], writes=[dqb])
                    kb.op("dve", lambda e: e.tensor_tensor(out=Q_[64:96, qt * 512:(qt + 1) * 512], in0=qa[64:96, :], in1=qb[64:96, :],
                                                           op=ALU.add), reads=[dqa, dqb], writes=[dQ_])
                    yield

            steps = []

            def run_step():
                while steps:
                    try:
                        next(steps[0])
                        return
                    except StopIteration:
                        steps.pop(0)

            def flush_steps():
                while steps:
                    run_step()

            def emit_norm(h, qt, ob):
                kb.op("dve", lambda e: e.tensor_copy(out=dhi[64:65, :], in_=psO[ob][64:65, :]), reads=[dpsO[ob]], writes=[dden])
                kb.op("dve", lambda e: e.tensor_tensor(out=dlo[64:65, :], in0=psO[ob][64:65, :], in1=dhi[64:65, :], op=ALU.subtract),
                      reads=[dpsO[ob], dden], writes=[dden])
                kb.op("pe", lambda e: e.matmul(psB[0:64, :], lhsT=onesb16[64:65, :], rhs=dhi[64:65, :], start=True, stop=False),
                      reads=[dden, dl], writes=[dpsB], inc=False)
                kb.op("pe", lambda e: e.matmul(psB[0:64, :], lhsT=onesb16[64:65, :], rhs=dlo[64:65, :], start=False, stop=True),
                      reads=[dden, dl], writes=[dpsB])
                kb.op("dve", lambda e: e.reciprocal(out=rec[:], in_=psB[0:64, :]), reads=[dpsB], writes=[drec])
                if h % 2 == 0:
                    kb.op("dve", lambda e: e.tensor_tensor(out=OT[0:64, h // 2, qt * 512:(qt + 1) * 512], in0=psO[ob][0:64, :], in1=rec[:],
                                                           op=ALU.mult), reads=[dpsO[ob], drec], wacc=[dOT])
                else:
                    sg_i = stg_rr[0] % 2
                    stg_rr[0] += 1
                    kb.op("dve", lambda e: e.tensor_tensor(out=stg[sg_i][:], in0=psO[ob][0:64, :], in1=rec[:], op=ALU.mult),
                          reads=[dpsO[ob], drec], writes=[dstg[sg_i]])
                    kb.dma("sp", OT[64:128, h // 2, qt * 512:(qt + 1) * 512], stg[sg_i][:], reads=[dstg[sg_i]], wacc=[dOT])

            kb.op("pool", lambda e: e.memset(Vt[:, :, :, 64:65], 1.0), writes=dVs)
            steps.append(v_steps(0))
            steps.append(setup_steps(0))
            flush_steps()
            for h in range(NH):
                hg, hl = h // 8, h % 8
                kbuf = h % 2
                if h == 0:
                    steps.append(v_steps(1))
                    steps.append(setup_steps(1))
                    for h0 in range(2, 8):
                        steps.append(v_steps(h0))
                else:
                    if h + 1 < NH:
                        steps.append(setup_steps(h + 1))
                    if h + 7 < NH:
                        steps.append(v_steps(h + 7))
                K_, dK_ = KT[kbuf], dKT[kbuf]
                Q_, dQ_ = QT[kbuf], dQT[kbuf]
                for qt in range(4):
                    ob = oit % 2
                    oit += 1

                    def emit_S(kp):
                        nonlocal sit
                        sbuf_i = sit % 2
                        sit += 1
                        pA, pB = psS[sbuf_i]
                        for i, pp in enumerate((pA, pB)):
                            kt = kp * 2 + i
                            kb.op("pe", lambda e, pp=pp, kt=kt: e.matmul(pp[:], lhsT=K_[:, kt * 128:(kt + 1) * 128],
                                                                         rhs=Q_[:, qt * 512:(qt + 1) * 512], start=True, stop=True),
                                  reads=[dK_, dQ_], writes=[dpsS[sbuf_i]], inc=(i == 1))
                        return sbuf_i

                    def emit_exp(sbuf_i):
                        nonlocal pit
                        pA, pB = psS[sbuf_i]
                        pt = pit % 3
                        pit += 1
                        kb.op("act", lambda e: e.activation(out=PT[pt][:, :], in_=psS2[sbuf_i][:, :], func=AF.Exp, scale=SC),
                              reads=[dpsS[sbuf_i]], writes=[dPT[pt]])
                        return pt

                    def emit_PV(kp, pt):
                        for i in range(2):
                            kt = kp * 2 + i
                            kb.op("pe", lambda e, i=i, kt=kt: e.matmul(psO[ob][0:65, :], lhsT=Vt[:, kt, hl, :],
                                                                       rhs=PT[pt][:, i * 512:(i + 1) * 512],
                                                                       start=(kt == 0), stop=(kt == 31)),
                                  reads=[dVs[hl], dPT[pt]], writes=[dpsO[ob]], inc=(kt == 31 or i == 1))

                    sb_ = {0: emit_S(0), 1: emit_S(1)}
                    for kp in range(16):
                        pt = emit_exp(sb_[kp])
                        if kp + 2 < 16:
                            sb_[kp + 2] = emit_S(kp + 2)
                        emit_PV(kp, pt)
                        if kp == 2 and pending_norm:
                            emit_norm(*pending_norm.pop())
                        elif kp >= 3:
                            run_step()
                    pending_norm.append((h, qt, ob))
                flush_steps()
            emit_norm(*pending_norm.pop())
            kb.drain_scope([dw, dl, dqa, dqb, dden, drec] + dstg + dVs + dKT + dQT + dPT + dpsS)
        xT = sb(nc, es, "xT4", [128, 8, TOK], F32)
        gt = sb(nc, es, "gt4", [128, 16], F32)
        dxs = [Dep() for _ in range(4)]
        kb.dma("sp", gt[:], I["gains"][:, :], writes=[C["d_const"]])
        dxk = [[Dep() for _ in range(4)] for _ in range(8)]
        load_x(kb, xT, dxs, I["xT_in"], I.get("d_x"), per_kq=dxk)
        with ExitStack() as es2:
            wo = sb(nc, es2, "wo4", [128, NH // 2, 1024], BF16)
            dwo = Dep()
            for hq in range(4):
                kb.dma("pool", wo[:, hq * 2:(hq + 1) * 2, :], I["w_o"][:, hq * 2:(hq + 1) * 2, :], wacc=[dwo])
            for dch in range(8):
                def cons(tt, pp, dpp, dch=dch):
                    kb.op("dve", lambda e: e.tensor_tensor(out=xT[:, dch, tt * 512:(tt + 1) * 512], in0=pp[:],
                                                           in1=xT[:, dch, tt * 512:(tt + 1) * 512], op=ALU.add),
                          reads=[dpp, dxk[dch][tt]], writes=[dxs[tt]])
                emit_proj(kb, C, lambda k, dch=dch: wo[:, k, dch * 128:(dch + 1) * 128], dwo, NH // 2, 128,
                          lambda k, tt: OT[:, k, tt * 512:(tt + 1) * 512], dOT, TOK, cons)
            kb.drain_scope([dwo, dOT])
        esO.close()
        emit_ffn(kb, C, xT, dxs, gt[:, 0:8], I["f11_wg"], I["f11_wu"], I["f11_wd"], "d")
        with ExitStack() as es2:
            oT = sb(nc, es2, "oT4", [128, 8, TOK], F32)
            d_o = Dep()
            emit_rmsnorm(kb, C, xT, dxs, gt[:, 8:16], oT, d_o, 0, TOK)
            ov = O["outT"].rearrange("(c p) t -> p c t", p=128)
            for k in range(8):
                kb.dma("sp", ov[:, k, :], oT[:, k, :], reads=[d_o], wacc=[d_out])
            kb.drain_scope([d_o, d_out])
        kb.drain_scope(dxs)
    return d_out


BFNP = ml_dtypes.bfloat16


def tile_kxm(w, gw=256):
    K, M = w.shape
    return np.ascontiguousarray(w.reshape(K // 128, 128, M // gw, gw).transpose(2, 1, 0, 3))


def pk(v):
    return np.ascontiguousarray(v.reshape(-1, 128).T)


def tile_wd(w):
    t = w.reshape(22, 128, 4, 256).transpose(2, 1, 0, 3)
    out = np.zeros((2, 4, 128, 12, 256), w.dtype)
    out[0] = t[:, :, 0:12, :]
    out[1, :, :, 0:10, :] = t[:, :, 12:22, :]
    return out


def ffn_w(inp, l, j, pre):
    return {pre + "wg": tile_kxm(inp["ffn_w_gate"][l, j]), pre + "wu": tile_kxm(inp["ffn_w_up"][l, j]),
            pre + "wd": tile_wd(inp["ffn_w_down"][l, j])}


def rope_tables(r):
    inv_freq = (10000.0 ** (-np.arange(0, 32, 2, dtype=np.float32) / 32)).astype(np.float32)
    pos = np.arange(r * TOK, (r + 1) * TOK, dtype=np.float32)
    ang = (pos[:, None] * inv_freq[None, :]).astype(np.float32)
    c = np.cos(ang).astype(np.float32).T
    s = np.sin(ang).astype(np.float32).T
    cs = np.zeros((32, 2, TOK), np.float32)
    cs[0:16, 0] = c
    cs[16:32, 0] = c
    cs[0:16, 1] = s
    cs[16:32, 1] = s
    return cs


def invcnt_table(r):
    t = np.arange(r * TOK, (r + 1) * TOK)
    out = np.zeros((4, TOK), np.float32)
    for g, w in enumerate((2, 4, 8, 16)):
        lo = np.clip(t - w // 2, 0, L)
        hi = np.clip(t - w // 2 + w, 0, L)
        out[g] = (1.0 / (hi - lo).astype(np.float32)).astype(np.float32)
    return out


def hyena_core_inputs(P, r):
    W = 512
    d = {}
    cw = P["hyena_conv_w"][0]
    cb = P["hyena_conv_b"][0]
    convw = np.zeros((128, 6, 3), np.float32)
    convb = np.zeros((128, 6), np.float32)
    for part in range(3):
        for ct in range(2):
            cols = part * W + 256 * r + ct * 128 + np.arange(128)
            convw[:, part * 2 + ct, :] = cw[:, cols].T
            convb[:, part * 2 + ct] = cb[cols]
    d["conv_w"] = convw
    d["conv_b"] = convb
    hb = np.zeros((128, 4), np.float32)
    for o in range(2):
        for ct in range(2):
            hb[:, o * 2 + ct] = P["hyena_bias"][0][o, 256 * r + ct * 128 + np.arange(128)]
    d["hy_bias"] = hb
    d["w1"] = np.ascontiguousarray(P["hyena_ffn_w1"][0])
    d["w2"] = np.ascontiguousarray(P["hyena_ffn_w2"][0])
    d["w3"] = np.ascontiguousarray(P["hyena_ffn_w3"][0])
    d["b123"] = np.ascontiguousarray(np.stack([P["hyena_ffn_b1"][0], P["hyena_ffn_b2"][0], P["hyena_ffn_b3"][0]], 1))
    d["sin_freq"] = np.ascontiguousarray(P["hyena_sin_freq"][0].T)
    wo = P["hyena_ffn_w_out"][0].reshape(64, 2, 2, W)
    dec = P["hyena_decay"][0]
    woc = np.zeros((64, 1024), np.float32)
    decc = np.zeros((128, 8), np.float32)
    for o in range(2):
        for dr in range(2):
            for ct in range(2):
                tix = (o * 2 + dr) * 2 + ct
                cols = 256 * r + ct * 128 + np.arange(128)
                woc[:, tix * 128:(tix + 1) * 128] = wo[:, o, dr, cols]
                decc[:, tix] = dec[o, dr, cols]
    d["w_out"] = woc
    d["decay"] = decc
    return d


P2_IN = {"conv_w": ([128, 6, 3], F32), "conv_b": ([128, 6], F32), "hy_bias": ([128, 4], F32), "w1": ([33, 64], F32),
         "w2": ([64, 64], F32), "w3": ([64, 64], F32), "b123": ([64, 3], F32), "sin_freq": ([64, 3], F32),
         "w_out": ([64, 1024], F32), "decay": ([128, 8], F32), "c_zT": ([33, L], F32), "c_trow": ([1, L], F32),
         "c_d1": ([32, 64], BF16), "c_f2": ([128, 32, 3, 128], BF16), "c_i1": ([128, 2, 256], BF16),
         "c_i2": ([64, 128, 32], BF16), "hin": ([768, L], BF16)}
P2_OUT = {"hout": ([256, L], BF16)}


def build_program(phase):
    nc = bass.Bass("TRN2", target_bir_lowering=False)
    ins, outs, emit = {1: (P1_IN, P1_OUT, emit_phase1), 3: (P3_IN, P3_OUT, emit_phase3),
                       4: (P4_IN, P4_OUT, emit_phase4), 2: (P2_IN, P2_OUT, None)}[phase]
    I = declare(nc, ins, "ExternalInput")
    O = declare(nc, outs, "ExternalOutput")
    with ExitStack() as es:
        kb = KB(nc, es)
        I["d_kv"] = Dep()
        def load_u32(es_, dst, part, ct, du, du32):
            ub = sb(nc, es_, "u_b", [128, L], BF16)
            kb.dma("sp", ub[:], I["hin"][part * 256 + ct * 128:part * 256 + (ct + 1) * 128, :], writes=[du])
            kb.op("act", lambda e: e.copy(out=dst, in_=ub[:]), reads=[du], writes=[du32])
        I["load_u32"] = load_u32
        I["load_apool"] = lambda ab, g, dab: kb.dma("sp", ab[:], I["apool"][g * 128:(g + 1) * 128, :], writes=[dab])
        I["load_yhy"] = lambda dst, c, d_y: kb.dma("sp", dst, I["yhyT"][c * 128:(c + 1) * 128, :], wacc=[d_y])
        if phase == 1:
            O["store_proj"] = lambda ch, ob, dob, d_out: kb.dma("sp", O["projT"][ch * 128:(ch + 1) * 128, :], ob[:], reads=[dob], wacc=[d_out])
        if phase == 2:
            I["hout"] = O["hout"]
            I["S_f"] = nc.dram_tensor("S_f", [1024, L], BF16).ap()
            I["S_z"] = nc.dram_tensor("S_z", [256, L], F32).ap()
            I["S_y"] = nc.dram_tensor("S_y", [256, L], F32).ap()
            I["d_hout"] = Dep()
            psb = [(ps(nc, es, "psb%d" % i, [128, 512]), Dep()) for i in range(8)]
            emit_hyena(kb, nc, I, psb)
            kb.wait_all("sp", [I["d_hout"]])
        else:
            C = common_tiles(nc, es, kb)
            d_out = emit(kb, nc, C, I, O)
            kb.wait_all("sp", [d_out])
    return nc


_PROGS = {}


def get_program(phase):
    if phase not in _PROGS:
        _PROGS[phase] = build_program(phase)
    return _PROGS[phase]


def run(phase, maps):
    res = run_bass_kernel_spmd(get_program(phase), maps, core_ids=list(range(NCORES)))
    return res.results


def kernel_unfused(**inp):
    inp = {k: np.asarray(v) for k, v in inp.items()}
    x = inp["x"]
    cores = [(c // 2, c % 2) for c in range(NCORES)]
    g1 = np.concatenate([pk(inp["norm_g"][0, 0]), pk(inp["norm_g"][0, 1])], axis=1)
    w_in = tile_kxm(inp["mix_w_in"][0])
    f00 = ffn_w(inp, 0, 0, "f00_")
    maps = []
    for (b, r) in cores:
        m = {"xT_in": np.ascontiguousarray(x[b, r * TOK:(r + 1) * TOK, :].T), "gains": g1, "w_in": w_in}
        m.update(f00)
        maps.append(m)
    r1 = run(1, maps)
    consts = {**hyena_constants(), **hyena_pos_constants()}
    hy = [hyena_core_inputs(inp, r) for r in range(2)]
    maps = []
    for (b, r) in cores:
        proj = np.concatenate([np.asarray(r1[2 * b]["projT"]), np.asarray(r1[2 * b + 1]["projT"])], axis=1)
        hin = np.concatenate([proj[512 + p * 512 + 256 * r: 512 + p * 512 + 256 * r + 256] for p in range(3)], axis=0)
        m = dict(consts)
        m.update(hy[r])
        m["hin"] = np.ascontiguousarray(hin)
        maps.append(m)
    r2 = run(2, maps)
    g3 = np.zeros((128, 32), np.float32)
    g3[:, 0:8] = pk(inp["norm_g"][0, 2])
    g3[:, 8:16] = pk(inp["norm_g"][1, 0])
    g3[:, 16:24] = pk(inp["norm_g"][1, 1])
    pool_w = np.ascontiguousarray(inp["pool_w"][0].transpose(1, 0, 2))
    pool_scale = pk(inp["pool_scale"][0])
    w_mo = tile_kxm(inp["mix_w_out"][0])
    f01 = ffn_w(inp, 0, 1, "f01_")
    f10 = ffn_w(inp, 1, 0, "f10_")
    w_dq = np.ascontiguousarray(inp["mla_w_dq"][0].reshape(8, 128, 256).transpose(1, 0, 2))
    w_dkv = np.ascontiguousarray(inp["mla_w_dkv"][0].reshape(8, 128, 160).transpose(1, 0, 2))
    qkv_g = np.concatenate([pk(inp["mla_q_norm_g"][0]), pk(inp["mla_kv_norm_g"][0])], axis=1)
    maps = []
    for (b, r) in cores:
        own = np.asarray(r1[2 * b + r]["projT"])[0:512]
        oth = np.asarray(r1[2 * b + 1 - r]["projT"])[0:512]
        ap = np.zeros((512, TOK + 32), BFNP)
        ap[:, 16:16 + TOK] = own
        if r == 0:
            ap[:, 16 + TOK:16 + TOK + 16] = oth[:, 0:16]
        else:
            ap[:, 0:16] = oth[:, TOK - 16:TOK]
        yhy = np.concatenate([np.asarray(r2[2 * b]["hout"])[:, r * TOK:(r + 1) * TOK],
                              np.asarray(r2[2 * b + 1]["hout"])[:, r * TOK:(r + 1) * TOK]], axis=0)
        m = {"xT_in": np.asarray(r1[2 * b + r]["xT_out"]), "gains": g3, "apool": ap, "invcnt": invcnt_table(r),
             "yhyT": np.ascontiguousarray(yhy), "pool_w": pool_w, "pool_scale": pool_scale, "w_mo": w_mo,
             "w_dq": w_dq, "w_dkv": w_dkv, "qkv_g": qkv_g, "rope_cs": rope_tables(r)}
        m.update(f01)
        m.update(f10)
        maps.append(m)
    r3 = run(3, maps)
    g4 = np.concatenate([pk(inp["norm_g"][1, 2]), pk(inp["final_norm_g"])], axis=1)
    w_uq = np.ascontiguousarray(inp["mla_w_uq"][0].reshape(2, 128, 1536).transpose(1, 0, 2))
    w_ukv = np.ascontiguousarray(inp["mla_w_ukv"][0])
    w_o = np.ascontiguousarray(inp["mla_w_o"][0].reshape(8, 128, 1024).transpose(1, 0, 2))
    f11 = ffn_w(inp, 1, 1, "f11_")
    maps = []
    for (b, r) in cores:
        kv = np.stack([np.asarray(r3[2 * b]["kvlatT"]), np.asarray(r3[2 * b + 1]["kvlatT"])], axis=0)
        rq = np.zeros((96, 2, TOK), np.float32)
        rq[64:96] = rope_tables(r)
        m = {"xT_in": np.asarray(r3[2 * b + r]["xT_out"]), "gains": g4, "cqnT": np.asarray(r3[2 * b + r]["cqnT"]),
             "kvlat_full": np.ascontiguousarray(kv), "w_uq": w_uq, "w_ukv": w_ukv, "w_o": w_o, "rope_q": rq}
        m.update(f11)
        maps.append(m)
    r4 = run(4, maps)
    out = np.zeros((4, L, D), np.float32)
    for c, (b, r) in enumerate(cores):
        out[b, r * TOK:(r + 1) * TOK, :] = np.asarray(r4[c]["outT"]).T
    return out


I32 = mybir.dt.int32
FUSED_IN = {}
FUSED_IN.update({"xT_in": P1_IN["xT_in"], "g1": ([128, 16], F32), "w_in": P1_IN["w_in"], **FFN_W("f00_")})
FUSED_IN.update({k: v for k, v in P2_IN.items() if k != "hin"})
FUSED_IN.update({k: v for k, v in P3_IN.items() if k not in ("xT_in", "gains", "apool", "yhyT")})
FUSED_IN.update({"g3": ([128, 32], F32)})
FUSED_IN.update({k: v for k, v in P4_IN.items() if k not in ("xT_in", "gains", "cqnT", "kvlat_full")})
FUSED_IN.update({"g4": ([128, 16], F32), "idx_u": ([128, 4], I32), "idx_y": ([128, 4], I32), "halo_mask": ([128, 2], F32)})
FUSED_OUT = {"outT": ([1024, TOK], F32)}


def build_fused(upto=4):
    nc = bass.Bass("TRN2", target_bir_lowering=False)
    IN = declare(nc, FUSED_IN, "ExternalInput")
    OUT = declare(nc, FUSED_OUT, "ExternalOutput")
    xs1 = nc.dram_tensor("xs1", [1024, TOK], F32)
    projP = nc.dram_tensor("projP_i", [512, TOK], BF16)
    projH = [nc.dram_tensor("projH%d_i" % i, [512, TOK], BF16) for i in range(3)]
    GH = [nc.dram_tensor("GH%d" % i, [1024, TOK], BF16) for i in range(3)]
    halo = nc.dram_tensor("halo_i", [512, 32], BF16)
    Ghalo = nc.dram_tensor("Ghalo", [1024, 32], BF16)
    hout = nc.dram_tensor("hout_i", [256, L], BF16)
    G2 = nc.dram_tensor("G2", [512, L], BF16)
    xs3 = nc.dram_tensor("xs3", [1024, TOK], F32)
    cqnT = nc.dram_tensor("cqnT_i", [256, TOK], BF16)
    kvl = nc.dram_tensor("kvl_i", [160, TOK], BF16)
    G3 = nc.dram_tensor("G3", [320, TOK], BF16)
    S_f = nc.dram_tensor("S_f", [1024, L], BF16)
    S_z = nc.dram_tensor("S_z", [256, L], F32)
    S_y = nc.dram_tensor("S_y", [256, L], F32)
    with ExitStack() as es:
        kb = KB(nc, es)
        C = common_tiles(nc, es, kb)
        idx_u = sb(nc, es, "idx_u", [128, 4], I32)
        idx_y = sb(nc, es, "idx_y", [128, 4], I32)
        hmask = sb(nc, es, "hmask", [128, 2], F32)
        kb.dma("pool", idx_u[:], IN["idx_u"][:, :], writes=[C["d_const"]])
        kb.dma("pool", idx_y[:], IN["idx_y"][:, :], writes=[C["d_const"]])
        kb.dma("pool", hmask[:], IN["halo_mask"][:, :], writes=[C["d_const"]])
        kb.wait_all("pool", [C["d_const"]])
        I1 = {"xT_in": IN["xT_in"], "gains": IN["g1"], "w_in": IN["w_in"], "f00_wg": IN["f00_wg"], "f00_wu": IN["f00_wu"], "f00_wd": IN["f00_wd"]}
        def store_proj(ch, ob, dob, d_out):
            if ch < 4:
                kb.dma("sp", projP.ap()[ch * 128:(ch + 1) * 128, :], ob[:], reads=[dob], wacc=[d_out])
                kb.dma("sp", halo.ap()[ch * 128:(ch + 1) * 128, 0:16], ob[:, 0:16], reads=[dob], wacc=[d_out])
                kb.dma("sp", halo.ap()[ch * 128:(ch + 1) * 128, 16:32], ob[:, TOK - 16:TOK], reads=[dob], wacc=[d_out])
            else:
                part, cc = (ch - 4) // 4, (ch - 4) % 4
                kb.dma("sp", projH[part].ap()[cc * 128:(cc + 1) * 128, :], ob[:], reads=[dob], wacc=[d_out])
        d1 = emit_phase1(kb, nc, C, I1, {"xT_out": xs1.ap(), "store_proj": store_proj})
        dG1 = Dep()
        for i in range(3):
            kb.allgather_pairs(projH[i], GH[i], [d1], dG1, acc=True)
        kb.allgather_pairs(halo, Ghalo, [d1], dG1, acc=True)
        if upto == 1:
            return _finish_early(kb, nc, OUT, xs1)
        I2 = {k: IN[k] for k in P2_IN if k != "hin"}
        I2.update({"hout": hout.ap(), "S_f": S_f.ap(), "S_z": S_z.ap(), "S_y": S_y.ap(), "d_hout": Dep()})
        def load_u32(es_, dst, part, ct, du, du32):
            ua = sb(nc, es_, "u_a", [128, L], BF16)
            ubb = sb(nc, es_, "u_bb", [128, L], BF16)
            for th in range(2):
                kb.dma("sp", ua[:, th * TOK:(th + 1) * TOK], GH[part].ap()[th * 512 + ct * 128:th * 512 + (ct + 1) * 128, :], reads=[dG1], wacc=[du])
                kb.dma("sp", ubb[:, th * TOK:(th + 1) * TOK], GH[part].ap()[th * 512 + 256 + ct * 128:th * 512 + 256 + (ct + 1) * 128, :], reads=[dG1], wacc=[du])
            kb.op("act", lambda e: e.activation(out=dst, in_=ua[:], func=AF.Copy, scale=hmask[:, 1:2]), reads=[du, C["d_const"]], writes=[du32])
            kb.op("dve", lambda e: e.scalar_tensor_tensor(out=dst, in0=ubb[:], scalar=hmask[:, 0:1], in1=dst, op0=ALU.mult, op1=ALU.add),
                  reads=[du, du32], writes=[du32])
        I2["load_u32"] = load_u32
        psb = [(C["ps_" + n], C["d_ps_" + n]) for n in "abcdefgh"]
        emit_hyena(kb, nc, I2, psb)
        dG2 = Dep()
        kb.allgather_pairs(hout, G2, [I2["d_hout"]], dG2)
        if upto == 2:
            return _finish_early(kb, nc, OUT, xs1)
        I3 = {k: IN[k] for k in P3_IN if k not in ("xT_in", "gains", "apool", "yhyT")}
        I3.update({"xT_in": xs1.ap(), "gains": IN["g3"], "halo_mask": hmask, "d_x": d1})
        projap = projP.ap()
        Ghap = Ghalo.ap()

        es_y3 = ExitStack()
        ytmp = [sb(nc, es_y3, "ytmp%d" % i, [128, L], BF16, side="right") for i in range(2)]
        dyt = [Dep(), Dep()]

        def load_yhy(dst, c, d_y):
            i = c % 2
            kb.dma("sp", ytmp[i][:], G2.ap()[c * 128:(c + 1) * 128, :], reads=[dG2], writes=[dyt[i]])
            kb.op("act", lambda e: e.activation(out=dst, in_=ytmp[i][:, 0:TOK], func=AF.Copy, scale=hmask[:, 1:2]),
                  reads=[dyt[i], C["d_const"]], writes=[d_y])
            kb.op("dve", lambda e: e.scalar_tensor_tensor(out=dst, in0=ytmp[i][:, TOK:L], scalar=hmask[:, 0:1], in1=dst,
                                                          op0=ALU.mult, op1=ALU.add), reads=[dyt[i], d_y], writes=[d_y])

        def load_apool(ab, g, dab):
            kb.dma("sp", ab[:, 16:16 + TOK], projap[g * 128:(g + 1) * 128, :], reads=[d1], wacc=[dab])
            kb.dma("sp", ab[:, 0:16], Ghap[g * 128:(g + 1) * 128, 16:32], reads=[dG1], wacc=[dab])
            kb.dma("sp", ab[:, 16 + TOK:32 + TOK], Ghap[512 + g * 128:512 + (g + 1) * 128, 0:16], reads=[dG1], wacc=[dab])
        I3["load_yhy"] = load_yhy

        def after_mixer():
            kb.drain_scope(dyt)
            es_y3.close()
        I3["after_mixer"] = after_mixer
        I3["load_apool"] = load_apool
        d3 = emit_phase3(kb, nc, C, I3, {"xT_out": xs3.ap(), "cqnT": cqnT.ap(), "kvlatT": kvl.ap()})
        kb.drain_scope([d3])
        dG3 = Dep()
        kb.allgather_pairs(kvl, G3, [d3], dG3)
        if upto == 3:
            return _finish_early(kb, nc, OUT, xs3)
        I4 = {k: IN[k] for k in P4_IN if k not in ("xT_in", "gains", "cqnT", "kvlat_full")}
        I4.update({"xT_in": xs3.ap(), "gains": IN["g4"], "cqnT": cqnT.ap(),
                   "kvlat_full": G3.ap().rearrange("(k c) t -> k c t", k=2), "d_kv": dG3})
        d4 = emit_phase4(kb, nc, C, I4, {"outT": OUT["outT"]})
        kb.wait_all("sp", [d4])
        kb.drain_scope([d4])
    return nc


def _finish_early(kb, nc, OUT, src):
    d = Dep()
    kb.dma("sp", OUT["outT"], src.ap(), writes=[d])
    kb.wait_all("sp", [d])
    return nc


_FUSED = []
UPTO = 4
DBG_STATIC_U = False


def kernel(**inp):
    inp = {k: np.asarray(v) for k, v in inp.items()}
    x = inp["x"]
    if not _FUSED:
        _FUSED.append(build_fused(UPTO))
    nc = _FUSED[0]
    cores = [(c // 2, c % 2) for c in range(NCORES)]
    shared = {}
    shared["g1"] = np.concatenate([pk(inp["norm_g"][0, 0]), pk(inp["norm_g"][0, 1])], axis=1)
    shared["w_in"] = tile_kxm(inp["mix_w_in"][0])
    shared.update(ffn_w(inp, 0, 0, "f00_"))
    shared.update(hyena_constants())
    shared.update(hyena_pos_constants())
    g3 = np.zeros((128, 32), np.float32)
    g3[:, 0:8] = pk(inp["norm_g"][0, 2])
    g3[:, 8:16] = pk(inp["norm_g"][1, 0])
    g3[:, 16:24] = pk(inp["norm_g"][1, 1])
    shared["g3"] = g3
    shared["pool_w"] = np.ascontiguousarray(inp["pool_w"][0].transpose(1, 0, 2))
    shared["pool_scale"] = pk(inp["pool_scale"][0])
    shared["w_mo"] = tile_kxm(inp["mix_w_out"][0])
    shared.update(ffn_w(inp, 0, 1, "f01_"))
    shared.update(ffn_w(inp, 1, 0, "f10_"))
    shared["w_dq"] = np.ascontiguousarray(inp["mla_w_dq"][0].reshape(8, 128, 256).transpose(1, 0, 2))
    shared["w_dkv"] = np.ascontiguousarray(inp["mla_w_dkv"][0].reshape(8, 128, 160).transpose(1, 0, 2))
    shared["qkv_g"] = np.concatenate([pk(inp["mla_q_norm_g"][0]), pk(inp["mla_kv_norm_g"][0])], axis=1)
    shared["g4"] = np.concatenate([pk(inp["norm_g"][1, 2]), pk(inp["final_norm_g"])], axis=1)
    shared["w_uq"] = np.ascontiguousarray(inp["mla_w_uq"][0].reshape(2, 128, 1536).transpose(1, 0, 2))
    shared["w_ukv"] = np.ascontiguousarray(inp["mla_w_ukv"][0])
    shared["w_o"] = np.ascontiguousarray(inp["mla_w_o"][0].reshape(8, 128, 1024).transpose(1, 0, 2))
    shared.update(ffn_w(inp, 1, 1, "f11_"))
    hy = [hyena_core_inputs(inp, r) for r in range(2)]
    p = np.arange(128)
    maps = []
    for (b, r) in cores:
        m = dict(shared)
        m.update(hy[r])
        m["xT_in"] = np.ascontiguousarray(x[b, r * TOK:(r + 1) * TOK, :].T)
        m["invcnt"] = invcnt_table(r)
        m["rope_cs"] = rope_tables(r)
        rq = np.zeros((96, 2, TOK), np.float32)
        rq[64:96] = m["rope_cs"]
        m["rope_q"] = rq
        iu = np.zeros((128, 4), np.int32)
        for ct in range(2):
            for th in range(2):
                iu[:, ct * 2 + th] = th * 512 + 256 * r + ct * 128 + p
        m["idx_u"] = iu
        iy = np.zeros((128, 4), np.int32)
        for c in range(4):
            iy[:, c] = (c // 2) * 512 + ((c % 2) * 128 + p) * 2 + r
        m["idx_y"] = iy
        hm = np.zeros((128, 2), np.float32)
        hm[:, 0] = 1.0 if r == 1 else 0.0
        hm[:, 1] = 1.0 if r == 0 else 0.0
        m["halo_mask"] = hm
        maps.append(m)
    res = run_bass_kernel_spmd(nc, maps, core_ids=list(range(NCORES)))
    out = np.zeros((4, L, D), np.float32)
    for c, (b, r) in enumerate(cores):
        out[b, r * TOK:(r + 1) * TOK, :] = np.asarray(res.results[c]["outT"]).T
    return out
```
